# Optimizing a Trainium2 kernel written in Bass

```python
import math
import jax, jax.numpy as jnp
from jax import lax
import numpy as np

D_MODEL = 1024
BATCH = 2
SEQ = 8192
DEPTH = 2

N_MIXERS = 2
EXPAND = 2
E_WIDTH = EXPAND * D_MODEL
HEAD_DIM = 128
N_SLOTS = E_WIDTH // HEAD_DIM
DILATED_GROUPS = ((128, 1), (512, 4), (2048, 16))
N_GROUPS = 3
ROT_DIM = HEAD_DIM // 4
ROPE_THETA = 500000.0
BLOCK = 128
POOL_WINDOWS = (2, 4, 8, 16)
N_POOL = 4
POOL_CH = E_WIDTH // N_POOL
RMS_EPS = 1e-6
NEG_INF = -1e30
N_ATTN_LAYERS = (DEPTH + 1) // 2
N_POOL_LAYERS = DEPTH // 2

kernel_name = "hybrid_dilated_attn_multiscale_pool"


def rmsnorm(x, g):
    x32 = x.astype(jnp.float32)
    y = x32 * lax.rsqrt(jnp.mean(x32 * x32, axis=-1, keepdims=True) + RMS_EPS)
    return (y * g.astype(jnp.float32)).astype(x.dtype)


def rope_partial(t, cos, sin):
    t32 = t.astype(jnp.float32)
    half = ROT_DIM // 2
    t1 = t32[..., :half]
    t2 = t32[..., half:ROT_DIM]
    out = jnp.concatenate([t1 * cos - t2 * sin, t2 * cos + t1 * sin, t32[..., ROT_DIM:]], axis=-1)
    return out.astype(t.dtype)


def dilated_window_attention(q, k, v, dilation, w_sub):
    B, S, H, Dh = q.shape
    L = S // dilation
    nb = -(-L // BLOCK)
    Lp = nb * BLOCK
    N = B * dilation

    def to_sub(t):
        t = t.reshape(B, L, dilation, H, Dh).transpose(0, 2, 1, 3, 4).reshape(N, L, H, Dh)
        return jnp.pad(t, ((0, 0), (0, Lp - L), (0, 0), (0, 0)))

    def band_keys(t):
        tp = jnp.pad(t, ((0, 0), (BLOCK, 0), (0, 0), (0, 0))).reshape(N, nb + 1, BLOCK, H, Dh)
        return jnp.concatenate([tp[:, :-1], tp[:, 1:]], axis=2)

    qs = to_sub(q).reshape(N, nb, BLOCK, H, Dh)
    kw = band_keys(to_sub(k))
    vw = band_keys(to_sub(v))

    qi = jnp.arange(BLOCK)[:, None]
    kj = jnp.arange(2 * BLOCK)[None, :]
    dist = qi + BLOCK - kj
    band = (dist >= 0) & (dist <= w_sub)
    not_first = (jnp.arange(nb) > 0)[:, None, None]
    valid = band[None] & (not_first | (kj >= BLOCK)[None])

    scale = 1.0 / math.sqrt(Dh)
    s = jnp.einsum('nbqhd,nbkhd->nbhqk', qs, kw).astype(jnp.float32) * scale
    s = jnp.where(valid[None, :, None], s, NEG_INF)
    m = jnp.max(s, axis=-1, keepdims=True)
    p = jnp.exp(s - m)
    den = jnp.sum(p, axis=-1)
    o = jnp.einsum('nbhqk,nbkhd->nbqhd', p, vw.astype(jnp.float32))
    o = o / jnp.transpose(den, (0, 1, 3, 2))[..., None]
    lse = jnp.transpose(m[..., 0] + jnp.log(den), (0, 1, 3, 2))

    o = o.reshape(N, Lp, H, Dh)[:, :L].reshape(B, dilation, L, H, Dh)
    o = o.transpose(0, 2, 1, 3, 4).reshape(B, S, H, Dh)
    lse = lse.reshape(N, Lp, H)[:, :L].reshape(B, dilation, L, H)
    lse = lse.transpose(0, 2, 1, 3).reshape(B, S, H)
    return o, lse


def attention_mixer(xn, cos, sin, w_in, w_out):
    B, S, D = xn.shape
    qkv_cols = N_GROUPS * 3 * E_WIDTH
    w_qkv = w_in[:, :qkv_cols].reshape(D, N_GROUPS, 3, N_SLOTS, HEAD_DIM)
    z = xn @ w_in[:, qkv_cols:]
    outs, lses = [], []
    for g, (window, dil) in enumerate(DILATED_GROUPS):
        qkv = jnp.einsum('bsd,dchk->cbshk', xn, w_qkv[:, g])
        q = rope_partial(qkv[0], cos, sin)
        k = rope_partial(qkv[1], cos, sin)
        o, lse = dilated_window_attention(q, k, qkv[2], dil, window // dil)
        outs.append(o)
        lses.append(lse)
    wts = jax.nn.softmax(jnp.stack(lses, axis=0), axis=0)
    y = jnp.sum(wts[..., None] * jnp.stack(outs, axis=0), axis=0)
    y = y.reshape(B, S, E_WIDTH).astype(xn.dtype) * jax.nn.silu(z)
    return y @ w_out


def pooling_mixer(xn, w_in, w_grp, b_grp, scale, w_out):
    B, S, D = xn.shape
    uz = xn @ w_in
    u, z = uz[..., :E_WIDTH], uz[..., E_WIDTH:]
    ug = u.reshape(B, S, N_POOL, POOL_CH).astype(jnp.float32)
    c = jnp.cumsum(ug, axis=1)
    t = jnp.arange(S)
    parts = []
    for g, w in enumerate(POOL_WINDOWS):
        cg = c[:, :, g]
        lag = jnp.pad(cg[:, :S - w], ((0, 0), (w, 0), (0, 0)))
        cnt = jnp.minimum(t + 1, w).astype(jnp.float32)[None, :, None]
        parts.append((cg - lag) / cnt)
    pooled = (jnp.stack(parts, axis=2) - ug).astype(xn.dtype)
    h = jnp.einsum('bsgc,gcd->bsgd', pooled, w_grp) + b_grp
    h = h.reshape(B, S, E_WIDTH) * scale
    y = h * jax.nn.silu(z)
    return y @ w_out


def setup_inputs(seed: int = 0) -> dict:
    key = jax.random.key(seed)
    ks = jax.random.split(key, 12)
    D, E = D_MODEL, E_WIDTH
    f32 = jnp.float32
    x = jax.random.normal(ks[0], (BATCH, SEQ, D), f32)
    offset = jax.random.randint(ks[1], (BATCH, 1), 0, 4096, dtype=jnp.int32)
    positions = (offset + jnp.arange(SEQ, dtype=jnp.int32)[None, :]).astype(jnp.int32)
    norm_pre = 1.0 + 0.1 * jax.random.normal(ks[2], (DEPTH, D), f32)
    norm_post = 1.0 + 0.1 * jax.random.normal(ks[3], (DEPTH, D), f32)
    attn_w_in = jax.random.normal(ks[4], (N_ATTN_LAYERS, D, N_GROUPS * 3 * E + E), f32) * D ** -0.5
    attn_w_out = jax.random.normal(ks[5], (N_ATTN_LAYERS, E, D), f32) * E ** -0.5
    pool_w_in = jax.random.normal(ks[6], (N_POOL_LAYERS, D, 2 * E), f32) * D ** -0.5
    pool_w_grp = jax.random.normal(ks[7], (N_POOL_LAYERS, N_POOL, POOL_CH, POOL_CH), f32) * POOL_CH ** -0.5
    pool_b_grp = 0.01 * jax.random.normal(ks[8], (N_POOL_LAYERS, N_POOL, POOL_CH), f32)
    pool_scale = 1.0 + 0.1 * jax.random.normal(ks[9], (N_POOL_LAYERS, E), f32)
    pool_w_out = jax.random.normal(ks[10], (N_POOL_LAYERS, E, D), f32) * E ** -0.5
    return {"x": x, "positions": positions, "norm_pre": norm_pre, "norm_post": norm_post,
            "attn_w_in": attn_w_in, "attn_w_out": attn_w_out,
            "pool_w_in": pool_w_in, "pool_w_grp": pool_w_grp, "pool_b_grp": pool_b_grp,
            "pool_scale": pool_scale, "pool_w_out": pool_w_out}


def reference(x, positions, norm_pre, norm_post, attn_w_in, attn_w_out,
              pool_w_in, pool_w_grp, pool_b_grp, pool_scale, pool_w_out):
    inv_freq = ROPE_THETA ** (-jnp.arange(0, ROT_DIM, 2, dtype=jnp.float32) / ROT_DIM)
    ang = positions.astype(jnp.float32)[..., None] * inv_freq
    cos = jnp.cos(ang)[:, :, None, :]
    sin = jnp.sin(ang)[:, :, None, :]
    h = x
    for i in range(DEPTH):
        xn = rmsnorm(h, norm_pre[i])
        j = i // N_MIXERS
        if i % N_MIXERS == 0:
            y = attention_mixer(xn, cos, sin, attn_w_in[j], attn_w_out[j])
        else:
            y = pooling_mixer(xn, pool_w_in[j], pool_w_grp[j], pool_b_grp[j],
                              pool_scale[j], pool_w_out[j])
        h = h + rmsnorm(y, norm_post[i])
    return h
```

```python
import contextlib
import os
import math
import numpy as np
import concourse.bass as bass
import concourse.mybir as mybir
from concourse.bass_utils import run_bass_kernel_spmd

F32 = mybir.dt.float32
BF16 = mybir.dt.bfloat16
I32 = mybir.dt.int32
AF = mybir.ActivationFunctionType
ALU = mybir.AluOpType
AX = mybir.AxisListType

ENGS = ("pe", "act", "dve", "pool", "sp")
DMAQ = ("act", "pool", "sp")
NDS = 8

D = 1024
E = 2048
T_OWN = 2048
EPS = 1e-6
TWO_PI = 2.0 * math.pi
INV2PI = 1.0 / TWO_PI
C1 = 6.28125
C2 = TWO_PI - C1
MAGIC = 12582912.0
PI_LO = 3.1415925
CW = 1044
XTOK = 4224
YTOK = 2176


class Op:
    __slots__ = ("eng", "fn", "deps", "dma", "signal", "sem", "val", "prev")

    def __init__(self, eng, fn, dma):
        self.eng = eng
        self.fn = fn
        self.dma = dma
        self.deps = []
        self.signal = False
        self.sem = None
        self.val = 0
        self.prev = None


class Prog:
    def __init__(self):
        self.ops = {e: [] for e in ENGS}
        self.lastw = {}
        self.readers = {}
        self.bar_idx = {e: 0 for e in ENGS}

    def add(self, eng, fn, reads=(), writes=(), dma=False):
        op = Op(eng, fn, dma)
        deps = {}
        psr = [r for r in reads if isinstance(r, tuple) and r[0] == "ps"]
        if psr:
            reads = [r for r in reads if not (isinstance(r, tuple) and r[0] == "ps")]
            writes = list(writes) + psr

        def need(d, raw):
            if d is None:
                return
            if (not dma) and (not d.dma) and d.eng == eng:
                if eng == "pe" or not raw:
                    return
            deps[id(d)] = d

        for r in reads:
            need(self.lastw.get(r), True)
        for r in writes:
            need(self.lastw.get(r), False)
            rd = self.readers.get(r)
            if rd:
                for k, v in rd.items():
                    if k == "dma":
                        for d in v:
                            need(d, False)
                    else:
                        need(v, False)
        for d in deps.values():
            d.signal = True
        op.deps = list(deps.values())
        for r in reads:
            rd = self.readers.setdefault(r, {})
            if dma:
                rd.setdefault("dma", []).append(op)
            else:
                rd[eng] = op
        for r in writes:
            self.lastw[r] = op
            self.readers[r] = {}
        self.ops[eng].append(op)
        return op

    def barrier(self):
        lasts = []
        for e in ENGS:
            for op in reversed(self.ops[e]):
                if (not op.dma) and op.fn is not None:
                    lasts.append(op)
                    break
        dmas = [op for e in ENGS for op in self.ops[e][self.bar_idx[e]:] if op.dma]
        for e in ENGS:
            w = Op(e, None, False)
            w.deps = [d for d in lasts if d.eng != e] + dmas
            for d in w.deps:
                d.signal = True
            self.ops[e].append(w)
        self.bar_idx = {e: len(self.ops[e]) for e in ENGS}
        self.lastw.clear()
        self.readers.clear()

    def finalize(self, nc, stack):
        self.psem = {e: stack.enter_context(nc.semaphore("p_" + e)) for e in ENGS}
        self.dsem = {e: [stack.enter_context(nc.semaphore("d_%s%d" % (e, i))) for i in range(NDS)]
                     for e in DMAQ}
        self.semobj = {}
        for e in ENGS:
            self.semobj[("p", e)] = self.psem[e]
        for e in DMAQ:
            for i in range(NDS):
                self.semobj[("d", e, i)] = self.dsem[e][i]
        for e in ENGS:
            cnt = 0
            di = 0
            for op in self.ops[e]:
                if op.fn is None:
                    continue
                if op.dma:
                    op.sem = ("d", e, di % NDS)
                    op.val = 16 * (di // NDS + 1)
                    op.prev = (op.sem, op.val - 16) if di >= NDS else None
                    di += 1
                elif op.signal:
                    cnt += 1
                    op.sem = ("p", e)
                    op.val = cnt

    def emit(self, eng, e):
        waited = {}
        for op in self.ops[eng]:
            w = {}
            for d in op.deps:
                if w.get(d.sem, 0) < d.val:
                    w[d.sem] = d.val
            if op.dma and op.prev is not None:
                if w.get(op.prev[0], 0) < op.prev[1]:
                    w[op.prev[0]] = op.prev[1]
            for sem, val in w.items():
                if waited.get(sem, 0) < val:
                    e.wait_ge(self.semobj[sem], val)
                    waited[sem] = val
            if op.fn is not None:
                ins = op.fn(e)
                if op.dma:
                    ins.then_inc(self.semobj[op.sem], 16)
                elif op.signal:
                    ins.then_inc(self.semobj[op.sem], 1)

    def run(self, nc, stack):
        self.barrier()
        self.finalize(nc, stack)
        block = stack.enter_context(nc.Block())

        @block.tensor
        def _(e):
            self.emit("pe", e)

        @block.scalar
        def _(e):
            self.emit("act", e)

        @block.vector
        def _(e):
            self.emit("dve", e)

        @block.gpsimd
        def _(e):
            self.emit("pool", e)

        @block.sync
        def _(e):
            self.emit("sp", e)


class TT:
    def __init__(self, t, pitch):
        self.t = t
        self.pitch = pitch

    def a(self, off, dims, p0=0, np_=128):
        return bass.AP(self.t, p0 * self.pitch + off, [[self.pitch, np_]] + [list(d) for d in dims])


def DAP(t, off, dims):
    return bass.AP(t, off, [list(d) for d in dims])


def MM(out, lhsT, rhs, start, stop):
    return lambda e: e.matmul(out, lhsT=lhsT, rhs=rhs, start=start, stop=stop)


def TR(out, in_, ident):
    return lambda e: e.transpose(out, in_, ident)


def ACT(out, in_, func, scale=None, bias=None):
    kw = {}
    if scale is not None:
        kw["scale"] = scale
    if bias is not None:
        kw["bias"] = bias
    return lambda e: e.activation(out=out, in_=in_, func=func, **kw)


def TTO(out, in0, in1, op):
    return lambda e: e.tensor_tensor(out=out, in0=in0, in1=in1, op=op)


def TS(out, in0, s1, op0, s2=None, op1=None):
    if op1 is None:
        return lambda e: e.tensor_scalar(out=out, in0=in0, scalar1=s1, scalar2=None, op0=op0)
    return lambda e: e.tensor_scalar(out=out, in0=in0, scalar1=s1, scalar2=s2, op0=op0, op1=op1)


def STT(out, in0, scalar, in1, op0, op1):
    return lambda e: e.scalar_tensor_tensor(out=out, in0=in0, scalar=scalar, in1=in1, op0=op0, op1=op1)


def CP(out, in_):
    return lambda e: e.tensor_copy(out=out, in_=in_)


def ACP(out, in_):
    return lambda e: e.activation(out=out, in_=in_, func=AF.Copy)


def RECIP(out, in_):
    return lambda e: e.reciprocal(out=out, in_=in_)


def RECIPF(out, in_):
    return lambda e: e.reciprocal_approx_fast(out=out, in_=in_)


def RSUM(out, in_):
    return lambda e: e.reduce_sum(out=out, in_=in_, axis=AX.X)


def MSET(ap, c):
    return lambda e: e.memset(ap, c)


def DMA(out, in_):
    return lambda e: e.dma_start(out=out, in_=in_)


class Ctx:
    def __init__(self, nc):
        self.nc = nc
        self.P = Prog()

    def sb(self, stack, name, free, dt):
        self.n = getattr(self, "n", 0) + 1
        t = stack.enter_context(self.nc.sbuf_tensor("%s_%d" % (name, self.n), [128, free], dt))
        return TT(t, free)

    def ps(self, stack, name, free, dt):
        self.n = getattr(self, "n", 0) + 1
        t = stack.enter_context(self.nc.psum_tensor("%s_%d" % (name, self.n), [128, free], dt))
        return TT(t, free)


def emit_norm_transpose(cx, stack, xsrc, ntiles, xnT, xnT_tok, gb, identb, row0=0, keep=None, hook=None,
                        ntp=4, src_res=None, manual=False):
    P = cx.P
    NS = 4
    XT = [cx.sb(stack, "XT%d" % i, 1024, F32) for i in range(NS)]
    SQ = [cx.sb(stack, "SQ%d" % i, 1024, F32) for i in range(2)]
    XNB = [cx.sb(stack, "XNB%d" % i, 1024, BF16) for i in range(NS)]
    ST = cx.sb(stack, "STn", 3 * 64, F32)
    TP = [cx.ps(stack, "TP%d" % i, 1024, BF16) for i in range(ntp)]

    def stage_a(tt):
        s, s2 = tt % NS, tt % 2
        P.add("sp", DMA(XT[s].a(0, [[1, 1024]]), DAP(xsrc, (row0 + tt * 128) * 1024, [[1024, 128], [1, 1024]])),
              reads=[src_res(tt)] if src_res else [], writes=[("XT", s)], dma=True)
        P.add("act", ACT(SQ[s2].a(0, [[1, 1024]]), XT[s].a(0, [[1, 1024]]), AF.Square),
              reads=[("XT", s)], writes=[("SQ", s2)])
        P.add("dve", RSUM(ST.a(tt, [[1, 1]]), SQ[s2].a(0, [[1, 1024]])), reads=[("SQ", s2)], writes=[("st0", tt)])

    def stage_b(tt):
        s = tt % NS
        P.add("act", ACT(ST.a(64 + tt, [[1, 1]]), ST.a(tt, [[1, 1]]), AF.Sqrt, scale=1.0 / D, bias=EPS),
              reads=[("st0", tt)], writes=[("st1", tt)])
        P.add("dve", RECIP(ST.a(128 + tt, [[1, 1]]), ST.a(64 + tt, [[1, 1]])), reads=[("st1", tt)], writes=[("st2", tt)])
        P.add("dve", STT(XNB[s].a(0, [[1, 1024]]), XT[s].a(0, [[1, 1024]]), ST.a(128 + tt, [[1, 1]]),
                         gb.a(0, [[1, 1024]]), ALU.mult, ALU.mult),
              reads=[("XT", s), ("st2", tt), "gb"], writes=[("XNB", s)])

    def stage_c(tt):
        s = tt % NS
        tp = tt % ntp
        for c in range(8):
            P.add("pe", TR(TP[tp].a(c * 128, [[1, 128]]), XNB[s].a(c * 128, [[1, 128]]), identb.a(0, [[1, 128]])),
                  reads=[("XNB", s), "identb"], writes=[("TP", tp)])
        P.add("act", ACP(xnT.a(tt * 128, [[xnT_tok, 8], [1, 128]]), TP[tp].a(0, [[128, 8], [1, 128]])),
              reads=[("TP", tp)], writes=["xnT"])

    if manual:
        return stage_a, stage_b, stage_c
    for it in range(ntiles + 2):
        if it < ntiles:
            stage_a(it)
        if 0 <= it - 1 < ntiles:
            stage_b(it - 1)
        if 0 <= it - 2 < ntiles:
            stage_c(it - 2)
        if hook is not None:
            hook(it)


def emit_out_phase(cx, stack, PS, YG, yg_tok, WO, xres, xres_row0, gpb, outd, ntiles=16, sfx="", yg_res=None,
                   post=None):
    P = cx.P
    XR = [cx.sb(stack, "XR%d" % i, 1024, F32) for i in range(2)]
    SQ = [cx.sb(stack, "SQo%d" % i, 1024, F32) for i in range(2)]
    TO = [cx.sb(stack, "TO%d" % i, 1024, F32) for i in range(2)]
    ST = cx.sb(stack, "STo", 3 * 32, F32)
    nb = len(PS)
    for tt in range(ntiles):
        s = tt % 2
        P.add("sp", DMA(XR[s].a(0, [[1, 1024]]), DAP(xres, (xres_row0 + tt * 128) * 1024, [[1024, 128], [1, 1024]])),
              writes=[("XR", s)], dma=True)
        banks = [(2 * tt) % nb, (2 * tt + 1) % nb]
        for hh in range(2):
            b = banks[hh]
            for ec in range(16):
                P.add("pe", MM(PS[b].a(0, [[1, 512]]), YG.a(ec * yg_tok + tt * 128, [[1, 128]]),
                               WO.a(ec * 1024 + hh * 512, [[1, 512]]), ec == 0, ec == 15),
                      reads=(yg_res(ec, tt) if yg_res else [("YG", ec)]) + [("WO", ec // 4)], writes=[("ps", b)])
            P.add("act", ACT(SQ[s].a(hh * 512, [[1, 512]]), PS[b].a(0, [[1, 512]]), AF.Square),
                  reads=[("ps", b)], writes=[("SQo", s, hh)])
        P.add("dve", RSUM(ST.a(tt, [[1, 1]]), SQ[s].a(0, [[1, 1024]])),
              reads=[("SQo", s, 0), ("SQo", s, 1)], writes=[("so0", tt)])
        P.add("act", ACT(ST.a(32 + tt, [[1, 1]]), ST.a(tt, [[1, 1]]), AF.Sqrt, scale=1.0 / D, bias=EPS),
              reads=[("so0", tt)], writes=[("so1", tt)])
        P.add("dve", RECIP(ST.a(64 + tt, [[1, 1]]), ST.a(32 + tt, [[1, 1]])), reads=[("so1", tt)], writes=[("so2", tt)])
        for hh in range(2):
            b = banks[hh]
            P.add("dve", STT(TO[s].a(hh * 512, [[1, 512]]), PS[b].a(0, [[1, 512]]), ST.a(64 + tt, [[1, 1]]),
                             gpb.a(hh * 512, [[1, 512]]), ALU.mult, ALU.mult),
                  reads=[("ps", b), ("so2", tt), "gpb"], writes=[("TO", s, hh)])
        P.add("pool", TTO(XR[s].a(0, [[1, 1024]]), XR[s].a(0, [[1, 1024]]), TO[s].a(0, [[1, 1024]]), ALU.add),
              reads=[("XR", s), ("TO", s, 0), ("TO", s, 1)], writes=[("XR", s)])
        P.add("sp", DMA(DAP(outd, tt * 128 * 1024, [[1024, 128], [1, 1024]]), XR[s].a(0, [[1, 1024]])),
              reads=[("XR", s)], writes=[("outd", tt)], dma=True)
        if post is not None:
            post(tt)
    if post is not None:
        post(None)


PARTS = [
    dict(g=0, d=1, R=[0], nkb=17),
    dict(g=1, d=4, R=[0, 1, 2, 3], nkb=5),
    dict(g=2, d=16, R=list(range(0, 16)), nkb=2),
]


def part_blocks(pt):
    d, nkb = pt["d"], pt["nkb"]
    nqb = nkb - 1
    kb0 = 16 // d - 1
    ktok0 = 2048 - 128 * d
    nk = 4096 - ktok0
    kblocks = [(c0, min(512, nk - c0), ktok0 + c0) for c0 in range(0, nk, 512)]
    qblocks = [(nb, 512, 2048 + 512 * nb) for nb in range(4)]
    return kblocks, qblocks, nqb, kb0


def cdims(dims):
    if len(dims) == 1:
        return [[1, dims[0][1]]]
    n1, n2 = dims[0][1], dims[1][1]
    return [[n2, n1], [1, n2]]


def build_fused():
    nc = bass.Bass("TRN2", target_bir_lowering=False)
    stop = None
    lim = [8, 4, 3]
    xin = nc.dram_tensor("xin", [XTOK, 1024], F32, kind="ExternalInput")
    pos = nc.dram_tensor("pos", [1, XTOK], I32, kind="ExternalInput")
    gpre = nc.dram_tensor("gpre", [1, 1024], F32, kind="ExternalInput")
    gpost = nc.dram_tensor("gpost", [1, 1024], F32, kind="ExternalInput")
    win = nc.dram_tensor("win", [1024, 20480], F32, kind="ExternalInput")
    wout = nc.dram_tensor("wout", [2048, 1024], F32, kind="ExternalInput")
    cst = nc.dram_tensor("cst", [128, CW], F32, kind="ExternalInput")
    gpre1 = nc.dram_tensor("gpre1", [1, 1024], F32, kind="ExternalInput")
    gpost1 = nc.dram_tensor("gpost1", [1, 1024], F32, kind="ExternalInput")
    win1 = nc.dram_tensor("win1", [1024, 4096], F32, kind="ExternalInput")
    wgrp = nc.dram_tensor("wgrp", [2048, 512], F32, kind="ExternalInput")
    wout1 = nc.dram_tensor("wout1", [2048, 1024], F32, kind="ExternalInput")
    cst1 = nc.dram_tensor("cst1", [128, 128 + 96], F32, kind="ExternalInput")
    ygd = nc.dram_tensor("ygd", [16, 128, YTOK], BF16, kind="Internal")
    h1 = nc.dram_tensor("h1d", [YTOK, 1024], F32, kind="Internal")
    outd = nc.dram_tensor("out", [YTOK, 1024], F32, kind="ExternalOutput")

    cx = Ctx(nc)
    P = cx.P
    scale_qk = 1.0 / math.sqrt(128.0)

    with contextlib.ExitStack() as s0:
        identb = cx.sb(s0, "identb", 128, BF16)
        permb = cx.sb(s0, "permb", 128, BF16)
        maskb = cx.sb(s0, "maskb", 256, BF16)
        onesH = cx.sb(s0, "onesH", 128, BF16)
        onesb = cx.sb(s0, "onesb", 128, BF16)
        cf = cx.sb(s0, "cf", 4, F32)
        MX = cx.sb(s0, "MX", 400, BF16)
        P.add("pool", DMA(MX.a(0, [[1, 400]]), DAP(cst, 644, [[CW, 128], [1, 400]])), writes=["MX"], dma=True)
        P.add("pool", DMA(identb.a(0, [[1, 128]]), DAP(cst, 0, [[CW, 128], [1, 128]])), writes=["identb"], dma=True)
        P.add("pool", DMA(permb.a(0, [[1, 128]]), DAP(cst, 128, [[CW, 128], [1, 128]])), writes=["permb"], dma=True)
        P.add("pool", DMA(maskb.a(0, [[1, 256]]), DAP(cst, 256, [[CW, 128], [1, 256]])), writes=["maskb"], dma=True)
        P.add("pool", DMA(onesH.a(0, [[1, 128]]), DAP(cst, 512, [[CW, 128], [1, 128]])), writes=["onesH"], dma=True)
        P.add("sp", DMA(cf.a(0, [[1, 4]]), DAP(cst, 640, [[CW, 128], [1, 4]])), writes=["cf"], dma=True)
        P.add("dve", MSET(onesb.a(0, [[1, 128]]), 1.0), writes=["onesb"])

        with contextlib.ExitStack() as s1:
            xnT = cx.sb(s1, "xnT", 8 * XTOK, BF16)
            cosb = cx.sb(s1, "cosb", XTOK, BF16)
            sinb = cx.sb(s1, "sinb", XTOK, BF16)

            with contextlib.ExitStack() as sA:
                gb = cx.sb(sA, "gb", 1024, F32)
                P.add("sp", DMA(gb.a(0, [[1, 1024]]), DAP(gpre, 0, [[0, 128], [1, 1024]])), writes=["gb"], dma=True)
                posi = cx.sb(sA, "posi", 1024, I32)
                tA = [cx.sb(sA, "tA%d" % i, 1024, F32) for i in range(5)]
                R32 = dict(p0=0, np_=32)
                def cs_chunk(q):
                    c0 = q * 1024
                    full = [[1, 1024 if q < 4 else 128]]
                    P.add("sp", DMA(posi.a(0, full, **R32), DAP(pos, c0, [[0, 32], full[0]])), writes=["posi"], dma=True)
                    ang, a1, a2, a3 = tA[0], tA[1], tA[2], tA[3]
                    P.add("dve", CP(a1.a(0, full, **R32), posi.a(0, full, **R32)), reads=["posi"], writes=["a1"])
                    P.add("dve", TS(ang.a(0, full, **R32), a1.a(0, full, **R32), cf.a(0, [[1, 1]], **R32), ALU.mult),
                          reads=["a1", "cf"], writes=["ang"])
                    P.add("dve", TS(a1.a(0, full, **R32), ang.a(0, full, **R32), INV2PI, ALU.mult), reads=["ang"], writes=["a1"])
                    P.add("dve", TS(a2.a(0, full, **R32), a1.a(0, full, **R32), MAGIC, ALU.add), reads=["a1"], writes=["a2"])
                    P.add("dve", TS(a1.a(0, full, **R32), a2.a(0, full, **R32), MAGIC, ALU.subtract), reads=["a2"], writes=["a1"])
                    P.add("dve", STT(a2.a(0, full, **R32), a1.a(0, full, **R32), -C1, ang.a(0, full, **R32), ALU.mult, ALU.add),
                          reads=["a1", "ang"], writes=["a2"])
                    P.add("dve", STT(a3.a(0, full, **R32), a1.a(0, full, **R32), -C2, a2.a(0, full, **R32), ALU.mult, ALU.add),
                          reads=["a1", "a2"], writes=["a3"])
                    P.add("dve", TS(a2.a(0, full, **R32), a3.a(0, full, **R32), -PI_LO, ALU.max, PI_LO, ALU.min),
                          reads=["a3"], writes=["a2"])
                    P.add("act", ACT(sinb.a(c0, full, **R32), a2.a(0, full, **R32), AF.Sin, scale=cf.a(1, [[1, 1]], **R32)),
                          reads=["a2", "cf"], writes=["sinb"])
                    P.add("dve", TS(a1.a(0, full, **R32), ang.a(0, full, **R32), INV2PI, ALU.mult, 0.25, ALU.add),
                          reads=["ang"], writes=["a1"])
                    P.add("dve", TS(a3.a(0, full, **R32), a1.a(0, full, **R32), MAGIC, ALU.add), reads=["a1"], writes=["a3"])
                    P.add("dve", TS(a1.a(0, full, **R32), a3.a(0, full, **R32), MAGIC, ALU.subtract), reads=["a3"], writes=["a1"])
                    a4 = tA[4]
                    P.add("dve", STT(a3.a(0, full, **R32), a1.a(0, full, **R32), -C1, ang.a(0, full, **R32), ALU.mult, ALU.add),
                          reads=["a1", "ang"], writes=["a3"])
                    P.add("dve", STT(a4.a(0, full, **R32), a1.a(0, full, **R32), -C2, a3.a(0, full, **R32), ALU.mult, ALU.add),
                          reads=["a1", "a3"], writes=["a4"])
                    P.add("dve", TS(a3.a(0, full, **R32), a4.a(0, full, **R32), 0.5 * math.pi, ALU.add), reads=["a4"], writes=["a3"])
                    P.add("dve", TS(a4.a(0, full, **R32), a3.a(0, full, **R32), -PI_LO, ALU.max, PI_LO, ALU.min),
                          reads=["a3"], writes=["a4"])
                    P.add("act", ACT(cosb.a(c0, full, **R32), a4.a(0, full, **R32), AF.Sin), reads=["a4"], writes=["cosb"])
                cs_at = {2: 0, 8: 1, 14: 2, 20: 3, 26: 4}
                emit_norm_transpose(cx, sA, xin, 33, xnT, XTOK, gb, identb,
                                    hook=lambda it: cs_chunk(cs_at[it]) if it in cs_at else None)
            P.barrier()

            with contextlib.ExitStack() as sB:
                PS = [cx.ps(sB, "PS%d" % i, 512, F32) for i in range(8)]
                QT = cx.sb(sB, "QT", 2 * 2048, BF16)
                KT = cx.sb(sB, "KT", 2 * 4096, BF16)
                V = cx.sb(sB, "V", 32 * 256, BF16)
                ND = cx.sb(sB, "ND", 4 * 2048, F32)
                NDb = TT(ND.t.bitcast(BF16), 16384)
                WQ = [cx.sb(sB, "WQ%d" % i, 2048, BF16) for i in range(2)]
                WK = [cx.sb(sB, "WK%d" % i, 2048, BF16) for i in range(2)]
                WV = [cx.sb(sB, "WV%d" % i, 2048, BF16) for i in range(2)]
                WZ = cx.sb(sB, "WZ", 2048, BF16)
                T1 = [cx.sb(sB, "T1%d" % i, 512, F32) for i in range(3)]
                T2 = [cx.sb(sB, "T2%d" % i, 512, F32) for i in range(2)]
                PT = [cx.sb(sB, "PT%d" % i, 512, BF16) for i in range(4)]
                mask4 = cx.sb(sB, "mask4", 512, BF16)
                for q_ in range(4):
                    P.add("pool", TS(mask4.a(q_ * 128, [[1, 128]]), maskb.a((q_ // 2) * 128, [[1, 128]]), -1.0, ALU.add,
                                     30000.0, ALU.mult),
                          reads=["maskb"], writes=["mask4"])
                SZ = [cx.sb(sB, "SZ%d" % i, 512, F32) for i in range(2)]
                QX = cx.sb(sB, "QX", 32, BF16)
                KX = cx.sb(sB, "KX", 256, BF16)
                VX = cx.sb(sB, "VX", 256, BF16)
                NDX = cx.sb(sB, "NDX", 64, F32)
                PTX = cx.sb(sB, "PTX", 256, BF16)
                PTC = cx.sb(sB, "PTC", 16, BF16)
                YGX = cx.sb(sB, "YGX", 32, BF16)
                SZX = cx.sb(sB, "SZX", 16, F32)
                MXP_OFF = [0, 16, 80]

                def wload(Wt, name, colbase):
                    P.add("pool", DMA(Wt.a(0, [[256, 8], [1, 256]]),
                                      DAP(win, colbase, [[20480, 128], [128 * 20480, 8], [1, 256]])),
                          writes=[name], dma=True)

                def load_group_weights(bt, g, par):
                    base = g * 6144 + bt * 256
                    wload(WK[par], ("WK", par), base + 2048)
                    wload(WV[par], ("WV", par), base + 4096)
                    wload(WQ[par], ("WQ", par), base)

                cnt = dict(a=0, b=0, v=0, t=0, t2=0, s=0, o=0, p=0, z=0)
                nphase = 0
                load_group_weights(0, 0, 0)

                class Pipe:
                    def __init__(self, depth):
                        self.q = []
                        self.depth = depth

                    def push(self, fn):
                        self.q.append(fn)
                        while len(self.q) > self.depth:
                            self.q.pop(0)()

                    def flush(self):
                        while self.q:
                            self.q.pop(0)()

                ppipe = Pipe(1)

                def proj_block(Wt, wname, j, dst, dres, dbase, n, tok_off, d=1, dpitch=0):
                    ba = cnt["a"] % 4
                    cnt["a"] += 1
                    t1s = cnt["t"] % 3
                    cnt["t"] += 1
                    if d == 1:
                        nat = [[1, n]]
                        ri_ = [[1, n]]
                        con = [[1, n]]
                        dap = [[1, n]]
                    else:
                        nat = [[1, n]]
                        ri_ = [[1, d], [d, n // d]]
                        con = [[n // d, d], [1, n // d]]
                        dap = [[dpitch, d], [1, n // d]]
                    for c in range(8):
                        P.add("pe", MM(PS[ba].a(0, nat), Wt.a(c * 256 + j * 128, [[1, 128]]),
                                       xnT.a(c * XTOK + tok_off, nat), c == 0, c == 7),
                              reads=[wname, "xnT"], writes=[("ps", ba)])
                    P.add("act", ACP(dst.a(dbase, dap), PS[ba].a(0, ri_)), reads=[("ps", ba)], writes=dres)
                    ppipe.flush()
                    P.add("dve", TTO(T1[t1s].a(0, con, p0=0, np_=32), PS[ba].a(0, ri_, p0=0, np_=32),
                                     cosb.a(tok_off, ri_, p0=0, np_=32), ALU.mult),
                          reads=[("ps", ba), "cosb"], writes=[("T1", t1s)])

                    def stage2():
                        bb = 4 + cnt["b"] % 2
                        cnt["b"] += 1
                        t2s = cnt["t2"] % 2
                        cnt["t2"] += 1
                        P.add("pe", MM(PS[bb].a(0, con), permb.a(0, [[1, 128]]), dst.a(dbase, dap), True, True),
                              reads=dres + ["permb"], writes=[("ps", bb)])
                        P.add("dve", TTO(T2[t2s].a(0, con, p0=0, np_=32), PS[bb].a(0, con, p0=0, np_=32),
                                         sinb.a(tok_off, ri_, p0=0, np_=32), ALU.mult),
                              reads=[("ps", bb), "sinb"], writes=[("T2", t2s)])
                        P.add("pool", TTO(dst.a(dbase, dap, p0=0, np_=32), T1[t1s].a(0, con, p0=0, np_=32),
                                          T2[t2s].a(0, con, p0=0, np_=32), ALU.add),
                              reads=[("T1", t1s), ("T2", t2s)], writes=dres)

                    ppipe.push(stage2)

                for bt in range(lim[0]):
                    for pi, pt in enumerate(PARTS[:lim[1]]):
                        g, d, R, nkb = pt["g"], pt["d"], pt["R"], pt["nkb"]
                        kblocks, qblocks, nqb, kb0 = part_blocks(pt)
                        par = nphase % 2
                        nphase += 1
                        if pi < 2:
                            load_group_weights(bt, g + 1, nphase % 2)
                        elif bt < 7:
                            load_group_weights(bt + 1, 0, nphase % 2)
                        if pi == 0:
                            wload(WZ, "WZ", 18432 + bt * 256)
                        for j in range(2):
                            for (nb, n, tok_off) in qblocks:
                                if d == 1:
                                    qres_w = [("QT", j, b_, q_) for b_ in range(4 * nb, 4 * nb + 4) for q_ in range(4)]
                                elif d == 4:
                                    qres_w = [("QT", j, r_ * 4 + nb, q_) for r_ in range(4) for q_ in range(4)]
                                else:
                                    qres_w = [("QT", j, r_, nb) for r_ in range(16)]
                                proj_block(WQ[par], ("WQ", par), j, QT, qres_w,
                                           j * 2048 + (512 * nb) // d, n, tok_off, d=d, dpitch=2048 // d)
                        for j in range(2):
                            for (c0, n, tok_off) in kblocks:
                                blk = c0 // 512
                                if d == 1:
                                    kres_w = [("KT", j, c0 // 128 + b_, q_) for b_ in range(n // 128) for q_ in range(4)]
                                elif d == 4:
                                    kres_w = [("KT", j, r_ * nkb + blk, q_) for r_ in range(4) for q_ in range(4)]
                                else:
                                    kres_w = [("KT", j, r_ * nkb + blk // 4, blk % 4) for r_ in range(16)]
                                proj_block(WK[par], ("WK", par), j, KT, kres_w,
                                           j * 4096 + c0 // d, n, tok_off, d=d, dpitch=128 * nkb)
                        nvt = len(R) * nkb
                        for vt in range(nvt):
                            ri, kb = vt // nkb, vt % nkb
                            tok0 = R[ri] + d * 128 * (kb0 + kb)
                            if vt % 2 == 0:
                                bv = 6 + cnt["v"] % 2
                                cnt["v"] += 1
                            for c in range(8):
                                P.add("pe", MM(PS[bv].a((vt % 2) * 256, [[1, 256]]), xnT.a(c * XTOK + tok0, [[d, 128]]),
                                               WV[par].a(c * 256, [[1, 256]]), c == 0, c == 7),
                                      reads=["xnT", ("WV", par)], writes=[("ps", bv)])
                            if vt % 2 == 1 or vt == nvt - 1:
                                v0 = vt - (vt % 2)
                                nn = vt - v0 + 1
                                P.add("act", ACP(V.a(v0 * 256, [[1, nn * 256]]), PS[bv].a(0, [[1, nn * 256]])),
                                      reads=[("ps", bv)], writes=[("V", v_) for v_ in range(v0, v0 + nn)])
                        ppipe.flush()
                        apipe = Pipe(2)
                        for ri in range(len(R) if lim[2] >= 2 else 0):
                            for qb in range(nqb):
                                qs = (ri * nqb + qb) * 128
                                kprev = ri * nkb + qb
                                kcur = kprev + 1
                                bs = cnt["s"] % 4
                                cnt["s"] += 1
                                pslot = cnt["p"] % 4
                                cnt["p"] += 1
                                P.add("pe", MM(PS[bs].a(0, [[1, 512]]), identb.a(0, [[1, 128]]), mask4.a(0, [[1, 512]]), True, False),
                                      reads=["identb", "mask4"], writes=[("ps", bs)])
                                for hh, kq in enumerate((qb, qb + 1)):
                                    for j in range(2):
                                        qres = [("QT", j, qs // 128, q_) for q_ in range(4)]
                                        kres = [("KT", j, ri * nkb + kq, q_) for q_ in range(4)]
                                        P.add("pe", MM(PS[bs].a(hh * 256 + j * 128, [[1, 128]]),
                                                       KT.a(j * 4096 + (ri * nkb + kq) * 128, [[1, 128]]),
                                                       QT.a(j * 2048 + qs, [[1, 128]]), False, hh == 1 and j == 1),
                                              reads=qres + kres, writes=[("ps", bs)])
                                P.add("act", ACT(PT[pslot].a(0, [[1, 512]]), PS[bs].a(0, [[1, 512]]), AF.Exp, scale=scale_qk),
                                      reads=[("ps", bs)], writes=[("PT", pslot)])

                                def stage2(ri=ri, qb=qb, kprev=kprev, kcur=kcur, pslot=pslot):
                                    bo = 4 + cnt["o"] % 4
                                    cnt["o"] += 1
                                    for j in range(2):
                                        for hh, kbk in enumerate((kprev, kcur)):
                                            P.add("pe", MM(PS[bo].a(j * 128, [[1, 128]]), V.a(kbk * 256 + j * 128, [[1, 128]]),
                                                           PT[pslot].a(hh * 256 + j * 128, [[1, 128]]), hh == 0, hh == 1),
                                                  reads=[("V", kbk), ("PT", pslot)], writes=[("ps", bo)])
                                    for hh in range(2):
                                        ones_t, ones_n = (onesH, "onesH") if (hh == 0 and qb == 0) else (onesb, "onesb")
                                        P.add("pe", MM(PS[bo].a(256, [[1, 256]]), ones_t.a(0, [[1, 128]]),
                                                       PT[pslot].a(hh * 256, [[1, 256]]), hh == 0, hh == 1),
                                              reads=[ones_n, ("PT", pslot)], writes=[("ps", bo)])
                                    tq0 = R[ri] + d * 128 * (16 // d + qb) - 2048
                                    ndres = [("ND", j, b_) for j in range(2) for b_ in range(tq0 // 128, tq0 // 128 + d)]
                                    ndap = ND.a(tq0, [[4096, 2], [2048, 2], [d, 128]])
                                    psap = PS[bo].a(0, [[256, 2], [128, 2], [1, 128]])
                                    if pi == 0:
                                        P.add("dve", CP(ndap, psap), reads=[("ps", bo)], writes=ndres)
                                    else:
                                        P.add("dve", TTO(ndap, psap, ndap, ALU.add),
                                              reads=[("ps", bo)] + ndres, writes=ndres)

                                apipe.push(stage2)
                        apipe.flush()
                        if True:
                            for j in range(2):
                                proj_block(WQ[par], ("WQ", par), j, QX, [("QX", j, 0)], j * 16, 16, 4096)
                                proj_block(WK[par], ("WK", par), j, KX, [("KX", j, 0)], j * 128, 128, 4096)
                            bv = 6 + cnt["v"] % 2
                            cnt["v"] += 1
                            for c in range(8):
                                P.add("pe", MM(PS[bv].a(0, [[1, 256]]), xnT.a(c * XTOK + 4096, [[1, 128]]),
                                               WV[par].a(c * 256, [[1, 256]]), c == 0, c == 7),
                                      reads=["xnT", ("WV", par)], writes=[("ps", bv)])
                            P.add("act", ACP(VX.a(0, [[1, 256]]), PS[bv].a(0, [[1, 256]])),
                                  reads=[("ps", bv)], writes=["VX"])
                            ppipe.flush()
                        nR = len(R)
                        for j in range(2):
                            bs = cnt["s"] % 4
                            cnt["s"] += 1
                            bo = 4 + cnt["o"] % 4
                            cnt["o"] += 1
                            for ri in range(nR):
                                kq = nkb - 1
                                kres = [("KT", j, ri * nkb + kq, q_) for q_ in range(4)]
                                P.add("pe", MM(PS[bs].a(ri * 16, [[1, 16]]), KT.a(j * 4096 + (ri * nkb + kq) * 128, [[1, 128]]),
                                               QX.a(j * 16, [[1, 16]]), True, True),
                                      reads=[("QX", j, 0)] + kres, writes=[("ps", bs)])
                            P.add("pe", MM(PS[bs].a(256, [[1, 16]]), KX.a(j * 128, [[1, 128]]),
                                           QX.a(j * 16, [[1, 16]]), True, True),
                                  reads=[("QX", j, 0), ("KX", j, 0)], writes=[("ps", bs)])
                            P.add("act", ACT(PTX.a(0, [[1, nR * 16]]), PS[bs].a(0, [[1, nR * 16]]), AF.Exp, scale=scale_qk),
                                  reads=[("ps", bs)], writes=["PTX"])
                            P.add("act", ACT(PTC.a(0, [[1, 16]]), PS[bs].a(256, [[1, 16]]),
                                             AF.Exp, scale=scale_qk),
                                  reads=[("ps", bs)], writes=["PTC"])
                            P.add("pool", TTO(PTX.a(0, [[1, nR * 16]]), PTX.a(0, [[1, nR * 16]]),
                                              MX.a(MXP_OFF[pi], [[1, nR * 16]]), ALU.mult),
                                  reads=["PTX", "MX"], writes=["PTX"])
                            P.add("pool", TTO(PTC.a(0, [[1, 16]]), PTC.a(0, [[1, 16]]),
                                              MX.a(336 + pi * 16, [[1, 16]]), ALU.mult),
                                  reads=["PTC", "MX"], writes=["PTC"])
                            for which in range(2):
                                for ri in range(nR):
                                    kbl = ri * nkb + nkb - 1
                                    lhs = V.a(kbl * 256 + j * 128, [[1, 128]]) if which == 0 else onesb.a(0, [[1, 128]])
                                    P.add("pe", MM(PS[bo].a(which * 16, [[1, 16]]), lhs, PTX.a(ri * 16, [[1, 16]]),
                                                   ri == 0, False),
                                          reads=[("V", kbl), "PTX", "onesb"], writes=[("ps", bo)])
                                lhs = VX.a(j * 128, [[1, 128]]) if which == 0 else onesb.a(0, [[1, 128]])
                                P.add("pe", MM(PS[bo].a(which * 16, [[1, 16]]), lhs, PTC.a(0, [[1, 16]]),
                                               False, True),
                                      reads=["VX", "PTC", "onesb"], writes=[("ps", bo)])
                            ndx = NDX.a(j * 16, [[32, 2], [1, 16]])
                            if pi == 0:
                                P.add("dve", CP(ndx, PS[bo].a(0, [[16, 2], [1, 16]])), reads=[("ps", bo)], writes=[("NDX", j)])
                            else:
                                P.add("dve", TTO(ndx, PS[bo].a(0, [[16, 2], [1, 16]]), ndx, ALU.add),
                                      reads=[("ps", bo), ("NDX", j)], writes=[("NDX", j)])
                    if lim[2] < 3:
                        continue
                    for j in range(2):
                        for hq in range(2):
                            r_ = [("ND", j, b_) for b_ in range(hq * 8, hq * 8 + 8)]
                            apx = ND.a(4096 + j * 2048 + hq * 1024, [[1, 1024]])
                            P.add("act", ACT(apx, apx, AF.Ln), reads=r_, writes=r_)
                            P.add("act", ACT(apx, apx, AF.Exp, scale=-1.0), reads=r_, writes=r_)
                    for j in range(2):
                        for tb in range(4):
                            bz = cnt["z"] % 3
                            cnt["z"] += 1
                            zs = cnt["z"] % 2
                            for c in range(8):
                                P.add("pe", MM(PS[bz].a(0, [[1, 512]]), WZ.a(c * 256 + j * 128, [[1, 128]]),
                                               xnT.a(c * XTOK + 2048 + tb * 512, [[1, 512]]), c == 0, c == 7),
                                      reads=["WZ", "xnT"], writes=[("ps", bz)])
                            P.add("act", ACT(SZ[zs].a(0, [[1, 512]]), PS[bz].a(0, [[1, 512]]), AF.Silu),
                                  reads=[("ps", bz)], writes=[("SZ", zs)])
                            r_ = [("ND", j, b_) for b_ in range(tb * 4, tb * 4 + 4)]
                            nap = ND.a(j * 2048 + tb * 512, [[1, 512]])
                            dap_ = ND.a(4096 + j * 2048 + tb * 512, [[1, 512]])
                            P.add("dve", TTO(nap, nap, dap_, ALU.mult), reads=r_, writes=r_)
                            P.add("dve", TTO(NDb.a(8192 + j * 2048 + tb * 512, [[1, 512]]), nap, SZ[zs].a(0, [[1, 512]]), ALU.mult),
                                  reads=r_ + [("SZ", zs)], writes=[("YGS", j, tb)] + [("ND", 0, b_) for b_ in range(16)])
                    P.add("sp", DMA(DAP(ygd, 2 * bt * 128 * YTOK, [[YTOK, 128], [128 * YTOK, 2], [1, 2048]]),
                                    NDb.a(8192, [[2048, 2], [1, 2048]])),
                          reads=[("YGS", j, tb) for j in range(2) for tb in range(4)] + [("ND", 0, b_) for b_ in range(16)],
                          dma=True)
                    P.add("dve", RECIP(NDX.a(32, [[1, 32]]), NDX.a(32, [[1, 32]])), reads=[("NDX", 0), ("NDX", 1)],
                          writes=[("NDX", 0), ("NDX", 1)])
                    P.add("dve", TTO(NDX.a(0, [[1, 32]]), NDX.a(0, [[1, 32]]), NDX.a(32, [[1, 32]]), ALU.mult),
                          reads=[("NDX", 0), ("NDX", 1)], writes=[("NDX", 0), ("NDX", 1)])
                    for j in range(2):
                        bz = cnt["z"] % 3
                        cnt["z"] += 1
                        for c in range(8):
                            P.add("pe", MM(PS[bz].a(0, [[1, 16]]), WZ.a(c * 256 + j * 128, [[1, 128]]),
                                           xnT.a(c * XTOK + 4096, [[1, 16]]), c == 0, c == 7),
                                  reads=["WZ", "xnT"], writes=[("ps", bz)])
                        P.add("act", ACT(SZX.a(0, [[1, 16]]), PS[bz].a(0, [[1, 16]]), AF.Silu),
                              reads=[("ps", bz)], writes=["SZX"])
                        P.add("dve", TTO(YGX.a(j * 16, [[1, 16]]), NDX.a(j * 16, [[1, 16]]), SZX.a(0, [[1, 16]]), ALU.mult),
                              reads=[("NDX", j), "SZX"], writes=["YGX"])
                    P.add("sp", DMA(DAP(ygd, 2 * bt * 128 * YTOK + 2048, [[YTOK, 128], [128 * YTOK, 2], [1, 16]]),
                                    YGX.a(0, [[16, 2], [1, 16]])),
                          reads=["YGX"], dma=True)
            P.barrier()

        with contextlib.ExitStack() as sX:
            xnT1 = cx.sb(sX, "xnT1", 8 * NT1, BF16)
            with contextlib.ExitStack() as sC:
                PS = [cx.ps(sC, "PSc%d" % i, 512, F32) for i in range(6)]
                YG = cx.sb(sC, "YG", 16 * YTOK, BF16)
                WO = cx.sb(sC, "WO", 16 * 1024, BF16)
                gpb = cx.sb(sC, "gpb", 1024, F32)
                gb1 = cx.sb(sC, "gb1", 1024, F32)
                P.add("sp", DMA(gpb.a(0, [[1, 1024]]), DAP(gpost, 0, [[0, 128], [1, 1024]])), writes=["gpb"], dma=True)
                P.add("sp", DMA(gb1.a(0, [[1, 1024]]), DAP(gpre1, 0, [[0, 128], [1, 1024]])), writes=["gb"], dma=True)
                for q in range(4):
                    P.add("pool", DMA(WO.a(q * 4096, [[1024, 4], [1, 1024]]),
                                      DAP(wout, q * 4 * 128 * 1024, [[1024, 128], [128 * 1024, 4], [1, 1024]])),
                          writes=[("WO", q)], dma=True)
                for r in range(5):
                    n_ = 512 if r < 4 else 16
                    P.add("sp", DMA(YG.a(512 * r, [[YTOK, 16], [1, n_]]),
                                    DAP(ygd, 512 * r, [[YTOK, 128], [128 * YTOK, 16], [1, n_]])),
                          writes=[("YG", "r", r)], dma=True)
                P.add("dve", MSET(YG.a(2064, [[YTOK, 16], [1, YTOK - 2064]]), 0.0), writes=[("YG", "r", 4)])
                st_a, st_b, st_c = emit_norm_transpose(cx, sC, h1, 17, xnT1, NT1, gb1, identb, ntp=2,
                                                       src_res=lambda t: ("outd", t), manual=True)
                LAG = 3

                def post(tt):
                    its = [tt - LAG] if tt is not None else list(range(17 - LAG, 17 + 2))
                    for it in its:
                        if 0 <= it < 17:
                            st_a(it)
                        if 0 <= it - 1 < 17:
                            st_b(it - 1)
                        if 0 <= it - 2 < 17:
                            st_c(it - 2)

                emit_out_phase(cx, sC, PS, YG, YTOK, WO, xin, 2048, gpb, h1, ntiles=17,
                               yg_res=lambda ec, tt: [("YG", "r", min(tt // 4, 4))], post=post)
            P.barrier()
            emit_l1(cx, sX, h1, gpre1, gpost1, win1, wgrp, wout1, cst1, outd, identb, xnT_pre=xnT1)
        P.run(nc, s0)
    return nc


POOL_W = (2, 4, 8, 16)
NT1 = 2176


def emit_l1(cx, s0, hin, gpre, gpost, win, wgrp, wout, cst, out, identb, xnT_pre=None):
    P = cx.P
    CW1 = 128 + 96
    if True:
        cf = cx.sb(s0, "cf1", 144, F32)
        YG = cx.sb(s0, "YG1", 16 * NT1, BF16)
        P.add("sp", DMA(cf.a(0, [[1, 96]]), DAP(cst, 128, [[CW1, 128], [1, 96]])), writes=["cf"], dma=True)
        for g_ in range(4):
            P.add("dve", TS(cf.a(96 + 4 * g_, [[1, 4]]), cf.a(16 + 4 * g_, [[1, 4]]), 1.0 / POOL_W[g_], ALU.mult),
                  reads=["cf"], writes=["cf"])
        P.add("dve", TS(cf.a(112, [[1, 16]]), cf.a(16, [[1, 16]]), -1.0, ALU.mult), reads=["cf"], writes=["cf"])
        P.add("dve", TTO(cf.a(128, [[1, 16]]), cf.a(0, [[1, 16]]), cf.a(16, [[1, 16]]), ALU.mult), reads=["cf"], writes=["cf"])
        with contextlib.ExitStack() as s1:
            if xnT_pre is not None:
                xnT = xnT_pre
            else:
                xnT = cx.sb(s1, "xnT1", 8 * NT1, BF16)
                with contextlib.ExitStack() as sA:
                    gb = cx.sb(sA, "gb1", 1024, F32)
                    P.add("sp", DMA(gb.a(0, [[1, 1024]]), DAP(gpre, 0, [[0, 128], [1, 1024]])), writes=["gb"], dma=True)
                    emit_norm_transpose(cx, sA, hin, 17, xnT, NT1, gb, identb)
                P.barrier()
            with contextlib.ExitStack() as sB:
                PS = [cx.ps(sB, "PS%d" % i, 512, F32) for i in range(8)]
                WG = [cx.sb(sB, "WG%d" % i, 4 * 512, BF16) for i in range(2)]
                WU = [cx.sb(sB, "WU%d" % i, 8 * 128, BF16) for i in range(2)]
                WZ = [cx.sb(sB, "WZ%d" % i, 8 * 128, BF16) for i in range(2)]
                UB = cx.sb(sB, "UB", 4 * NT1, BF16)
                UW = 16 + NT1
                VV = [cx.sb(sB, "VV%d" % i, UW, F32) for i in range(2)]
                SA = cx.sb(sB, "SA", UW, F32)
                SBf = cx.sb(sB, "SB", UW, F32)
                SZ = [cx.sb(sB, "SZ%d" % i, NT1, F32) for i in range(2)]
                for i in range(2):
                    P.add("dve", MSET(VV[i].a(0, [[1, 16]]), 0.0), writes=[("VV", i, 0)])
                A1 = [cx.sb(sB, "A1%d" % i, UW, F32) for i in range(2)]
                A0 = cx.sb(sB, "A0", 32, F32)
                HALF = (("dve", 0, UW),)
                blocks = [(tb * 512, 512) for tb in range(4)] + [(2048, 128)]
                cnt = dict(a=0, z=0, h=0)

                def both(rd):
                    return [(rd[0], rd[1], 0)]

                for g in range(4):
                    w = POOL_W[g]
                    nsteps = g + 1
                    gp = g % 2
                    P.add("pool", DMA(WG[gp].a(0, [[512, 4], [1, 512]]),
                                      DAP(wgrp, g * 512 * 512, [[512, 128], [128 * 512, 4], [1, 512]])),
                          writes=[("WG", gp)], dma=True)
                    for ic in range(4):
                        uc = g * 4 + ic
                        ws = uc % 2
                        P.add("pool", DMA(WU[ws].a(0, [[128, 8], [1, 128]]),
                                          DAP(win, uc * 128, [[4096, 128], [128 * 4096, 8], [1, 128]])),
                              writes=[("WU", ws)], dma=True)
                        for (t0, n) in blocks:
                            ba = cnt["a"] % 3
                            cnt["a"] += 1
                            for c in range(8):
                                P.add("pe", MM(PS[ba].a(0, [[1, n]]), WU[ws].a(c * 128, [[1, 128]]),
                                               xnT.a(c * NT1 + t0, [[1, n]]), c == 0, c == 7),
                                      reads=[("WU", ws), "xnT"], writes=[("ps", ba)])
                            P.add("act", ACP(UB.a(ic * NT1 + t0, [[1, n]]), PS[ba].a(0, [[1, n]])),
                                  reads=[("ps", ba)], writes=[("UB", ic, t0)])
                    for oc in range(4):
                        e_ = g * 4 + oc
                        zs = e_ % 2
                        v = e_ % 2
                        P.add("pool", DMA(WZ[zs].a(0, [[128, 8], [1, 128]]),
                                          DAP(win, 2048 + e_ * 128, [[4096, 128], [128 * 4096, 8], [1, 128]])),
                              writes=[("WZ", zs)], dma=True)
                        for (t0, n) in blocks:
                            bh = 3 + cnt["h"] % 2
                            cnt["h"] += 1
                            for ic in range(4):
                                P.add("pe", MM(PS[bh].a(0, [[1, n]]), WG[gp].a(ic * 512 + oc * 128, [[1, 128]]),
                                               UB.a(ic * NT1 + t0, [[1, n]]), ic == 0, ic == 3),
                                      reads=[("WG", gp), ("UB", ic, t0)], writes=[("ps", bh)])
                            hv = 0
                            wres = [("VV", v, 0), ("VV", v, 1)] if hv == 2 else [("VV", v, hv)]
                            P.add("act", ACP(VV[v].a(16 + t0, [[1, n]]), PS[bh].a(0, [[1, n]])),
                                  reads=[("ps", bh)], writes=wres)
                            P.add("act", ACT(A1[v].a(16 + t0, [[1, n]]), PS[bh].a(0, [[1, n]]), AF.Identity,
                                             scale=cf.a(112 + e_, [[1, 1]]), bias=cf.a(128 + e_, [[1, 1]])),
                                  reads=[("ps", bh), "cf"], writes=[("A1", v)])
                        for (t0, n) in blocks:
                            bz = 5 + cnt["z"] % 3
                            cnt["z"] += 1
                            for c in range(8):
                                P.add("pe", MM(PS[bz].a(0, [[1, n]]), WZ[zs].a(c * 128, [[1, 128]]),
                                               xnT.a(c * NT1 + t0, [[1, n]]), c == 0, c == 7),
                                      reads=[("WZ", zs), "xnT"], writes=[("ps", bz)])
                            hv = 0
                            wres = [("SZ", v, 0), ("SZ", v, 1)] if hv == 2 else [("SZ", v, hv)]
                            P.add("act", ACT(SZ[v].a(t0, [[1, n]]), PS[bz].a(0, [[1, n]]), AF.Silu),
                                  reads=[("ps", bz)], writes=wres)
                        P.add("dve", CP(A0.a(16, [[1, 16]]), A1[v].a(16, [[1, 16]])), reads=[("A1", v)], writes=["A0"])
                        src, srcn = VV[v], ("VV", v)
                        m = 1
                        for st in range(nsteps):
                            dst, dstn = (SA, ("SA", 0)) if st % 2 == 0 else (SBf, ("SB", 0))
                            lo = 2 * m - 1
                            for hi_, (eng, c0, c1) in enumerate(HALF):
                                c0_ = max(c0, lo)
                                P.add(eng, TTO(dst.a(c0_, [[1, c1 - c0_]]), src.a(c0_, [[1, c1 - c0_]]),
                                               src.a(c0_ - m, [[1, c1 - c0_]]), ALU.add),
                                      reads=both(srcn) if hi_ == 1 else [(srcn[0], srcn[1], 0)],
                                      writes=[(dstn[0], dstn[1], hi_)])
                            src, srcn = dst, dstn
                            m *= 2
                        for hi_, (eng, c0, c1) in enumerate(HALF):
                            c0_ = max(c0, 16)
                            rs = [(srcn[0], srcn[1], hi_)]
                            a1s = [("A1", v)]
                            if eng == "dve":
                                P.add(eng, STT(A1[v].a(c0_, [[1, c1 - c0_]]), src.a(c0_, [[1, c1 - c0_]]), cf.a(96 + e_, [[1, 1]]),
                                               A1[v].a(c0_, [[1, c1 - c0_]]), ALU.mult, ALU.add),
                                      reads=rs + a1s + ["cf"], writes=a1s)
                                P.add(eng, TTO(src.a(16, [[1, 16]]), src.a(16, [[1, 16]]), cf.a(32 + g * 16, [[1, 16]]), ALU.mult),
                                      reads=rs + ["cf"], writes=rs)
                                P.add(eng, STT(A1[v].a(16, [[1, 16]]), src.a(16, [[1, 16]]), cf.a(16 + e_, [[1, 1]]),
                                               A0.a(16, [[1, 16]]), ALU.mult, ALU.add),
                                      reads=rs + ["A0", "cf"], writes=a1s)
                            else:
                                P.add(eng, TS(src.a(c0_, [[1, c1 - c0_]]), src.a(c0_, [[1, c1 - c0_]]), cf.a(96 + e_, [[1, 1]]), ALU.mult),
                                      reads=rs + ["cf"], writes=rs)
                                P.add(eng, TTO(A1[v].a(c0_, [[1, c1 - c0_]]), A1[v].a(c0_, [[1, c1 - c0_]]), src.a(c0_, [[1, c1 - c0_]]), ALU.add),
                                      reads=rs + a1s, writes=a1s)
                            P.add(eng, TTO(YG.a(e_ * NT1 + c0_ - 16, [[1, c1 - c0_]]), A1[v].a(c0_, [[1, c1 - c0_]]),
                                           SZ[v].a(c0_ - 16, [[1, c1 - c0_]]), ALU.mult),
                                  reads=a1s + [("SZ", v, hi_)], writes=[("YG", e_)])
            P.barrier()
        with contextlib.ExitStack() as sC:
            PS = [cx.ps(sC, "PSc%d" % i, 512, F32) for i in range(8)]
            WO = cx.sb(sC, "WO", 16 * 1024, BF16)
            gpb = cx.sb(sC, "gpb", 1024, F32)
            P.add("sp", DMA(gpb.a(0, [[1, 1024]]), DAP(gpost, 0, [[0, 128], [1, 1024]])), writes=["gpb"], dma=True)
            for q in range(4):
                P.add("pool", DMA(WO.a(q * 4096, [[1024, 4], [1, 1024]]),
                                  DAP(wout, q * 4 * 128 * 1024, [[1024, 128], [128 * 1024, 4], [1, 1024]])),
                      writes=[("WO", q)], dma=True)
            emit_out_phase(cx, sC, PS, YG, NT1, WO, hin, 0, gpb, out, ntiles=17, sfx="1")


def _consts_l0(first_chunk):
    c = np.zeros((128, CW), np.float32)
    c[:, 0:128] = np.eye(128, dtype=np.float32)
    for m in range(16):
        c[m + 16, 128 + m] = 1.0
        c[m, 128 + 16 + m] = 1.0
    k = np.arange(128)[:, None]
    q = np.arange(128)[None, :]
    c[:, 256:384] = (k >= q).astype(np.float32)
    c[:, 384:512] = (k <= q).astype(np.float32)
    c[:, 512:640] = 0.0 if first_chunk else 1.0
    invf = (np.float32(500000.0) ** (-np.arange(0, 32, 2, dtype=np.float32) / np.float32(32.0))).astype(np.float32)
    p = np.arange(128)
    c[:, 640] = invf[p % 16]
    c[:, 641] = np.where((p % 32) < 16, -1.0, 1.0)
    off = 644
    q16 = np.arange(16)[None, :]
    for pt in PARTS:
        d = pt["d"]
        for r in pt["R"]:
            c[:, off:off + 16] = ((q16 % d == r) & (k >= q16 // d)).astype(np.float32)
            off += 16
    assert off == 644 + 336
    k16 = np.arange(128)[:, None]
    for pt in PARTS:
        d = pt["d"]
        inR = np.isin(q16 % d, np.array(pt["R"]))
        c[:, off:off + 16] = ((k16 <= q16) & ((q16 - k16) % d == 0) & inR & (k16 < 16)).astype(np.float32)
        off += 16
    return c


def _consts_l1(b_grp, scale, chunk):
    c = np.zeros((128, 128 + 96), np.float32)
    c[:, 0:128] = np.eye(128, dtype=np.float32)
    c[:, 128:144] = b_grp.reshape(16, 128).T
    c[:, 144:160] = scale.reshape(16, 128).T
    t = np.arange(16) + chunk * 2048
    for g, w in enumerate(POOL_W):
        c[:, 160 + g * 16:160 + (g + 1) * 16] = (1.0 / np.minimum(t + 1, w)).astype(np.float32)[None, :]
    return c


_NC_CACHE = {}


def make_in_maps(x, positions, norm_pre, norm_post, attn_w_in, attn_w_out,
                 pool_w_in, pool_w_grp, pool_b_grp, pool_scale, pool_w_out, ncores=8):
    B, S, _ = x.shape
    cpb = ncores // B
    f = lambda a: np.ascontiguousarray(np.asarray(a, dtype=np.float32))
    shared = dict(
        gpre=f(norm_pre[0:1]), gpost=f(norm_post[0:1]), win=f(attn_w_in[0]), wout=f(attn_w_out[0]),
        gpre1=f(norm_pre[1:2]), gpost1=f(norm_post[1:2]), win1=f(pool_w_in[0]),
        wgrp=f(np.asarray(pool_w_grp[0]).reshape(2048, 512)), wout1=f(pool_w_out[0]))
    in_maps = []
    for core in range(ncores):
        b, ch = core // cpb, core % cpb
        s0 = ch * 2048
        xin = np.zeros((XTOK, D), np.float32)
        pos = np.zeros((1, XTOK), np.int32)
        xin[2048:4096] = x[b, s0:s0 + 2048]
        pos[0, 2048:4096] = positions[b, s0:s0 + 2048]
        if ch > 0:
            xin[:2048] = x[b, s0 - 2048:s0]
            pos[0, :2048] = positions[b, s0 - 2048:s0]
        if s0 + 2048 < S:
            xin[4096:4112] = x[b, s0 + 2048:s0 + 2064]
            pos[0, 4096:4112] = positions[b, s0 + 2048:s0 + 2064]
        m = dict(shared)
        m.update(xin=xin, pos=pos, cst=_consts_l0(ch == 0),
                 cst1=_consts_l1(np.asarray(pool_b_grp[0], np.float32), np.asarray(pool_scale[0], np.float32), ch))
        in_maps.append(m)
    return in_maps


def kernel(x, positions, norm_pre, norm_post, attn_w_in, attn_w_out,
           pool_w_in, pool_w_grp, pool_b_grp, pool_scale, pool_w_out):
    x = np.ascontiguousarray(np.asarray(x, dtype=np.float32))
    positions = np.asarray(positions).astype(np.int32)
    B, S, _ = x.shape
    ncores = 8
    cpb = ncores // B
    in_maps = make_in_maps(x, positions, norm_pre, norm_post, attn_w_in, attn_w_out,
                           pool_w_in, pool_w_grp, pool_b_grp, pool_scale, pool_w_out, ncores)
    if "f" not in _NC_CACHE:
        _NC_CACHE["f"] = build_fused()
    res = run_bass_kernel_spmd(_NC_CACHE["f"], in_maps, core_ids=list(range(ncores)))
    out = np.zeros((B, S, D), np.float32)
    for core in range(ncores):
        b, ch = core // cpb, core % cpb
        s0 = ch * 2048
        o = res.results[core]["out"]
        lo = 0 if ch == 0 else 16
        hi = min(2064, S - s0)
        out[b, s0 + lo:s0 + hi] = o[lo:hi]
    return out
```

```python
import contextlib
import os
import math
import numpy as np
import concourse.bass as bass
import concourse.mybir as mybir
from concourse.bass_utils import run_bass_kernel_spmd

F32 = mybir.dt.float32
BF16 = mybir.dt.bfloat16
I32 = mybir.dt.int32
AF = mybir.ActivationFunctionType
ALU = mybir.AluOpType
AX = mybir.AxisListType

ENGS = ("pe", "act", "dve", "pool", "sp")
DMAQ = ("act", "pool", "sp")
NDS = 8

D = 1024
E = 2048
T_OWN = 2048
EPS = 1e-6
TWO_PI = 2.0 * math.pi
INV2PI = 1.0 / TWO_PI
C1 = 6.28125
C2 = TWO_PI - C1
MAGIC = 12582912.0
PI_LO = 3.1415925
CW = 1044
XTOK = 4224
YTOK = 2176


class Op:
    __slots__ = ("eng", "fn", "deps", "dma", "signal", "sem", "val", "prev")

    def __init__(self, eng, fn, dma):
        self.eng = eng
        self.fn = fn
        self.dma = dma
        self.deps = []
        self.signal = False
        self.sem = None
        self.val = 0
        self.prev = None


class Prog:
    def __init__(self):
        self.ops = {e: [] for e in ENGS}
        self.lastw = {}
        self.readers = {}
        self.bar_idx = {e: 0 for e in ENGS}

    def add(self, eng, fn, reads=(), writes=(), dma=False):
        op = Op(eng, fn, dma)
        deps = {}
        psr = [r for r in reads if isinstance(r, tuple) and r[0] == "ps"]
        if psr:
            reads = [r for r in reads if not (isinstance(r, tuple) and r[0] == "ps")]
            writes = list(writes) + psr

        def need(d, raw):
            if d is None:
                return
            if (not dma) and (not d.dma) and d.eng == eng:
                if eng == "pe" or not raw:
                    return
            deps[id(d)] = d

        for r in reads:
            need(self.lastw.get(r), True)
        for r in writes:
            need(self.lastw.get(r), False)
            rd = self.readers.get(r)
            if rd:
                for k, v in rd.items():
                    if k == "dma":
                        for d in v:
                            need(d, False)
                    else:
                        need(v, False)
        for d in deps.values():
            d.signal = True
        op.deps = list(deps.values())
        for r in reads:
            rd = self.readers.setdefault(r, {})
            if dma:
                rd.setdefault("dma", []).append(op)
            else:
                rd[eng] = op
        for r in writes:
            self.lastw[r] = op
            self.readers[r] = {}
        self.ops[eng].append(op)
        return op

    def barrier(self):
        lasts = []
        for e in ENGS:
            for op in reversed(self.ops[e]):
                if (not op.dma) and op.fn is not None:
                    lasts.append(op)
                    break
        dmas = [op for e in ENGS for op in self.ops[e][self.bar_idx[e]:] if op.dma]
        for e in ENGS:
            w = Op(e, None, False)
            w.deps = [d for d in lasts if d.eng != e] + dmas
            for d in w.deps:
                d.signal = True
            self.ops[e].append(w)
        self.bar_idx = {e: len(self.ops[e]) for e in ENGS}
        self.lastw.clear()
        self.readers.clear()

    def finalize(self, nc, stack):
        self.psem = {e: stack.enter_context(nc.semaphore("p_" + e)) for e in ENGS}
        self.dsem = {e: [stack.enter_context(nc.semaphore("d_%s%d" % (e, i))) for i in range(NDS)]
                     for e in DMAQ}
        self.semobj = {}
        for e in ENGS:
            self.semobj[("p", e)] = self.psem[e]
        for e in DMAQ:
            for i in range(NDS):
                self.semobj[("d", e, i)] = self.dsem[e][i]
        for e in ENGS:
            cnt = 0
            di = 0
            for op in self.ops[e]:
                if op.fn is None:
                    continue
                if op.dma:
                    op.sem = ("d", e, di % NDS)
                    op.val = 16 * (di // NDS + 1)
                    op.prev = (op.sem, op.val - 16) if di >= NDS else None
                    di += 1
                elif op.signal:
                    cnt += 1
                    op.sem = ("p", e)
                    op.val = cnt

    def emit(self, eng, e):
        waited = {}
        for op in self.ops[eng]:
            w = {}
            for d in op.deps:
                if w.get(d.sem, 0) < d.val:
                    w[d.sem] = d.val
            if op.dma and op.prev is not None:
                if w.get(op.prev[0], 0) < op.prev[1]:
                    w[op.prev[0]] = op.prev[1]
            for sem, val in w.items():
                if waited.get(sem, 0) < val:
                    e.wait_ge(self.semobj[sem], val)
                    waited[sem] = val
            if op.fn is not None:
                ins = op.fn(e)
                if op.dma:
                    ins.then_inc(self.semobj[op.sem], 16)
                elif op.signal:
                    ins.then_inc(self.semobj[op.sem], 1)

    def run(self, nc, stack):
        self.barrier()
        self.finalize(nc, stack)
        block = stack.enter_context(nc.Block())

        @block.tensor
        def _(e):
            self.emit("pe", e)

        @block.scalar
        def _(e):
            self.emit("act", e)

        @block.vector
        def _(e):
            self.emit("dve", e)

        @block.gpsimd
        def _(e):
            self.emit("pool", e)

        @block.sync
        def _(e):
            self.emit("sp", e)


class TT:
    def __init__(self, t, pitch):
        self.t = t
        self.pitch = pitch

    def a(self, off, dims, p0=0, np_=128):
        return bass.AP(self.t, p0 * self.pitch + off, [[self.pitch, np_]] + [list(d) for d in dims])


def DAP(t, off, dims):
    return bass.AP(t, off, [list(d) for d in dims])


def MM(out, lhsT, rhs, start, stop):
    return lambda e: e.matmul(out, lhsT=lhsT, rhs=rhs, start=start, stop=stop)


def TR(out, in_, ident):
    return lambda e: e.transpose(out, in_, ident)


def ACT(out, in_, func, scale=None, bias=None):
    kw = {}
    if scale is not None:
        kw["scale"] = scale
    if bias is not None:
        kw["bias"] = bias
    return lambda e: e.activation(out=out, in_=in_, func=func, **kw)


def TTO(out, in0, in1, op):
    return lambda e: e.tensor_tensor(out=out, in0=in0, in1=in1, op=op)


def TS(out, in0, s1, op0, s2=None, op1=None):
    if op1 is None:
        return lambda e: e.tensor_scalar(out=out, in0=in0, scalar1=s1, scalar2=None, op0=op0)
    return lambda e: e.tensor_scalar(out=out, in0=in0, scalar1=s1, scalar2=s2, op0=op0, op1=op1)


def STT(out, in0, scalar, in1, op0, op1):
    return lambda e: e.scalar_tensor_tensor(out=out, in0=in0, scalar=scalar, in1=in1, op0=op0, op1=op1)


def CP(out, in_):
    return lambda e: e.tensor_copy(out=out, in_=in_)


def ACP(out, in_):
    return lambda e: e.activation(out=out, in_=in_, func=AF.Copy)


def RECIP(out, in_):
    return lambda e: e.reciprocal(out=out, in_=in_)


def RECIPF(out, in_):
    return lambda e: e.reciprocal_approx_fast(out=out, in_=in_)


def RSUM(out, in_):
    return lambda e: e.reduce_sum(out=out, in_=in_, axis=AX.X)


def MSET(ap, c):
    return lambda e: e.memset(ap, c)


def DMA(out, in_):
    return lambda e: e.dma_start(out=out, in_=in_)


class Ctx:
    def __init__(self, nc):
        self.nc = nc
        self.P = Prog()

    def sb(self, stack, name, free, dt):
        self.n = getattr(self, "n", 0) + 1
        t = stack.enter_context(self.nc.sbuf_tensor("%s_%d" % (name, self.n), [128, free], dt))
        return TT(t, free)

    def ps(self, stack, name, free, dt):
        self.n = getattr(self, "n", 0) + 1
        t = stack.enter_context(self.nc.psum_tensor("%s_%d" % (name, self.n), [128, free], dt))
        return TT(t, free)


def emit_norm_transpose(cx, stack, xsrc, ntiles, xnT, xnT_tok, gb, identb, row0=0, keep=None, hook=None):
    P = cx.P
    NS = 4
    XT = [cx.sb(stack, "XT%d" % i, 1024, F32) for i in range(NS)]
    SQ = [cx.sb(stack, "SQ%d" % i, 1024, F32) for i in range(2)]
    XNB = [cx.sb(stack, "XNB%d" % i, 1024, BF16) for i in range(NS)]
    ST = cx.sb(stack, "STn", 3 * 64, F32)
    TP = [cx.ps(stack, "TP%d" % i, 1024, BF16) for i in range(NS)]

    def stage_a(tt):
        s, s2 = tt % NS, tt % 2
        P.add("sp", DMA(XT[s].a(0, [[1, 1024]]), DAP(xsrc, (row0 + tt * 128) * 1024, [[1024, 128], [1, 1024]])),
              writes=[("XT", s)], dma=True)
        P.add("act", ACT(SQ[s2].a(0, [[1, 1024]]), XT[s].a(0, [[1, 1024]]), AF.Square),
              reads=[("XT", s)], writes=[("SQ", s2)])
        P.add("dve", RSUM(ST.a(tt, [[1, 1]]), SQ[s2].a(0, [[1, 1024]])), reads=[("SQ", s2)], writes=[("st0", tt)])

    def stage_b(tt):
        s = tt % NS
        P.add("act", ACT(ST.a(64 + tt, [[1, 1]]), ST.a(tt, [[1, 1]]), AF.Sqrt, scale=1.0 / D, bias=EPS),
              reads=[("st0", tt)], writes=[("st1", tt)])
        P.add("dve", RECIP(ST.a(128 + tt, [[1, 1]]), ST.a(64 + tt, [[1, 1]])), reads=[("st1", tt)], writes=[("st2", tt)])
        P.add("dve", STT(XNB[s].a(0, [[1, 1024]]), XT[s].a(0, [[1, 1024]]), ST.a(128 + tt, [[1, 1]]),
                         gb.a(0, [[1, 1024]]), ALU.mult, ALU.mult),
              reads=[("XT", s), ("st2", tt), "gb"], writes=[("XNB", s)])

    def stage_c(tt):
        s = tt % NS
        for c in range(8):
            P.add("pe", TR(TP[s].a(c * 128, [[1, 128]]), XNB[s].a(c * 128, [[1, 128]]), identb.a(0, [[1, 128]])),
                  reads=[("XNB", s), "identb"], writes=[("TP", s)])
        P.add("act", ACP(xnT.a(tt * 128, [[xnT_tok, 8], [1, 128]]), TP[s].a(0, [[128, 8], [1, 128]])),
              reads=[("TP", s)], writes=["xnT"])

    for it in range(ntiles + 2):
        if it < ntiles:
            stage_a(it)
        if 0 <= it - 1 < ntiles:
            stage_b(it - 1)
        if 0 <= it - 2 < ntiles:
            stage_c(it - 2)
        if hook is not None:
            hook(it)


def emit_out_phase(cx, stack, PS, YG, yg_tok, WO, xres, xres_row0, gpb, outd, ntiles=16, sfx="", yg_res=None):
    P = cx.P
    NX = 4
    XR = [cx.sb(stack, "XR%d" % i, 1024, F32) for i in range(NX)]
    SQ = [cx.sb(stack, "SQo%d" % i, 1024, F32) for i in range(2)]
    TO = [cx.sb(stack, "TO%d" % i, 1024, F32) for i in range(2)]
    ST = cx.sb(stack, "STo", 3 * 32, F32)
    for tt in range(ntiles):
        s = tt % 2
        sx = tt % NX
        P.add("sp", DMA(XR[sx].a(0, [[1, 1024]]), DAP(xres, (xres_row0 + tt * 128) * 1024, [[1024, 128], [1, 1024]])),
              writes=[("XR", sx)], dma=True)
        banks = [(2 * tt) % 8, (2 * tt + 1) % 8]
        for hh in range(2):
            b = banks[hh]
            for ec in range(16):
                P.add("pe", MM(PS[b].a(0, [[1, 512]]), YG.a(ec * yg_tok + tt * 128, [[1, 128]]),
                               WO.a(ec * 1024 + hh * 512, [[1, 512]]), ec == 0, ec == 15),
                      reads=(yg_res(ec, tt) if yg_res else [("YG", ec)]) + [("WO", ec // 4)], writes=[("ps", b)])
            P.add("act", ACT(SQ[s].a(hh * 512, [[1, 512]]), PS[b].a(0, [[1, 512]]), AF.Square),
                  reads=[("ps", b)], writes=[("SQo", s, hh)])
        P.add("dve", RSUM(ST.a(tt, [[1, 1]]), SQ[s].a(0, [[1, 1024]])),
              reads=[("SQo", s, 0), ("SQo", s, 1)], writes=[("so0", tt)])
        P.add("act", ACT(ST.a(32 + tt, [[1, 1]]), ST.a(tt, [[1, 1]]), AF.Sqrt, scale=1.0 / D, bias=EPS),
              reads=[("so0", tt)], writes=[("so1", tt)])
        P.add("dve", RECIP(ST.a(64 + tt, [[1, 1]]), ST.a(32 + tt, [[1, 1]])), reads=[("so1", tt)], writes=[("so2", tt)])
        for hh in range(2):
            b = banks[hh]
            P.add("dve", STT(TO[s].a(hh * 512, [[1, 512]]), PS[b].a(0, [[1, 512]]), ST.a(64 + tt, [[1, 1]]),
                             gpb.a(hh * 512, [[1, 512]]), ALU.mult, ALU.mult),
                  reads=[("ps", b), ("so2", tt), "gpb"], writes=[("TO", s, hh)])
        P.add("pool", TTO(XR[sx].a(0, [[1, 1024]]), XR[sx].a(0, [[1, 1024]]), TO[s].a(0, [[1, 1024]]), ALU.add),
              reads=[("XR", sx), ("TO", s, 0), ("TO", s, 1)], writes=[("XR", sx)])
        P.add("sp", DMA(DAP(outd, tt * 128 * 1024, [[1024, 128], [1, 1024]]), XR[sx].a(0, [[1, 1024]])),
              reads=[("XR", sx)], dma=True)


PARTS = [
    dict(g=0, d=1, R=[0], nkb=17),
    dict(g=1, d=4, R=[0, 1, 2, 3], nkb=5),
    dict(g=2, d=16, R=list(range(0, 16)), nkb=2),
]


def part_blocks(pt):
    d, nkb = pt["d"], pt["nkb"]
    nqb = nkb - 1
    kb0 = 16 // d - 1
    ktok0 = 2048 - 128 * d
    nk = 4096 - ktok0
    kblocks = [(c0, min(512, nk - c0), ktok0 + c0) for c0 in range(0, nk, 512)]
    qblocks = [(nb, 512, 2048 + 512 * nb) for nb in range(4)]
    return kblocks, qblocks, nqb, kb0


def cdims(dims):
    if len(dims) == 1:
        return [[1, dims[0][1]]]
    n1, n2 = dims[0][1], dims[1][1]
    return [[n2, n1], [1, n2]]


def build_fused():
    nc = bass.Bass("TRN2", target_bir_lowering=False)
    stop = None
    lim = [8, 4, 3]
    xin = nc.dram_tensor("xin", [XTOK, 1024], F32, kind="ExternalInput")
    pos = nc.dram_tensor("pos", [1, XTOK], I32, kind="ExternalInput")
    gpre = nc.dram_tensor("gpre", [1, 1024], F32, kind="ExternalInput")
    gpost = nc.dram_tensor("gpost", [1, 1024], F32, kind="ExternalInput")
    win = nc.dram_tensor("win", [1024, 20480], F32, kind="ExternalInput")
    wout = nc.dram_tensor("wout", [2048, 1024], F32, kind="ExternalInput")
    cst = nc.dram_tensor("cst", [128, CW], F32, kind="ExternalInput")
    gpre1 = nc.dram_tensor("gpre1", [1, 1024], F32, kind="ExternalInput")
    gpost1 = nc.dram_tensor("gpost1", [1, 1024], F32, kind="ExternalInput")
    win1 = nc.dram_tensor("win1", [1024, 4096], F32, kind="ExternalInput")
    wgrp = nc.dram_tensor("wgrp", [2048, 512], F32, kind="ExternalInput")
    wout1 = nc.dram_tensor("wout1", [2048, 1024], F32, kind="ExternalInput")
    cst1 = nc.dram_tensor("cst1", [128, 128 + 96], F32, kind="ExternalInput")
    ygd = nc.dram_tensor("ygd", [16, 128, YTOK], BF16, kind="Internal")
    h1 = nc.dram_tensor("h1d", [YTOK, 1024], F32, kind="Internal")
    outd = nc.dram_tensor("out", [YTOK, 1024], F32, kind="ExternalOutput")

    cx = Ctx(nc)
    P = cx.P
    scale_qk = 1.0 / math.sqrt(128.0)

    with contextlib.ExitStack() as s0:
        identb = cx.sb(s0, "identb", 128, BF16)
        permb = cx.sb(s0, "permb", 128, BF16)
        maskb = cx.sb(s0, "maskb", 256, BF16)
        onesH = cx.sb(s0, "onesH", 128, BF16)
        onesb = cx.sb(s0, "onesb", 128, BF16)
        cf = cx.sb(s0, "cf", 4, F32)
        MX = cx.sb(s0, "MX", 400, BF16)
        P.add("pool", DMA(MX.a(0, [[1, 400]]), DAP(cst, 644, [[CW, 128], [1, 400]])), writes=["MX"], dma=True)
        P.add("pool", DMA(identb.a(0, [[1, 128]]), DAP(cst, 0, [[CW, 128], [1, 128]])), writes=["identb"], dma=True)
        P.add("pool", DMA(permb.a(0, [[1, 128]]), DAP(cst, 128, [[CW, 128], [1, 128]])), writes=["permb"], dma=True)
        P.add("pool", DMA(maskb.a(0, [[1, 256]]), DAP(cst, 256, [[CW, 128], [1, 256]])), writes=["maskb"], dma=True)
        P.add("pool", DMA(onesH.a(0, [[1, 128]]), DAP(cst, 512, [[CW, 128], [1, 128]])), writes=["onesH"], dma=True)
        P.add("sp", DMA(cf.a(0, [[1, 4]]), DAP(cst, 640, [[CW, 128], [1, 4]])), writes=["cf"], dma=True)
        P.add("dve", MSET(onesb.a(0, [[1, 128]]), 1.0), writes=["onesb"])

        with contextlib.ExitStack() as s1:
            xnT = cx.sb(s1, "xnT", 8 * XTOK, BF16)
            cosb = cx.sb(s1, "cosb", XTOK, BF16)
            sinb = cx.sb(s1, "sinb", XTOK, BF16)

            with contextlib.ExitStack() as sA:
                gb = cx.sb(sA, "gb", 1024, F32)
                P.add("sp", DMA(gb.a(0, [[1, 1024]]), DAP(gpre, 0, [[0, 128], [1, 1024]])), writes=["gb"], dma=True)
                posi = cx.sb(sA, "posi", 1024, I32)
                tA = [cx.sb(sA, "tA%d" % i, 1024, F32) for i in range(5)]
                R32 = dict(p0=0, np_=32)
                def cs_chunk(q):
                    c0 = q * 1024
                    full = [[1, 1024 if q < 4 else 128]]
                    P.add("sp", DMA(posi.a(0, full, **R32), DAP(pos, c0, [[0, 32], full[0]])), writes=["posi"], dma=True)
                    ang, a1, a2, a3 = tA[0], tA[1], tA[2], tA[3]
                    P.add("dve", CP(a1.a(0, full, **R32), posi.a(0, full, **R32)), reads=["posi"], writes=["a1"])
                    P.add("dve", TS(ang.a(0, full, **R32), a1.a(0, full, **R32), cf.a(0, [[1, 1]], **R32), ALU.mult),
                          reads=["a1", "cf"], writes=["ang"])
                    P.add("dve", TS(a1.a(0, full, **R32), ang.a(0, full, **R32), INV2PI, ALU.mult), reads=["ang"], writes=["a1"])
                    P.add("dve", TS(a2.a(0, full, **R32), a1.a(0, full, **R32), MAGIC, ALU.add), reads=["a1"], writes=["a2"])
                    P.add("dve", TS(a1.a(0, full, **R32), a2.a(0, full, **R32), MAGIC, ALU.subtract), reads=["a2"], writes=["a1"])
                    P.add("dve", STT(a2.a(0, full, **R32), a1.a(0, full, **R32), -C1, ang.a(0, full, **R32), ALU.mult, ALU.add),
                          reads=["a1", "ang"], writes=["a2"])
                    P.add("dve", STT(a3.a(0, full, **R32), a1.a(0, full, **R32), -C2, a2.a(0, full, **R32), ALU.mult, ALU.add),
                          reads=["a1", "a2"], writes=["a3"])
                    P.add("dve", TS(a2.a(0, full, **R32), a3.a(0, full, **R32), -PI_LO, ALU.max, PI_LO, ALU.min),
                          reads=["a3"], writes=["a2"])
                    P.add("act", ACT(sinb.a(c0, full, **R32), a2.a(0, full, **R32), AF.Sin, scale=cf.a(1, [[1, 1]], **R32)),
                          reads=["a2", "cf"], writes=["sinb"])
                    P.add("dve", TS(a1.a(0, full, **R32), ang.a(0, full, **R32), INV2PI, ALU.mult, 0.25, ALU.add),
                          reads=["ang"], writes=["a1"])
                    P.add("dve", TS(a3.a(0, full, **R32), a1.a(0, full, **R32), MAGIC, ALU.add), reads=["a1"], writes=["a3"])
                    P.add("dve", TS(a1.a(0, full, **R32), a3.a(0, full, **R32), MAGIC, ALU.subtract), reads=["a3"], writes=["a1"])
                    a4 = tA[4]
                    P.add("dve", STT(a3.a(0, full, **R32), a1.a(0, full, **R32), -C1, ang.a(0, full, **R32), ALU.mult, ALU.add),
                          reads=["a1", "ang"], writes=["a3"])
                    P.add("dve", STT(a4.a(0, full, **R32), a1.a(0, full, **R32), -C2, a3.a(0, full, **R32), ALU.mult, ALU.add),
                          reads=["a1", "a3"], writes=["a4"])
                    P.add("dve", TS(a3.a(0, full, **R32), a4.a(0, full, **R32), 0.5 * math.pi, ALU.add), reads=["a4"], writes=["a3"])
                    P.add("dve", TS(a4.a(0, full, **R32), a3.a(0, full, **R32), -PI_LO, ALU.max, PI_LO, ALU.min),
                          reads=["a3"], writes=["a4"])
                    P.add("act", ACT(cosb.a(c0, full, **R32), a4.a(0, full, **R32), AF.Sin), reads=["a4"], writes=["cosb"])
                cs_at = {2: 0, 8: 1, 14: 2, 20: 3, 26: 4}
                emit_norm_transpose(cx, sA, xin, 33, xnT, XTOK, gb, identb,
                                    hook=lambda it: cs_chunk(cs_at[it]) if it in cs_at else None)
            P.barrier()

            with contextlib.ExitStack() as sB:
                PS = [cx.ps(sB, "PS%d" % i, 512, F32) for i in range(8)]
                QT = cx.sb(sB, "QT", 2 * 2048, BF16)
                KT = cx.sb(sB, "KT", 2 * 4096, BF16)
                V = cx.sb(sB, "V", 32 * 256, BF16)
                ND = cx.sb(sB, "ND", 4 * 2048, F32)
                NDb = TT(ND.t.bitcast(BF16), 16384)
                WQ = [cx.sb(sB, "WQ%d" % i, 2048, BF16) for i in range(2)]
                WK = [cx.sb(sB, "WK%d" % i, 2048, BF16) for i in range(2)]
                WV = [cx.sb(sB, "WV%d" % i, 2048, BF16) for i in range(2)]
                WZ = cx.sb(sB, "WZ", 2048, BF16)
                T1 = [cx.sb(sB, "T1%d" % i, 512, F32) for i in range(3)]
                T2 = [cx.sb(sB, "T2%d" % i, 512, F32) for i in range(2)]
                PT = [cx.sb(sB, "PT%d" % i, 512, BF16) for i in range(4)]
                mask4 = cx.sb(sB, "mask4", 512, BF16)
                for q_ in range(4):
                    P.add("pool", TS(mask4.a(q_ * 128, [[1, 128]]), maskb.a((q_ // 2) * 128, [[1, 128]]), -1.0, ALU.add,
                                     30000.0, ALU.mult),
                          reads=["maskb"], writes=["mask4"])
                SZ = [cx.sb(sB, "SZ%d" % i, 512, F32) for i in range(2)]
                QX = cx.sb(sB, "QX", 32, BF16)
                KX = cx.sb(sB, "KX", 256, BF16)
                VX = cx.sb(sB, "VX", 256, BF16)
                NDX = cx.sb(sB, "NDX", 64, F32)
                PTX = cx.sb(sB, "PTX", 256, BF16)
                PTC = cx.sb(sB, "PTC", 16, BF16)
                YGX = cx.sb(sB, "YGX", 32, BF16)
                SZX = cx.sb(sB, "SZX", 16, F32)
                MXP_OFF = [0, 16, 80]

                def wload(Wt, name, colbase):
                    P.add("pool", DMA(Wt.a(0, [[256, 8], [1, 256]]),
                                      DAP(win, colbase, [[20480, 128], [128 * 20480, 8], [1, 256]])),
                          writes=[name], dma=True)

                def load_group_weights(bt, g, par):
                    base = g * 6144 + bt * 256
                    wload(WK[par], ("WK", par), base + 2048)
                    wload(WV[par], ("WV", par), base + 4096)
                    wload(WQ[par], ("WQ", par), base)

                cnt = dict(a=0, b=0, v=0, t=0, t2=0, s=0, o=0, p=0, z=0)
                nphase = 0
                load_group_weights(0, 0, 0)

                class Pipe:
                    def __init__(self, depth):
                        self.q = []
                        self.depth = depth

                    def push(self, fn):
                        self.q.append(fn)
                        while len(self.q) > self.depth:
                            self.q.pop(0)()

                    def flush(self):
                        while self.q:
                            self.q.pop(0)()

                ppipe = Pipe(1)

                def proj_block(Wt, wname, j, dst, dres, dbase, n, tok_off, d=1, dpitch=0):
                    ba = cnt["a"] % 4
                    cnt["a"] += 1
                    t1s = cnt["t"] % 3
                    cnt["t"] += 1
                    if d == 1:
                        nat = [[1, n]]
                        ri_ = [[1, n]]
                        con = [[1, n]]
                        dap = [[1, n]]
                    else:
                        nat = [[1, n]]
                        ri_ = [[1, d], [d, n // d]]
                        con = [[n // d, d], [1, n // d]]
                        dap = [[dpitch, d], [1, n // d]]
                    for c in range(8):
                        P.add("pe", MM(PS[ba].a(0, nat), Wt.a(c * 256 + j * 128, [[1, 128]]),
                                       xnT.a(c * XTOK + tok_off, nat), c == 0, c == 7),
                              reads=[wname, "xnT"], writes=[("ps", ba)])
                    P.add("act", ACP(dst.a(dbase, dap), PS[ba].a(0, ri_)), reads=[("ps", ba)], writes=dres)
                    ppipe.flush()
                    P.add("dve", TTO(T1[t1s].a(0, con, p0=0, np_=32), PS[ba].a(0, ri_, p0=0, np_=32),
                                     cosb.a(tok_off, ri_, p0=0, np_=32), ALU.mult),
                          reads=[("ps", ba), "cosb"], writes=[("T1", t1s)])

                    def stage2():
                        bb = 4 + cnt["b"] % 2
                        cnt["b"] += 1
                        t2s = cnt["t2"] % 2
                        cnt["t2"] += 1
                        P.add("pe", MM(PS[bb].a(0, con), permb.a(0, [[1, 128]]), dst.a(dbase, dap), True, True),
                              reads=dres + ["permb"], writes=[("ps", bb)])
                        P.add("dve", TTO(T2[t2s].a(0, con, p0=0, np_=32), PS[bb].a(0, con, p0=0, np_=32),
                                         sinb.a(tok_off, ri_, p0=0, np_=32), ALU.mult),
                              reads=[("ps", bb), "sinb"], writes=[("T2", t2s)])
                        P.add("pool", TTO(dst.a(dbase, dap, p0=0, np_=32), T1[t1s].a(0, con, p0=0, np_=32),
                                          T2[t2s].a(0, con, p0=0, np_=32), ALU.add),
                              reads=[("T1", t1s), ("T2", t2s)], writes=dres)

                    ppipe.push(stage2)

                for bt in range(lim[0]):
                    for pi, pt in enumerate(PARTS[:lim[1]]):
                        g, d, R, nkb = pt["g"], pt["d"], pt["R"], pt["nkb"]
                        kblocks, qblocks, nqb, kb0 = part_blocks(pt)
                        par = nphase % 2
                        nphase += 1
                        if pi < 2:
                            load_group_weights(bt, g + 1, nphase % 2)
                        elif bt < 7:
                            load_group_weights(bt + 1, 0, nphase % 2)
                        if pi == 0:
                            wload(WZ, "WZ", 18432 + bt * 256)
                        for j in range(2):
                            for (nb, n, tok_off) in qblocks:
                                if d == 1:
                                    qres_w = [("QT", j, b_, q_) for b_ in range(4 * nb, 4 * nb + 4) for q_ in range(4)]
                                elif d == 4:
                                    qres_w = [("QT", j, r_ * 4 + nb, q_) for r_ in range(4) for q_ in range(4)]
                                else:
                                    qres_w = [("QT", j, r_, nb) for r_ in range(16)]
                                proj_block(WQ[par], ("WQ", par), j, QT, qres_w,
                                           j * 2048 + (512 * nb) // d, n, tok_off, d=d, dpitch=2048 // d)
                        for j in range(2):
                            for (c0, n, tok_off) in kblocks:
                                blk = c0 // 512
                                if d == 1:
                                    kres_w = [("KT", j, c0 // 128 + b_, q_) for b_ in range(n // 128) for q_ in range(4)]
                                elif d == 4:
                                    kres_w = [("KT", j, r_ * nkb + blk, q_) for r_ in range(4) for q_ in range(4)]
                                else:
                                    kres_w = [("KT", j, r_ * nkb + blk // 4, blk % 4) for r_ in range(16)]
                                proj_block(WK[par], ("WK", par), j, KT, kres_w,
                                           j * 4096 + c0 // d, n, tok_off, d=d, dpitch=128 * nkb)
                        nvt = len(R) * nkb
                        for vt in range(nvt):
                            ri, kb = vt // nkb, vt % nkb
                            tok0 = R[ri] + d * 128 * (kb0 + kb)
                            if vt % 2 == 0:
                                bv = 6 + cnt["v"] % 2
                                cnt["v"] += 1
                            for c in range(8):
                                P.add("pe", MM(PS[bv].a((vt % 2) * 256, [[1, 256]]), xnT.a(c * XTOK + tok0, [[d, 128]]),
                                               WV[par].a(c * 256, [[1, 256]]), c == 0, c == 7),
                                      reads=["xnT", ("WV", par)], writes=[("ps", bv)])
                            if vt % 2 == 1 or vt == nvt - 1:
                                v0 = vt - (vt % 2)
                                nn = vt - v0 + 1
                                P.add("act", ACP(V.a(v0 * 256, [[1, nn * 256]]), PS[bv].a(0, [[1, nn * 256]])),
                                      reads=[("ps", bv)], writes=[("V", v_) for v_ in range(v0, v0 + nn)])
                        ppipe.flush()
                        apipe = Pipe(2)
                        for ri in range(len(R) if lim[2] >= 2 else 0):
                            for qb in range(nqb):
                                qs = (ri * nqb + qb) * 128
                                kprev = ri * nkb + qb
                                kcur = kprev + 1
                                bs = cnt["s"] % 4
                                cnt["s"] += 1
                                pslot = cnt["p"] % 4
                                cnt["p"] += 1
                                P.add("pe", MM(PS[bs].a(0, [[1, 512]]), identb.a(0, [[1, 128]]), mask4.a(0, [[1, 512]]), True, False),
                                      reads=["identb", "mask4"], writes=[("ps", bs)])
                                for hh, kq in enumerate((qb, qb + 1)):
                                    for j in range(2):
                                        qres = [("QT", j, qs // 128, q_) for q_ in range(4)]
                                        kres = [("KT", j, ri * nkb + kq, q_) for q_ in range(4)]
                                        P.add("pe", MM(PS[bs].a(hh * 256 + j * 128, [[1, 128]]),
                                                       KT.a(j * 4096 + (ri * nkb + kq) * 128, [[1, 128]]),
                                                       QT.a(j * 2048 + qs, [[1, 128]]), False, hh == 1 and j == 1),
                                              reads=qres + kres, writes=[("ps", bs)])
                                P.add("act", ACT(PT[pslot].a(0, [[1, 512]]), PS[bs].a(0, [[1, 512]]), AF.Exp, scale=scale_qk),
                                      reads=[("ps", bs)], writes=[("PT", pslot)])

                                def stage2(ri=ri, qb=qb, kprev=kprev, kcur=kcur, pslot=pslot):
                                    bo = 4 + cnt["o"] % 4
                                    cnt["o"] += 1
                                    for j in range(2):
                                        for hh, kbk in enumerate((kprev, kcur)):
                                            P.add("pe", MM(PS[bo].a(j * 128, [[1, 128]]), V.a(kbk * 256 + j * 128, [[1, 128]]),
                                                           PT[pslot].a(hh * 256 + j * 128, [[1, 128]]), hh == 0, hh == 1),
                                                  reads=[("V", kbk), ("PT", pslot)], writes=[("ps", bo)])
                                    for hh in range(2):
                                        ones_t, ones_n = (onesH, "onesH") if (hh == 0 and qb == 0) else (onesb, "onesb")
                                        P.add("pe", MM(PS[bo].a(256, [[1, 256]]), ones_t.a(0, [[1, 128]]),
                                                       PT[pslot].a(hh * 256, [[1, 256]]), hh == 0, hh == 1),
                                              reads=[ones_n, ("PT", pslot)], writes=[("ps", bo)])
                                    tq0 = R[ri] + d * 128 * (16 // d + qb) - 2048
                                    ndres = [("ND", j, b_) for j in range(2) for b_ in range(tq0 // 128, tq0 // 128 + d)]
                                    ndap = ND.a(tq0, [[4096, 2], [2048, 2], [d, 128]])
                                    psap = PS[bo].a(0, [[256, 2], [128, 2], [1, 128]])
                                    if pi == 0:
                                        P.add("dve", CP(ndap, psap), reads=[("ps", bo)], writes=ndres)
                                    else:
                                        P.add("dve", TTO(ndap, psap, ndap, ALU.add),
                                              reads=[("ps", bo)] + ndres, writes=ndres)

                                apipe.push(stage2)
                        apipe.flush()
                        if True:
                            for j in range(2):
                                proj_block(WQ[par], ("WQ", par), j, QX, [("QX", j, 0)], j * 16, 16, 4096)
                                proj_block(WK[par], ("WK", par), j, KX, [("KX", j, 0)], j * 128, 128, 4096)
                            bv = 6 + cnt["v"] % 2
                            cnt["v"] += 1
                            for c in range(8):
                                P.add("pe", MM(PS[bv].a(0, [[1, 256]]), xnT.a(c * XTOK + 4096, [[1, 128]]),
                                               WV[par].a(c * 256, [[1, 256]]), c == 0, c == 7),
                                      reads=["xnT", ("WV", par)], writes=[("ps", bv)])
                            P.add("act", ACP(VX.a(0, [[1, 256]]), PS[bv].a(0, [[1, 256]])),
                                  reads=[("ps", bv)], writes=["VX"])
                            ppipe.flush()
                        nR = len(R)
                        for j in range(2):
                            bs = cnt["s"] % 4
                            cnt["s"] += 1
                            bo = 4 + cnt["o"] % 4
                            cnt["o"] += 1
                            for ri in range(nR):
                                kq = nkb - 1
                                kres = [("KT", j, ri * nkb + kq, q_) for q_ in range(4)]
                                P.add("pe", MM(PS[bs].a(ri * 16, [[1, 16]]), KT.a(j * 4096 + (ri * nkb + kq) * 128, [[1, 128]]),
                                               QX.a(j * 16, [[1, 16]]), True, True),
                                      reads=[("QX", j, 0)] + kres, writes=[("ps", bs)])
                            P.add("pe", MM(PS[bs].a(256, [[1, 16]]), KX.a(j * 128, [[1, 128]]),
                                           QX.a(j * 16, [[1, 16]]), True, True),
                                  reads=[("QX", j, 0), ("KX", j, 0)], writes=[("ps", bs)])
                            P.add("act", ACT(PTX.a(0, [[1, nR * 16]]), PS[bs].a(0, [[1, nR * 16]]), AF.Exp, scale=scale_qk),
                                  reads=[("ps", bs)], writes=["PTX"])
                            P.add("act", ACT(PTC.a(0, [[1, 16]]), PS[bs].a(256, [[1, 16]]),
                                             AF.Exp, scale=scale_qk),
                                  reads=[("ps", bs)], writes=["PTC"])
                            P.add("pool", TTO(PTX.a(0, [[1, nR * 16]]), PTX.a(0, [[1, nR * 16]]),
                                              MX.a(MXP_OFF[pi], [[1, nR * 16]]), ALU.mult),
                                  reads=["PTX", "MX"], writes=["PTX"])
                            P.add("pool", TTO(PTC.a(0, [[1, 16]]), PTC.a(0, [[1, 16]]),
                                              MX.a(336 + pi * 16, [[1, 16]]), ALU.mult),
                                  reads=["PTC", "MX"], writes=["PTC"])
                            for which in range(2):
                                for ri in range(nR):
                                    kbl = ri * nkb + nkb - 1
                                    lhs = V.a(kbl * 256 + j * 128, [[1, 128]]) if which == 0 else onesb.a(0, [[1, 128]])
                                    P.add("pe", MM(PS[bo].a(which * 16, [[1, 16]]), lhs, PTX.a(ri * 16, [[1, 16]]),
                                                   ri == 0, False),
                                          reads=[("V", kbl), "PTX", "onesb"], writes=[("ps", bo)])
                                lhs = VX.a(j * 128, [[1, 128]]) if which == 0 else onesb.a(0, [[1, 128]])
                                P.add("pe", MM(PS[bo].a(which * 16, [[1, 16]]), lhs, PTC.a(0, [[1, 16]]),
                                               False, True),
                                      reads=["VX", "PTC", "onesb"], writes=[("ps", bo)])
                            ndx = NDX.a(j * 16, [[32, 2], [1, 16]])
                            if pi == 0:
                                P.add("dve", CP(ndx, PS[bo].a(0, [[16, 2], [1, 16]])), reads=[("ps", bo)], writes=[("NDX", j)])
                            else:
                                P.add("dve", TTO(ndx, PS[bo].a(0, [[16, 2], [1, 16]]), ndx, ALU.add),
                                      reads=[("ps", bo), ("NDX", j)], writes=[("NDX", j)])
                    if lim[2] < 3:
                        continue
                    for j in range(2):
                        for hq in range(2):
                            r_ = [("ND", j, b_) for b_ in range(hq * 8, hq * 8 + 8)]
                            apx = ND.a(4096 + j * 2048 + hq * 1024, [[1, 1024]])
                            P.add("act", ACT(apx, apx, AF.Ln), reads=r_, writes=r_)
                            P.add("act", ACT(apx, apx, AF.Exp, scale=-1.0), reads=r_, writes=r_)
                    for j in range(2):
                        for tb in range(4):
                            bz = cnt["z"] % 3
                            cnt["z"] += 1
                            zs = cnt["z"] % 2
                            for c in range(8):
                                P.add("pe", MM(PS[bz].a(0, [[1, 512]]), WZ.a(c * 256 + j * 128, [[1, 128]]),
                                               xnT.a(c * XTOK + 2048 + tb * 512, [[1, 512]]), c == 0, c == 7),
                                      reads=["WZ", "xnT"], writes=[("ps", bz)])
                            P.add("act", ACT(SZ[zs].a(0, [[1, 512]]), PS[bz].a(0, [[1, 512]]), AF.Silu),
                                  reads=[("ps", bz)], writes=[("SZ", zs)])
                            r_ = [("ND", j, b_) for b_ in range(tb * 4, tb * 4 + 4)]
                            nap = ND.a(j * 2048 + tb * 512, [[1, 512]])
                            dap_ = ND.a(4096 + j * 2048 + tb * 512, [[1, 512]])
                            P.add("dve", TTO(nap, nap, dap_, ALU.mult), reads=r_, writes=r_)
                            P.add("dve", TTO(NDb.a(8192 + j * 2048 + tb * 512, [[1, 512]]), nap, SZ[zs].a(0, [[1, 512]]), ALU.mult),
                                  reads=r_ + [("SZ", zs)], writes=[("YGS", j, tb)] + [("ND", 0, b_) for b_ in range(16)])
                    P.add("sp", DMA(DAP(ygd, 2 * bt * 128 * YTOK, [[YTOK, 128], [128 * YTOK, 2], [1, 2048]]),
                                    NDb.a(8192, [[2048, 2], [1, 2048]])),
                          reads=[("YGS", j, tb) for j in range(2) for tb in range(4)] + [("ND", 0, b_) for b_ in range(16)],
                          dma=True)
                    P.add("dve", RECIP(NDX.a(32, [[1, 32]]), NDX.a(32, [[1, 32]])), reads=[("NDX", 0), ("NDX", 1)],
                          writes=[("NDX", 0), ("NDX", 1)])
                    P.add("dve", TTO(NDX.a(0, [[1, 32]]), NDX.a(0, [[1, 32]]), NDX.a(32, [[1, 32]]), ALU.mult),
                          reads=[("NDX", 0), ("NDX", 1)], writes=[("NDX", 0), ("NDX", 1)])
                    for j in range(2):
                        bz = cnt["z"] % 3
                        cnt["z"] += 1
                        for c in range(8):
                            P.add("pe", MM(PS[bz].a(0, [[1, 16]]), WZ.a(c * 256 + j * 128, [[1, 128]]),
                                           xnT.a(c * XTOK + 4096, [[1, 16]]), c == 0, c == 7),
                                  reads=["WZ", "xnT"], writes=[("ps", bz)])
                        P.add("act", ACT(SZX.a(0, [[1, 16]]), PS[bz].a(0, [[1, 16]]), AF.Silu),
                              reads=[("ps", bz)], writes=["SZX"])
                        P.add("dve", TTO(YGX.a(j * 16, [[1, 16]]), NDX.a(j * 16, [[1, 16]]), SZX.a(0, [[1, 16]]), ALU.mult),
                              reads=[("NDX", j), "SZX"], writes=["YGX"])
                    P.add("sp", DMA(DAP(ygd, 2 * bt * 128 * YTOK + 2048, [[YTOK, 128], [128 * YTOK, 2], [1, 16]]),
                                    YGX.a(0, [[16, 2], [1, 16]])),
                          reads=["YGX"], dma=True)
            P.barrier()

        with contextlib.ExitStack() as sC:
            PS = [cx.ps(sC, "PSc%d" % i, 512, F32) for i in range(8)]
            YG = cx.sb(sC, "YG", 16 * YTOK, BF16)
            WO = cx.sb(sC, "WO", 16 * 1024, BF16)
            gpb = cx.sb(sC, "gpb", 1024, F32)
            P.add("sp", DMA(gpb.a(0, [[1, 1024]]), DAP(gpost, 0, [[0, 128], [1, 1024]])), writes=["gpb"], dma=True)
            for q in range(4):
                P.add("pool", DMA(WO.a(q * 4096, [[1024, 4], [1, 1024]]),
                                  DAP(wout, q * 4 * 128 * 1024, [[1024, 128], [128 * 1024, 4], [1, 1024]])),
                      writes=[("WO", q)], dma=True)
            for r in range(5):
                n_ = 512 if r < 4 else 16
                P.add("sp", DMA(YG.a(512 * r, [[YTOK, 16], [1, n_]]),
                                DAP(ygd, 512 * r, [[YTOK, 128], [128 * YTOK, 16], [1, n_]])),
                      writes=[("YG", "r", r)], dma=True)
            P.add("dve", MSET(YG.a(2064, [[YTOK, 16], [1, YTOK - 2064]]), 0.0), writes=[("YG", "r", 4)])
            emit_out_phase(cx, sC, PS, YG, YTOK, WO, xin, 2048, gpb, h1, ntiles=17,
                           yg_res=lambda ec, tt: [("YG", "r", min(tt // 4, 4))])
        P.barrier()
        emit_l1(cx, s0, h1, gpre1, gpost1, win1, wgrp, wout1, cst1, outd, identb)
        P.run(nc, s0)
    return nc


POOL_W = (2, 4, 8, 16)
NT1 = 2176


def emit_l1(cx, s0, hin, gpre, gpost, win, wgrp, wout, cst, out, identb):
    P = cx.P
    CW1 = 128 + 96
    if True:
        cf = cx.sb(s0, "cf1", 144, F32)
        YG = cx.sb(s0, "YG1", 16 * NT1, BF16)
        P.add("sp", DMA(cf.a(0, [[1, 96]]), DAP(cst, 128, [[CW1, 128], [1, 96]])), writes=["cf"], dma=True)
        for g_ in range(4):
            P.add("dve", TS(cf.a(96 + 4 * g_, [[1, 4]]), cf.a(16 + 4 * g_, [[1, 4]]), 1.0 / POOL_W[g_], ALU.mult),
                  reads=["cf"], writes=["cf"])
        P.add("dve", TS(cf.a(112, [[1, 16]]), cf.a(16, [[1, 16]]), -1.0, ALU.mult), reads=["cf"], writes=["cf"])
        P.add("dve", TTO(cf.a(128, [[1, 16]]), cf.a(0, [[1, 16]]), cf.a(16, [[1, 16]]), ALU.mult), reads=["cf"], writes=["cf"])
        with contextlib.ExitStack() as s1:
            xnT = cx.sb(s1, "xnT1", 8 * NT1, BF16)
            with contextlib.ExitStack() as sA:
                gb = cx.sb(sA, "gb1", 1024, F32)
                P.add("sp", DMA(gb.a(0, [[1, 1024]]), DAP(gpre, 0, [[0, 128], [1, 1024]])), writes=["gb"], dma=True)
                emit_norm_transpose(cx, sA, hin, 17, xnT, NT1, gb, identb)
            P.barrier()
            with contextlib.ExitStack() as sB:
                PS = [cx.ps(sB, "PS%d" % i, 512, F32) for i in range(8)]
                WG = [cx.sb(sB, "WG%d" % i, 4 * 512, BF16) for i in range(2)]
                WU = [cx.sb(sB, "WU%d" % i, 8 * 128, BF16) for i in range(2)]
                WZ = [cx.sb(sB, "WZ%d" % i, 8 * 128, BF16) for i in range(2)]
                UB = cx.sb(sB, "UB", 4 * NT1, BF16)
                UW = 16 + NT1
                VV = [cx.sb(sB, "VV%d" % i, UW, F32) for i in range(2)]
                SA = cx.sb(sB, "SA", UW, F32)
                SBf = cx.sb(sB, "SB", UW, F32)
                SZ = [cx.sb(sB, "SZ%d" % i, NT1, F32) for i in range(2)]
                for i in range(2):
                    P.add("dve", MSET(VV[i].a(0, [[1, 16]]), 0.0), writes=[("VV", i, 0)])
                A1 = [cx.sb(sB, "A1%d" % i, UW, F32) for i in range(2)]
                A0 = cx.sb(sB, "A0", 32, F32)
                HALF = (("dve", 0, UW),)
                blocks = [(tb * 512, 512) for tb in range(4)] + [(2048, 128)]
                cnt = dict(a=0, z=0, h=0)

                def both(rd):
                    return [(rd[0], rd[1], 0)]

                for g in range(4):
                    w = POOL_W[g]
                    nsteps = g + 1
                    gp = g % 2
                    P.add("pool", DMA(WG[gp].a(0, [[512, 4], [1, 512]]),
                                      DAP(wgrp, g * 512 * 512, [[512, 128], [128 * 512, 4], [1, 512]])),
                          writes=[("WG", gp)], dma=True)
                    for ic in range(4):
                        uc = g * 4 + ic
                        ws = uc % 2
                        P.add("pool", DMA(WU[ws].a(0, [[128, 8], [1, 128]]),
                                          DAP(win, uc * 128, [[4096, 128], [128 * 4096, 8], [1, 128]])),
                              writes=[("WU", ws)], dma=True)
                        for (t0, n) in blocks:
                            ba = cnt["a"] % 3
                            cnt["a"] += 1
                            for c in range(8):
                                P.add("pe", MM(PS[ba].a(0, [[1, n]]), WU[ws].a(c * 128, [[1, 128]]),
                                               xnT.a(c * NT1 + t0, [[1, n]]), c == 0, c == 7),
                                      reads=[("WU", ws), "xnT"], writes=[("ps", ba)])
                            P.add("act", ACP(UB.a(ic * NT1 + t0, [[1, n]]), PS[ba].a(0, [[1, n]])),
                                  reads=[("ps", ba)], writes=[("UB", ic, t0)])
                    for oc in range(4):
                        e_ = g * 4 + oc
                        zs = e_ % 2
                        v = e_ % 2
                        P.add("pool", DMA(WZ[zs].a(0, [[128, 8], [1, 128]]),
                                          DAP(win, 2048 + e_ * 128, [[4096, 128], [128 * 4096, 8], [1, 128]])),
                              writes=[("WZ", zs)], dma=True)
                        for (t0, n) in blocks:
                            bh = 3 + cnt["h"] % 2
                            cnt["h"] += 1
                            for ic in range(4):
                                P.add("pe", MM(PS[bh].a(0, [[1, n]]), WG[gp].a(ic * 512 + oc * 128, [[1, 128]]),
                                               UB.a(ic * NT1 + t0, [[1, n]]), ic == 0, ic == 3),
                                      reads=[("WG", gp), ("UB", ic, t0)], writes=[("ps", bh)])
                            hv = 0
                            wres = [("VV", v, 0), ("VV", v, 1)] if hv == 2 else [("VV", v, hv)]
                            P.add("act", ACP(VV[v].a(16 + t0, [[1, n]]), PS[bh].a(0, [[1, n]])),
                                  reads=[("ps", bh)], writes=wres)
                            P.add("act", ACT(A1[v].a(16 + t0, [[1, n]]), PS[bh].a(0, [[1, n]]), AF.Identity,
                                             scale=cf.a(112 + e_, [[1, 1]]), bias=cf.a(128 + e_, [[1, 1]])),
                                  reads=[("ps", bh), "cf"], writes=[("A1", v)])
                        for (t0, n) in blocks:
                            bz = 5 + cnt["z"] % 3
                            cnt["z"] += 1
                            for c in range(8):
                                P.add("pe", MM(PS[bz].a(0, [[1, n]]), WZ[zs].a(c * 128, [[1, 128]]),
                                               xnT.a(c * NT1 + t0, [[1, n]]), c == 0, c == 7),
                                      reads=[("WZ", zs), "xnT"], writes=[("ps", bz)])
                            hv = 0
                            wres = [("SZ", v, 0), ("SZ", v, 1)] if hv == 2 else [("SZ", v, hv)]
                            P.add("act", ACT(SZ[v].a(t0, [[1, n]]), PS[bz].a(0, [[1, n]]), AF.Silu),
                                  reads=[("ps", bz)], writes=wres)
                        P.add("dve", CP(A0.a(16, [[1, 16]]), A1[v].a(16, [[1, 16]])), reads=[("A1", v)], writes=["A0"])
                        src, srcn = VV[v], ("VV", v)
                        m = 1
                        for st in range(nsteps):
                            dst, dstn = (SA, ("SA", 0)) if st % 2 == 0 else (SBf, ("SB", 0))
                            lo = 2 * m - 1
                            for hi_, (eng, c0, c1) in enumerate(HALF):
                                c0_ = max(c0, lo)
                                P.add(eng, TTO(dst.a(c0_, [[1, c1 - c0_]]), src.a(c0_, [[1, c1 - c0_]]),
                                               src.a(c0_ - m, [[1, c1 - c0_]]), ALU.add),
                                      reads=both(srcn) if hi_ == 1 else [(srcn[0], srcn[1], 0)],
                                      writes=[(dstn[0], dstn[1], hi_)])
                            src, srcn = dst, dstn
                            m *= 2
                        for hi_, (eng, c0, c1) in enumerate(HALF):
                            c0_ = max(c0, 16)
                            rs = [(srcn[0], srcn[1], hi_)]
                            a1s = [("A1", v)]
                            if eng == "dve":
                                P.add(eng, STT(A1[v].a(c0_, [[1, c1 - c0_]]), src.a(c0_, [[1, c1 - c0_]]), cf.a(96 + e_, [[1, 1]]),
                                               A1[v].a(c0_, [[1, c1 - c0_]]), ALU.mult, ALU.add),
                                      reads=rs + a1s + ["cf"], writes=a1s)
                                P.add(eng, TTO(src.a(16, [[1, 16]]), src.a(16, [[1, 16]]), cf.a(32 + g * 16, [[1, 16]]), ALU.mult),
                                      reads=rs + ["cf"], writes=rs)
                                P.add(eng, STT(A1[v].a(16, [[1, 16]]), src.a(16, [[1, 16]]), cf.a(16 + e_, [[1, 1]]),
                                               A0.a(16, [[1, 16]]), ALU.mult, ALU.add),
                                      reads=rs + ["A0", "cf"], writes=a1s)
                            else:
                                P.add(eng, TS(src.a(c0_, [[1, c1 - c0_]]), src.a(c0_, [[1, c1 - c0_]]), cf.a(96 + e_, [[1, 1]]), ALU.mult),
                                      reads=rs + ["cf"], writes=rs)
                                P.add(eng, TTO(A1[v].a(c0_, [[1, c1 - c0_]]), A1[v].a(c0_, [[1, c1 - c0_]]), src.a(c0_, [[1, c1 - c0_]]), ALU.add),
                                      reads=rs + a1s, writes=a1s)
                            P.add(eng, TTO(YG.a(e_ * NT1 + c0_ - 16, [[1, c1 - c0_]]), A1[v].a(c0_, [[1, c1 - c0_]]),
                                           SZ[v].a(c0_ - 16, [[1, c1 - c0_]]), ALU.mult),
                                  reads=a1s + [("SZ", v, hi_)], writes=[("YG", e_)])
            P.barrier()
        with contextlib.ExitStack() as sC:
            PS = [cx.ps(sC, "PSc%d" % i, 512, F32) for i in range(8)]
            WO = cx.sb(sC, "WO", 16 * 1024, BF16)
            gpb = cx.sb(sC, "gpb", 1024, F32)
            P.add("sp", DMA(gpb.a(0, [[1, 1024]]), DAP(gpost, 0, [[0, 128], [1, 1024]])), writes=["gpb"], dma=True)
            for q in range(4):
                P.add("pool", DMA(WO.a(q * 4096, [[1024, 4], [1, 1024]]),
                                  DAP(wout, q * 4 * 128 * 1024, [[1024, 128], [128 * 1024, 4], [1, 1024]])),
                      writes=[("WO", q)], dma=True)
            emit_out_phase(cx, sC, PS, YG, NT1, WO, hin, 0, gpb, out, ntiles=17, sfx="1")


def _consts_l0(first_chunk):
    c = np.zeros((128, CW), np.float32)
    c[:, 0:128] = np.eye(128, dtype=np.float32)
    for m in range(16):
        c[m + 16, 128 + m] = 1.0
        c[m, 128 + 16 + m] = 1.0
    k = np.arange(128)[:, None]
    q = np.arange(128)[None, :]
    c[:, 256:384] = (k >= q).astype(np.float32)
    c[:, 384:512] = (k <= q).astype(np.float32)
    c[:, 512:640] = 0.0 if first_chunk else 1.0
    invf = (np.float32(500000.0) ** (-np.arange(0, 32, 2, dtype=np.float32) / np.float32(32.0))).astype(np.float32)
    p = np.arange(128)
    c[:, 640] = invf[p % 16]
    c[:, 641] = np.where((p % 32) < 16, -1.0, 1.0)
    off = 644
    q16 = np.arange(16)[None, :]
    for pt in PARTS:
        d = pt["d"]
        for r in pt["R"]:
            c[:, off:off + 16] = ((q16 % d == r) & (k >= q16 // d)).astype(np.float32)
            off += 16
    assert off == 644 + 336
    k16 = np.arange(128)[:, None]
    for pt in PARTS:
        d = pt["d"]
        inR = np.isin(q16 % d, np.array(pt["R"]))
        c[:, off:off + 16] = ((k16 <= q16) & ((q16 - k16) % d == 0) & inR & (k16 < 16)).astype(np.float32)
        off += 16
    return c


def _consts_l1(b_grp, scale, chunk):
    c = np.zeros((128, 128 + 96), np.float32)
    c[:, 0:128] = np.eye(128, dtype=np.float32)
    c[:, 128:144] = b_grp.reshape(16, 128).T
    c[:, 144:160] = scale.reshape(16, 128).T
    t = np.arange(16) + chunk * 2048
    for g, w in enumerate(POOL_W):
        c[:, 160 + g * 16:160 + (g + 1) * 16] = (1.0 / np.minimum(t + 1, w)).astype(np.float32)[None, :]
    return c


_NC_CACHE = {}


def make_in_maps(x, positions, norm_pre, norm_post, attn_w_in, attn_w_out,
                 pool_w_in, pool_w_grp, pool_b_grp, pool_scale, pool_w_out, ncores=8):
    B, S, _ = x.shape
    cpb = ncores // B
    f = lambda a: np.ascontiguousarray(np.asarray(a, dtype=np.float32))
    shared = dict(
        gpre=f(norm_pre[0:1]), gpost=f(norm_post[0:1]), win=f(attn_w_in[0]), wout=f(attn_w_out[0]),
        gpre1=f(norm_pre[1:2]), gpost1=f(norm_post[1:2]), win1=f(pool_w_in[0]),
        wgrp=f(np.asarray(pool_w_grp[0]).reshape(2048, 512)), wout1=f(pool_w_out[0]))
    in_maps = []
    for core in range(ncores):
        b, ch = core // cpb, core % cpb
        s0 = ch * 2048
        xin = np.zeros((XTOK, D), np.float32)
        pos = np.zeros((1, XTOK), np.int32)
        xin[2048:4096] = x[b, s0:s0 + 2048]
        pos[0, 2048:4096] = positions[b, s0:s0 + 2048]
        if ch > 0:
            xin[:2048] = x[b, s0 - 2048:s0]
            pos[0, :2048] = positions[b, s0 - 2048:s0]
        if s0 + 2048 < S:
            xin[4096:4112] = x[b, s0 + 2048:s0 + 2064]
            pos[0, 4096:4112] = positions[b, s0 + 2048:s0 + 2064]
        m = dict(shared)
        m.update(xin=xin, pos=pos, cst=_consts_l0(ch == 0),
                 cst1=_consts_l1(np.asarray(pool_b_grp[0], np.float32), np.asarray(pool_scale[0], np.float32), ch))
        in_maps.append(m)
    return in_maps


def kernel(x, positions, norm_pre, norm_post, attn_w_in, attn_w_out,
           pool_w_in, pool_w_grp, pool_b_grp, pool_scale, pool_w_out):
    x = np.ascontiguousarray(np.asarray(x, dtype=np.float32))
    positions = np.asarray(positions).astype(np.int32)
    B, S, _ = x.shape
    ncores = 8
    cpb = ncores // B
    in_maps = make_in_maps(x, positions, norm_pre, norm_post, attn_w_in, attn_w_out,
                           pool_w_in, pool_w_grp, pool_b_grp, pool_scale, pool_w_out, ncores)
    if "f" not in _NC_CACHE:
        _NC_CACHE["f"] = build_fused()
    res = run_bass_kernel_spmd(_NC_CACHE["f"], in_maps, core_ids=list(range(ncores)))
    out = np.zeros((B, S, D), np.float32)
    for core in range(ncores):
        b, ch = core // cpb, core % cpb
        s0 = ch * 2048
        o = res.results[core]["out"]
        lo = 0 if ch == 0 else 16
        hi = min(2064, S - s0)
        out[b, s0 + lo:s0 + hi] = o[lo:hi]
    return out
```

```python
import contextlib
import os
import math
import numpy as np
import concourse.bass as bass
import concourse.mybir as mybir
from concourse.bass_utils import run_bass_kernel_spmd

F32 = mybir.dt.float32
BF16 = mybir.dt.bfloat16
I32 = mybir.dt.int32
AF = mybir.ActivationFunctionType
ALU = mybir.AluOpType
AX = mybir.AxisListType

ENGS = ("pe", "act", "dve", "pool", "sp")
DMAQ = ("act", "pool", "sp")
NDS = 8

D = 1024
E = 2048
T_OWN = 2048
EPS = 1e-6
TWO_PI = 2.0 * math.pi
INV2PI = 1.0 / TWO_PI
C1 = 6.28125
C2 = TWO_PI - C1
MAGIC = 12582912.0
PI_LO = 3.1415925
CW = 1044
XTOK = 4224
YTOK = 2176


class Op:
    __slots__ = ("eng", "fn", "deps", "dma", "signal", "sem", "val", "prev")

    def __init__(self, eng, fn, dma):
        self.eng = eng
        self.fn = fn
        self.dma = dma
        self.deps = []
        self.signal = False
        self.sem = None
        self.val = 0
        self.prev = None


class Prog:
    def __init__(self):
        self.ops = {e: [] for e in ENGS}
        self.lastw = {}
        self.readers = {}
        self.bar_idx = {e: 0 for e in ENGS}

    def add(self, eng, fn, reads=(), writes=(), dma=False):
        op = Op(eng, fn, dma)
        deps = {}
        psr = [r for r in reads if isinstance(r, tuple) and r[0] == "ps"]
        if psr:
            reads = [r for r in reads if not (isinstance(r, tuple) and r[0] == "ps")]
            writes = list(writes) + psr

        def need(d, raw):
            if d is None:
                return
            if (not dma) and (not d.dma) and d.eng == eng:
                if eng == "pe" or not raw:
                    return
            deps[id(d)] = d

        for r in reads:
            need(self.lastw.get(r), True)
        for r in writes:
            need(self.lastw.get(r), False)
            rd = self.readers.get(r)
            if rd:
                for k, v in rd.items():
                    if k == "dma":
                        for d in v:
                            need(d, False)
                    else:
                        need(v, False)
        for d in deps.values():
            d.signal = True
        op.deps = list(deps.values())
        for r in reads:
            rd = self.readers.setdefault(r, {})
            if dma:
                rd.setdefault("dma", []).append(op)
            else:
                rd[eng] = op
        for r in writes:
            self.lastw[r] = op
            self.readers[r] = {}
        self.ops[eng].append(op)
        return op

    def barrier(self):
        lasts = []
        for e in ENGS:
            for op in reversed(self.ops[e]):
                if (not op.dma) and op.fn is not None:
                    lasts.append(op)
                    break
        dmas = [op for e in ENGS for op in self.ops[e][self.bar_idx[e]:] if op.dma]
        for e in ENGS:
            w = Op(e, None, False)
            w.deps = [d for d in lasts if d.eng != e] + dmas
            for d in w.deps:
                d.signal = True
            self.ops[e].append(w)
        self.bar_idx = {e: len(self.ops[e]) for e in ENGS}
        self.lastw.clear()
        self.readers.clear()

    def finalize(self, nc, stack):
        self.psem = {e: stack.enter_context(nc.semaphore("p_" + e)) for e in ENGS}
        self.dsem = {e: [stack.enter_context(nc.semaphore("d_%s%d" % (e, i))) for i in range(NDS)]
                     for e in DMAQ}
        self.semobj = {}
        for e in ENGS:
            self.semobj[("p", e)] = self.psem[e]
        for e in DMAQ:
            for i in range(NDS):
                self.semobj[("d", e, i)] = self.dsem[e][i]
        for e in ENGS:
            cnt = 0
            di = 0
            for op in self.ops[e]:
                if op.fn is None:
                    continue
                if op.dma:
                    op.sem = ("d", e, di % NDS)
                    op.val = 16 * (di // NDS + 1)
                    op.prev = (op.sem, op.val - 16) if di >= NDS else None
                    di += 1
                elif op.signal:
                    cnt += 1
                    op.sem = ("p", e)
                    op.val = cnt

    def emit(self, eng, e):
        waited = {}
        for op in self.ops[eng]:
            w = {}
            for d in op.deps:
                if w.get(d.sem, 0) < d.val:
                    w[d.sem] = d.val
            if op.dma and op.prev is not None:
                if w.get(op.prev[0], 0) < op.prev[1]:
                    w[op.prev[0]] = op.prev[1]
            for sem, val in w.items():
                if waited.get(sem, 0) < val:
                    e.wait_ge(self.semobj[sem], val)
                    waited[sem] = val
            if op.fn is not None:
                ins = op.fn(e)
                if op.dma:
                    ins.then_inc(self.semobj[op.sem], 16)
                elif op.signal:
                    ins.then_inc(self.semobj[op.sem], 1)

    def run(self, nc, stack):
        self.barrier()
        self.finalize(nc, stack)
        block = stack.enter_context(nc.Block())

        @block.tensor
        def _(e):
            self.emit("pe", e)

        @block.scalar
        def _(e):
            self.emit("act", e)

        @block.vector
        def _(e):
            self.emit("dve", e)

        @block.gpsimd
        def _(e):
            self.emit("pool", e)

        @block.sync
        def _(e):
            self.emit("sp", e)


class TT:
    def __init__(self, t, pitch, base=0):
        self.t = t
        self.pitch = pitch
        self.base = base

    def a(self, off, dims, p0=0, np_=128):
        return bass.AP(self.t, p0 * self.pitch + self.base + off, [[self.pitch, np_]] + [list(d) for d in dims])


def DAP(t, off, dims):
    return bass.AP(t, off, [list(d) for d in dims])


def MM(out, lhsT, rhs, start, stop):
    return lambda e: e.matmul(out, lhsT=lhsT, rhs=rhs, start=start, stop=stop)


def TR(out, in_, ident):
    return lambda e: e.transpose(out, in_, ident)


def ACT(out, in_, func, scale=None, bias=None):
    kw = {}
    if scale is not None:
        kw["scale"] = scale
    if bias is not None:
        kw["bias"] = bias
    return lambda e: e.activation(out=out, in_=in_, func=func, **kw)


def TTO(out, in0, in1, op):
    return lambda e: e.tensor_tensor(out=out, in0=in0, in1=in1, op=op)


def TS(out, in0, s1, op0, s2=None, op1=None):
    if op1 is None:
        return lambda e: e.tensor_scalar(out=out, in0=in0, scalar1=s1, scalar2=None, op0=op0)
    return lambda e: e.tensor_scalar(out=out, in0=in0, scalar1=s1, scalar2=s2, op0=op0, op1=op1)


def STT(out, in0, scalar, in1, op0, op1):
    return lambda e: e.scalar_tensor_tensor(out=out, in0=in0, scalar=scalar, in1=in1, op0=op0, op1=op1)


def CP(out, in_):
    return lambda e: e.tensor_copy(out=out, in_=in_)


def ACP(out, in_):
    return lambda e: e.activation(out=out, in_=in_, func=AF.Copy)


def RECIP(out, in_):
    return lambda e: e.reciprocal(out=out, in_=in_)


def RECIPF(out, in_):
    return lambda e: e.reciprocal_approx_fast(out=out, in_=in_)


def RSUM(out, in_):
    return lambda e: e.reduce_sum(out=out, in_=in_, axis=AX.X)


def MSET(ap, c):
    return lambda e: e.memset(ap, c)


def DMA(out, in_):
    return lambda e: e.dma_start(out=out, in_=in_)


class Ctx:
    def __init__(self, nc):
        self.nc = nc
        self.P = Prog()

    def sb(self, stack, name, free, dt):
        self.n = getattr(self, "n", 0) + 1
        t = stack.enter_context(self.nc.sbuf_tensor("%s_%d" % (name, self.n), [128, free], dt))
        return TT(t, free)

    def ps(self, stack, name, free, dt):
        self.n = getattr(self, "n", 0) + 1
        t = stack.enter_context(self.nc.psum_tensor("%s_%d" % (name, self.n), [128, free], dt))
        return TT(t, free)


def emit_norm_transpose(cx, stack, xsrc, ntiles, xnT, xnT_tok, gb, identb, row0=0, keep=None, hook=None):
    P = cx.P
    NS = 4
    XT = [cx.sb(stack, "XT%d" % i, 1024, F32) for i in range(NS)]
    SQ = [cx.sb(stack, "SQ%d" % i, 1024, F32) for i in range(2)]
    XNB = [cx.sb(stack, "XNB%d" % i, 1024, BF16) for i in range(NS)]
    ST = cx.sb(stack, "STn", 3 * 64, F32)
    TP = [cx.ps(stack, "TP%d" % i, 1024, BF16) for i in range(NS)]

    def stage_a(tt):
        s, s2 = tt % NS, tt % 2
        P.add("sp", DMA(XT[s].a(0, [[1, 1024]]), DAP(xsrc, (row0 + tt * 128) * 1024, [[1024, 128], [1, 1024]])),
              writes=[("XT", s)], dma=True)
        P.add("act", ACT(SQ[s2].a(0, [[1, 1024]]), XT[s].a(0, [[1, 1024]]), AF.Square),
              reads=[("XT", s)], writes=[("SQ", s2)])
        P.add("dve", RSUM(ST.a(tt, [[1, 1]]), SQ[s2].a(0, [[1, 1024]])), reads=[("SQ", s2)], writes=[("st0", tt)])

    def stage_b(tt):
        s = tt % NS
        P.add("act", ACT(ST.a(64 + tt, [[1, 1]]), ST.a(tt, [[1, 1]]), AF.Sqrt, scale=1.0 / D, bias=EPS),
              reads=[("st0", tt)], writes=[("st1", tt)])
        P.add("dve", RECIP(ST.a(128 + tt, [[1, 1]]), ST.a(64 + tt, [[1, 1]])), reads=[("st1", tt)], writes=[("st2", tt)])
        P.add("dve", STT(XNB[s].a(0, [[1, 1024]]), XT[s].a(0, [[1, 1024]]), ST.a(128 + tt, [[1, 1]]),
                         gb.a(0, [[1, 1024]]), ALU.mult, ALU.mult),
              reads=[("XT", s), ("st2", tt), "gb"], writes=[("XNB", s)])

    def stage_c(tt):
        s = tt % NS
        for c in range(8):
            P.add("pe", TR(TP[s].a(c * 128, [[1, 128]]), XNB[s].a(c * 128, [[1, 128]]), identb.a(0, [[1, 128]])),
                  reads=[("XNB", s), "identb"], writes=[("TP", s)])
        P.add("act", ACP(xnT.a(tt * 128, [[xnT_tok, 8], [1, 128]]), TP[s].a(0, [[128, 8], [1, 128]])),
              reads=[("TP", s)], writes=["xnT"])

    for it in range(ntiles + 2):
        if it < ntiles:
            stage_a(it)
        if 0 <= it - 1 < ntiles:
            stage_b(it - 1)
        if 0 <= it - 2 < ntiles:
            stage_c(it - 2)
        if hook is not None:
            hook(it)


def emit_out_phase(cx, stack, PS, YG, yg_tok, WO, xres, xres_row0, gpb, outd, ntiles=16, sfx="", yg_res=None):
    P = cx.P
    NX = 4
    XR = [cx.sb(stack, "XR%d" % i, 1024, F32) for i in range(NX)]
    SQ = [cx.sb(stack, "SQo%d" % i, 1024, F32) for i in range(2)]
    TO = [cx.sb(stack, "TO%d" % i, 1024, F32) for i in range(2)]
    ST = cx.sb(stack, "STo", 3 * 32, F32)
    for tt in range(ntiles):
        s = tt % 2
        sx = tt % NX
        P.add("sp", DMA(XR[sx].a(0, [[1, 1024]]), DAP(xres, (xres_row0 + tt * 128) * 1024, [[1024, 128], [1, 1024]])),
              writes=[("XR", sx)], dma=True)
        banks = [(2 * tt) % 8, (2 * tt + 1) % 8]
        for hh in range(2):
            b = banks[hh]
            for ec in range(16):
                P.add("pe", MM(PS[b].a(0, [[1, 512]]), YG.a(ec * yg_tok + tt * 128, [[1, 128]]),
                               WO.a(ec * 1024 + hh * 512, [[1, 512]]), ec == 0, ec == 15),
                      reads=(yg_res(ec, tt) if yg_res else [("YG", ec)]) + [("WO", ec // 4)], writes=[("ps", b)])
            P.add("act", ACT(SQ[s].a(hh * 512, [[1, 512]]), PS[b].a(0, [[1, 512]]), AF.Square),
                  reads=[("ps", b)], writes=[("SQo", s, hh)])
        P.add("dve", RSUM(ST.a(tt, [[1, 1]]), SQ[s].a(0, [[1, 1024]])),
              reads=[("SQo", s, 0), ("SQo", s, 1)], writes=[("so0", tt)])
        P.add("act", ACT(ST.a(32 + tt, [[1, 1]]), ST.a(tt, [[1, 1]]), AF.Sqrt, scale=1.0 / D, bias=EPS),
              reads=[("so0", tt)], writes=[("so1", tt)])
        P.add("dve", RECIP(ST.a(64 + tt, [[1, 1]]), ST.a(32 + tt, [[1, 1]])), reads=[("so1", tt)], writes=[("so2", tt)])
        for hh in range(2):
            b = banks[hh]
            P.add("dve", STT(TO[s].a(hh * 512, [[1, 512]]), PS[b].a(0, [[1, 512]]), ST.a(64 + tt, [[1, 1]]),
                             gpb.a(hh * 512, [[1, 512]]), ALU.mult, ALU.mult),
                  reads=[("ps", b), ("so2", tt), "gpb"], writes=[("TO", s, hh)])
        P.add("pool", TTO(XR[sx].a(0, [[1, 1024]]), XR[sx].a(0, [[1, 1024]]), TO[s].a(0, [[1, 1024]]), ALU.add),
              reads=[("XR", sx), ("TO", s, 0), ("TO", s, 1)], writes=[("XR", sx)])
        P.add("sp", DMA(DAP(outd, tt * 128 * 1024, [[1024, 128], [1, 1024]]), XR[sx].a(0, [[1, 1024]])),
              reads=[("XR", sx)], dma=True)


PARTS = [
    dict(g=0, d=1, R=[0], nkb=17),
    dict(g=1, d=4, R=[0, 1, 2, 3], nkb=5),
    dict(g=2, d=16, R=list(range(0, 16)), nkb=2),
]


def part_blocks(pt):
    d, nkb = pt["d"], pt["nkb"]
    nqb = nkb - 1
    kb0 = 16 // d - 1
    ktok0 = 2048 - 128 * d
    nk = 4096 - ktok0
    kblocks = [(c0, min(512, nk - c0), ktok0 + c0) for c0 in range(0, nk, 512)]
    qblocks = [(nb, 512, 2048 + 512 * nb) for nb in range(4)]
    return kblocks, qblocks, nqb, kb0


def cdims(dims):
    if len(dims) == 1:
        return [[1, dims[0][1]]]
    n1, n2 = dims[0][1], dims[1][1]
    return [[n2, n1], [1, n2]]


def build_fused():
    nc = bass.Bass("TRN2", target_bir_lowering=False)
    stop = None
    lim = [8, 4, 3]
    xin = nc.dram_tensor("xin", [XTOK, 1024], F32, kind="ExternalInput")
    pos = nc.dram_tensor("pos", [1, XTOK], I32, kind="ExternalInput")
    gpre = nc.dram_tensor("gpre", [1, 1024], F32, kind="ExternalInput")
    gpost = nc.dram_tensor("gpost", [1, 1024], F32, kind="ExternalInput")
    win = nc.dram_tensor("win", [1024, 20480], F32, kind="ExternalInput")
    wout = nc.dram_tensor("wout", [2048, 1024], F32, kind="ExternalInput")
    cst = nc.dram_tensor("cst", [128, CW], F32, kind="ExternalInput")
    gpre1 = nc.dram_tensor("gpre1", [1, 1024], F32, kind="ExternalInput")
    gpost1 = nc.dram_tensor("gpost1", [1, 1024], F32, kind="ExternalInput")
    win1 = nc.dram_tensor("win1", [1024, 4096], F32, kind="ExternalInput")
    wgrp = nc.dram_tensor("wgrp", [2048, 512], F32, kind="ExternalInput")
    wout1 = nc.dram_tensor("wout1", [2048, 1024], F32, kind="ExternalInput")
    cst1 = nc.dram_tensor("cst1", [128, 128 + 96], F32, kind="ExternalInput")
    ygd = nc.dram_tensor("ygd", [16, 128, YTOK], BF16, kind="Internal")
    h1 = nc.dram_tensor("h1d", [YTOK, 1024], F32, kind="Internal")
    outd = nc.dram_tensor("out", [YTOK, 1024], F32, kind="ExternalOutput")

    cx = Ctx(nc)
    P = cx.P
    scale_qk = 1.0 / math.sqrt(128.0)

    with contextlib.ExitStack() as s0:
        identb = cx.sb(s0, "identb", 128, BF16)
        permb = cx.sb(s0, "permb", 128, BF16)
        maskb = cx.sb(s0, "maskb", 256, BF16)
        onesH = cx.sb(s0, "onesH", 128, BF16)
        onesb = cx.sb(s0, "onesb", 128, BF16)
        cf = cx.sb(s0, "cf", 4, F32)
        MX = cx.sb(s0, "MX", 400, BF16)
        KVW = cx.sb(s0, "KVW", 16 * 1024, BF16)
        P.add("pool", DMA(MX.a(0, [[1, 400]]), DAP(cst, 644, [[CW, 128], [1, 400]])), writes=["MX"], dma=True)
        P.add("pool", DMA(identb.a(0, [[1, 128]]), DAP(cst, 0, [[CW, 128], [1, 128]])), writes=["identb"], dma=True)
        P.add("pool", DMA(permb.a(0, [[1, 128]]), DAP(cst, 128, [[CW, 128], [1, 128]])), writes=["permb"], dma=True)
        P.add("pool", DMA(maskb.a(0, [[1, 256]]), DAP(cst, 256, [[CW, 128], [1, 256]])), writes=["maskb"], dma=True)
        P.add("pool", DMA(onesH.a(0, [[1, 128]]), DAP(cst, 512, [[CW, 128], [1, 128]])), writes=["onesH"], dma=True)
        P.add("sp", DMA(cf.a(0, [[1, 4]]), DAP(cst, 640, [[CW, 128], [1, 4]])), writes=["cf"], dma=True)
        P.add("dve", MSET(onesb.a(0, [[1, 128]]), 1.0), writes=["onesb"])

        with contextlib.ExitStack() as s1:
            xnT = cx.sb(s1, "xnT", 8 * XTOK, BF16)
            cosb = cx.sb(s1, "cosb", XTOK, BF16)
            sinb = cx.sb(s1, "sinb", XTOK, BF16)

            with contextlib.ExitStack() as sA:
                gb = cx.sb(sA, "gb", 1024, F32)
                P.add("sp", DMA(gb.a(0, [[1, 1024]]), DAP(gpre, 0, [[0, 128], [1, 1024]])), writes=["gb"], dma=True)
                posi = cx.sb(sA, "posi", 1024, I32)
                tA = [cx.sb(sA, "tA%d" % i, 1024, F32) for i in range(5)]
                R32 = dict(p0=0, np_=32)
                def cs_chunk(q):
                    c0 = q * 1024
                    full = [[1, 1024 if q < 4 else 128]]
                    P.add("sp", DMA(posi.a(0, full, **R32), DAP(pos, c0, [[0, 32], full[0]])), writes=["posi"], dma=True)
                    ang, a1, a2, a3 = tA[0], tA[1], tA[2], tA[3]
                    P.add("dve", CP(a1.a(0, full, **R32), posi.a(0, full, **R32)), reads=["posi"], writes=["a1"])
                    P.add("dve", TS(ang.a(0, full, **R32), a1.a(0, full, **R32), cf.a(0, [[1, 1]], **R32), ALU.mult),
                          reads=["a1", "cf"], writes=["ang"])
                    P.add("dve", TS(a1.a(0, full, **R32), ang.a(0, full, **R32), INV2PI, ALU.mult), reads=["ang"], writes=["a1"])
                    P.add("dve", TS(a2.a(0, full, **R32), a1.a(0, full, **R32), MAGIC, ALU.add), reads=["a1"], writes=["a2"])
                    P.add("dve", TS(a1.a(0, full, **R32), a2.a(0, full, **R32), MAGIC, ALU.subtract), reads=["a2"], writes=["a1"])
                    P.add("dve", STT(a2.a(0, full, **R32), a1.a(0, full, **R32), -C1, ang.a(0, full, **R32), ALU.mult, ALU.add),
                          reads=["a1", "ang"], writes=["a2"])
                    P.add("dve", STT(a3.a(0, full, **R32), a1.a(0, full, **R32), -C2, a2.a(0, full, **R32), ALU.mult, ALU.add),
                          reads=["a1", "a2"], writes=["a3"])
                    P.add("dve", TS(a2.a(0, full, **R32), a3.a(0, full, **R32), -PI_LO, ALU.max, PI_LO, ALU.min),
                          reads=["a3"], writes=["a2"])
                    P.add("act", ACT(sinb.a(c0, full, **R32), a2.a(0, full, **R32), AF.Sin, scale=cf.a(1, [[1, 1]], **R32)),
                          reads=["a2", "cf"], writes=["sinb"])
                    P.add("dve", TS(a1.a(0, full, **R32), ang.a(0, full, **R32), INV2PI, ALU.mult, 0.25, ALU.add),
                          reads=["ang"], writes=["a1"])
                    P.add("dve", TS(a3.a(0, full, **R32), a1.a(0, full, **R32), MAGIC, ALU.add), reads=["a1"], writes=["a3"])
                    P.add("dve", TS(a1.a(0, full, **R32), a3.a(0, full, **R32), MAGIC, ALU.subtract), reads=["a3"], writes=["a1"])
                    a4 = tA[4]
                    P.add("dve", STT(a3.a(0, full, **R32), a1.a(0, full, **R32), -C1, ang.a(0, full, **R32), ALU.mult, ALU.add),
                          reads=["a1", "ang"], writes=["a3"])
                    P.add("dve", STT(a4.a(0, full, **R32), a1.a(0, full, **R32), -C2, a3.a(0, full, **R32), ALU.mult, ALU.add),
                          reads=["a1", "a3"], writes=["a4"])
                    P.add("dve", TS(a3.a(0, full, **R32), a4.a(0, full, **R32), 0.5 * math.pi, ALU.add), reads=["a4"], writes=["a3"])
                    P.add("dve", TS(a4.a(0, full, **R32), a3.a(0, full, **R32), -PI_LO, ALU.max, PI_LO, ALU.min),
                          reads=["a3"], writes=["a4"])
                    P.add("act", ACT(cosb.a(c0, full, **R32), a4.a(0, full, **R32), AF.Sin), reads=["a4"], writes=["cosb"])
                cs_at = {2: 0, 8: 1, 14: 2, 20: 3, 26: 4}
                emit_norm_transpose(cx, sA, xin, 33, xnT, XTOK, gb, identb,
                                    hook=lambda it: cs_chunk(cs_at[it]) if it in cs_at else None)
            P.barrier()

            with contextlib.ExitStack() as sB:
                PS = [cx.ps(sB, "PS%d" % i, 512, F32) for i in range(8)]
                QT = cx.sb(sB, "QT", 2 * 2048, BF16)
                KT = TT(KVW.t, 16 * 1024, 0)
                V = TT(KVW.t, 16 * 1024, 8192)
                ND = cx.sb(sB, "ND", 4 * 2048, F32)
                NDb = TT(ND.t.bitcast(BF16), 16384)
                WQ = [cx.sb(sB, "WQ%d" % i, 2048, BF16) for i in range(2)]
                WK = [cx.sb(sB, "WK%d" % i, 2048, BF16) for i in range(2)]
                WV = [cx.sb(sB, "WV%d" % i, 2048, BF16) for i in range(2)]
                WZ = cx.sb(sB, "WZ", 2048, BF16)
                T1 = [cx.sb(sB, "T1%d" % i, 512, F32) for i in range(3)]
                T2 = [cx.sb(sB, "T2%d" % i, 512, F32) for i in range(2)]
                PT = [cx.sb(sB, "PT%d" % i, 512, BF16) for i in range(4)]
                mask4 = cx.sb(sB, "mask4", 512, BF16)
                for q_ in range(4):
                    P.add("pool", TS(mask4.a(q_ * 128, [[1, 128]]), maskb.a((q_ // 2) * 128, [[1, 128]]), -1.0, ALU.add,
                                     30000.0, ALU.mult),
                          reads=["maskb"], writes=["mask4"])
                SZ = [cx.sb(sB, "SZ%d" % i, 512, F32) for i in range(2)]
                QX = cx.sb(sB, "QX", 32, BF16)
                KX = cx.sb(sB, "KX", 256, BF16)
                VX = cx.sb(sB, "VX", 256, BF16)
                NDX = cx.sb(sB, "NDX", 64, F32)
                PTX = cx.sb(sB, "PTX", 256, BF16)
                PTC = cx.sb(sB, "PTC", 16, BF16)
                YGX = cx.sb(sB, "YGX", 32, BF16)
                SZX = cx.sb(sB, "SZX", 16, F32)
                MXP_OFF = [0, 16, 80]

                def wload(Wt, name, colbase):
                    P.add("pool", DMA(Wt.a(0, [[256, 8], [1, 256]]),
                                      DAP(win, colbase, [[20480, 128], [128 * 20480, 8], [1, 256]])),
                          writes=[name], dma=True)

                def load_group_weights(bt, g, par):
                    base = g * 6144 + bt * 256
                    wload(WK[par], ("WK", par), base + 2048)
                    wload(WV[par], ("WV", par), base + 4096)
                    wload(WQ[par], ("WQ", par), base)

                cnt = dict(a=0, b=0, v=0, t=0, t2=0, s=0, o=0, p=0, z=0)
                nphase = 0
                load_group_weights(0, 0, 0)

                class Pipe:
                    def __init__(self, depth):
                        self.q = []
                        self.depth = depth

                    def push(self, fn):
                        self.q.append(fn)
                        while len(self.q) > self.depth:
                            self.q.pop(0)()

                    def flush(self):
                        while self.q:
                            self.q.pop(0)()

                ppipe = Pipe(1)

                def proj_block(Wt, wname, j, dst, dres, dbase, n, tok_off, d=1, dpitch=0):
                    ba = cnt["a"] % 4
                    cnt["a"] += 1
                    t1s = cnt["t"] % 3
                    cnt["t"] += 1
                    if d == 1:
                        nat = [[1, n]]
                        ri_ = [[1, n]]
                        con = [[1, n]]
                        dap = [[1, n]]
                    else:
                        nat = [[1, n]]
                        ri_ = [[1, d], [d, n // d]]
                        con = [[n // d, d], [1, n // d]]
                        dap = [[dpitch, d], [1, n // d]]
                    for c in range(8):
                        P.add("pe", MM(PS[ba].a(0, nat), Wt.a(c * 256 + j * 128, [[1, 128]]),
                                       xnT.a(c * XTOK + tok_off, nat), c == 0, c == 7),
                              reads=[wname, "xnT"], writes=[("ps", ba)])
                    P.add("act", ACP(dst.a(dbase, dap), PS[ba].a(0, ri_)), reads=[("ps", ba)], writes=dres)
                    ppipe.flush()
                    P.add("dve", TTO(T1[t1s].a(0, con, p0=0, np_=32), PS[ba].a(0, ri_, p0=0, np_=32),
                                     cosb.a(tok_off, ri_, p0=0, np_=32), ALU.mult),
                          reads=[("ps", ba), "cosb"], writes=[("T1", t1s)])

                    def stage2():
                        bb = 4 + cnt["b"] % 2
                        cnt["b"] += 1
                        t2s = cnt["t2"] % 2
                        cnt["t2"] += 1
                        P.add("pe", MM(PS[bb].a(0, con), permb.a(0, [[1, 128]]), dst.a(dbase, dap), True, True),
                              reads=dres + ["permb"], writes=[("ps", bb)])
                        P.add("dve", TTO(T2[t2s].a(0, con, p0=0, np_=32), PS[bb].a(0, con, p0=0, np_=32),
                                         sinb.a(tok_off, ri_, p0=0, np_=32), ALU.mult),
                              reads=[("ps", bb), "sinb"], writes=[("T2", t2s)])
                        P.add("pool", TTO(dst.a(dbase, dap, p0=0, np_=32), T1[t1s].a(0, con, p0=0, np_=32),
                                          T2[t2s].a(0, con, p0=0, np_=32), ALU.add),
                              reads=[("T1", t1s), ("T2", t2s)], writes=dres)

                    ppipe.push(stage2)

                for bt in range(lim[0]):
                    for pi, pt in enumerate(PARTS[:lim[1]]):
                        g, d, R, nkb = pt["g"], pt["d"], pt["R"], pt["nkb"]
                        kblocks, qblocks, nqb, kb0 = part_blocks(pt)
                        par = nphase % 2
                        nphase += 1
                        if pi < 2:
                            load_group_weights(bt, g + 1, nphase % 2)
                        elif bt < 7:
                            load_group_weights(bt + 1, 0, nphase % 2)
                        if pi == 0:
                            wload(WZ, "WZ", 18432 + bt * 256)
                        for j in range(2):
                            for (nb, n, tok_off) in qblocks:
                                if d == 1:
                                    qres_w = [("QT", j, b_, q_) for b_ in range(4 * nb, 4 * nb + 4) for q_ in range(4)]
                                elif d == 4:
                                    qres_w = [("QT", j, r_ * 4 + nb, q_) for r_ in range(4) for q_ in range(4)]
                                else:
                                    qres_w = [("QT", j, r_, nb) for r_ in range(16)]
                                proj_block(WQ[par], ("WQ", par), j, QT, qres_w,
                                           j * 2048 + (512 * nb) // d, n, tok_off, d=d, dpitch=2048 // d)
                        for j in range(2):
                            for (c0, n, tok_off) in kblocks:
                                blk = c0 // 512
                                if d == 1:
                                    kres_w = [("KT", j, c0 // 128 + b_, q_) for b_ in range(n // 128) for q_ in range(4)]
                                elif d == 4:
                                    kres_w = [("KT", j, r_ * nkb + blk, q_) for r_ in range(4) for q_ in range(4)]
                                else:
                                    kres_w = [("KT", j, r_ * nkb + blk // 4, blk % 4) for r_ in range(16)]
                                proj_block(WK[par], ("WK", par), j, KT, kres_w,
                                           j * 4096 + c0 // d, n, tok_off, d=d, dpitch=128 * nkb)
                        nvt = len(R) * nkb
                        for vt in range(nvt):
                            ri, kb = vt // nkb, vt % nkb
                            tok0 = R[ri] + d * 128 * (kb0 + kb)
                            if vt % 2 == 0:
                                bv = 6 + cnt["v"] % 2
                                cnt["v"] += 1
                            for c in range(8):
                                P.add("pe", MM(PS[bv].a((vt % 2) * 256, [[1, 256]]), xnT.a(c * XTOK + tok0, [[d, 128]]),
                                               WV[par].a(c * 256, [[1, 256]]), c == 0, c == 7),
                                      reads=["xnT", ("WV", par)], writes=[("ps", bv)])
                            if vt % 2 == 1 or vt == nvt - 1:
                                v0 = vt - (vt % 2)
                                nn = vt - v0 + 1
                                P.add("act", ACP(V.a(v0 * 256, [[1, nn * 256]]), PS[bv].a(0, [[1, nn * 256]])),
                                      reads=[("ps", bv)], writes=[("V", v_) for v_ in range(v0, v0 + nn)])
                        ppipe.flush()
                        apipe = Pipe(2)
                        for ri in range(len(R) if lim[2] >= 2 else 0):
                            for qb in range(nqb):
                                qs = (ri * nqb + qb) * 128
                                kprev = ri * nkb + qb
                                kcur = kprev + 1
                                bs = cnt["s"] % 4
                                cnt["s"] += 1
                                pslot = cnt["p"] % 4
                                cnt["p"] += 1
                                P.add("pe", MM(PS[bs].a(0, [[1, 512]]), identb.a(0, [[1, 128]]), mask4.a(0, [[1, 512]]), True, False),
                                      reads=["identb", "mask4"], writes=[("ps", bs)])
                                for hh, kq in enumerate((qb, qb + 1)):
                                    for j in range(2):
                                        qres = [("QT", j, qs // 128, q_) for q_ in range(4)]
                                        kres = [("KT", j, ri * nkb + kq, q_) for q_ in range(4)]
                                        P.add("pe", MM(PS[bs].a(hh * 256 + j * 128, [[1, 128]]),
                                                       KT.a(j * 4096 + (ri * nkb + kq) * 128, [[1, 128]]),
                                                       QT.a(j * 2048 + qs, [[1, 128]]), False, hh == 1 and j == 1),
                                              reads=qres + kres, writes=[("ps", bs)])
                                P.add("act", ACT(PT[pslot].a(0, [[1, 512]]), PS[bs].a(0, [[1, 512]]), AF.Exp, scale=scale_qk),
                                      reads=[("ps", bs)], writes=[("PT", pslot)])

                                def stage2(ri=ri, qb=qb, kprev=kprev, kcur=kcur, pslot=pslot):
                                    bo = 4 + cnt["o"] % 4
                                    cnt["o"] += 1
                                    for j in range(2):
                                        for hh, kbk in enumerate((kprev, kcur)):
                                            P.add("pe", MM(PS[bo].a(j * 128, [[1, 128]]), V.a(kbk * 256 + j * 128, [[1, 128]]),
                                                           PT[pslot].a(hh * 256 + j * 128, [[1, 128]]), hh == 0, hh == 1),
                                                  reads=[("V", kbk), ("PT", pslot)], writes=[("ps", bo)])
                                    for hh in range(2):
                                        ones_t, ones_n = (onesH, "onesH") if (hh == 0 and qb == 0) else (onesb, "onesb")
                                        P.add("pe", MM(PS[bo].a(256, [[1, 256]]), ones_t.a(0, [[1, 128]]),
                                                       PT[pslot].a(hh * 256, [[1, 256]]), hh == 0, hh == 1),
                                              reads=[ones_n, ("PT", pslot)], writes=[("ps", bo)])
                                    tq0 = R[ri] + d * 128 * (16 // d + qb) - 2048
                                    ndres = [("ND", j, b_) for j in range(2) for b_ in range(tq0 // 128, tq0 // 128 + d)]
                                    ndap = ND.a(tq0, [[4096, 2], [2048, 2], [d, 128]])
                                    psap = PS[bo].a(0, [[256, 2], [128, 2], [1, 128]])
                                    if pi == 0:
                                        P.add("dve", CP(ndap, psap), reads=[("ps", bo)], writes=ndres)
                                    else:
                                        P.add("dve", TTO(ndap, psap, ndap, ALU.add),
                                              reads=[("ps", bo)] + ndres, writes=ndres)

                                apipe.push(stage2)
                        apipe.flush()
                        if True:
                            for j in range(2):
                                proj_block(WQ[par], ("WQ", par), j, QX, [("QX", j, 0)], j * 16, 16, 4096)
                                proj_block(WK[par], ("WK", par), j, KX, [("KX", j, 0)], j * 128, 128, 4096)
                            bv = 6 + cnt["v"] % 2
                            cnt["v"] += 1
                            for c in range(8):
                                P.add("pe", MM(PS[bv].a(0, [[1, 256]]), xnT.a(c * XTOK + 4096, [[1, 128]]),
                                               WV[par].a(c * 256, [[1, 256]]), c == 0, c == 7),
                                      reads=["xnT", ("WV", par)], writes=[("ps", bv)])
                            P.add("act", ACP(VX.a(0, [[1, 256]]), PS[bv].a(0, [[1, 256]])),
                                  reads=[("ps", bv)], writes=["VX"])
                            ppipe.flush()
                        nR = len(R)
                        for j in range(2):
                            bs = cnt["s"] % 4
                            cnt["s"] += 1
                            bo = 4 + cnt["o"] % 4
                            cnt["o"] += 1
                            for ri in range(nR):
                                kq = nkb - 1
                                kres = [("KT", j, ri * nkb + kq, q_) for q_ in range(4)]
                                P.add("pe", MM(PS[bs].a(ri * 16, [[1, 16]]), KT.a(j * 4096 + (ri * nkb + kq) * 128, [[1, 128]]),
                                               QX.a(j * 16, [[1, 16]]), True, True),
                                      reads=[("QX", j, 0)] + kres, writes=[("ps", bs)])
                            P.add("pe", MM(PS[bs].a(256, [[1, 16]]), KX.a(j * 128, [[1, 128]]),
                                           QX.a(j * 16, [[1, 16]]), True, True),
                                  reads=[("QX", j, 0), ("KX", j, 0)], writes=[("ps", bs)])
                            P.add("act", ACT(PTX.a(0, [[1, nR * 16]]), PS[bs].a(0, [[1, nR * 16]]), AF.Exp, scale=scale_qk),
                                  reads=[("ps", bs)], writes=["PTX"])
                            P.add("act", ACT(PTC.a(0, [[1, 16]]), PS[bs].a(256, [[1, 16]]),
                                             AF.Exp, scale=scale_qk),
                                  reads=[("ps", bs)], writes=["PTC"])
                            P.add("pool", TTO(PTX.a(0, [[1, nR * 16]]), PTX.a(0, [[1, nR * 16]]),
                                              MX.a(MXP_OFF[pi], [[1, nR * 16]]), ALU.mult),
                                  reads=["PTX", "MX"], writes=["PTX"])
                            P.add("pool", TTO(PTC.a(0, [[1, 16]]), PTC.a(0, [[1, 16]]),
                                              MX.a(336 + pi * 16, [[1, 16]]), ALU.mult),
                                  reads=["PTC", "MX"], writes=["PTC"])
                            for which in range(2):
                                for ri in range(nR):
                                    kbl = ri * nkb + nkb - 1
                                    lhs = V.a(kbl * 256 + j * 128, [[1, 128]]) if which == 0 else onesb.a(0, [[1, 128]])
                                    P.add("pe", MM(PS[bo].a(which * 16, [[1, 16]]), lhs, PTX.a(ri * 16, [[1, 16]]),
                                                   ri == 0, False),
                                          reads=[("V", kbl), "PTX", "onesb"], writes=[("ps", bo)])
                                lhs = VX.a(j * 128, [[1, 128]]) if which == 0 else onesb.a(0, [[1, 128]])
                                P.add("pe", MM(PS[bo].a(which * 16, [[1, 16]]), lhs, PTC.a(0, [[1, 16]]),
                                               False, True),
                                      reads=["VX", "PTC", "onesb"], writes=[("ps", bo)])
                            ndx = NDX.a(j * 16, [[32, 2], [1, 16]])
                            if pi == 0:
                                P.add("dve", CP(ndx, PS[bo].a(0, [[16, 2], [1, 16]])), reads=[("ps", bo)], writes=[("NDX", j)])
                            else:
                                P.add("dve", TTO(ndx, PS[bo].a(0, [[16, 2], [1, 16]]), ndx, ALU.add),
                                      reads=[("ps", bo), ("NDX", j)], writes=[("NDX", j)])
                    if bt == lim[0] - 1:
                        kv_all = ([("KT", j_, b_, q_) for j_ in range(2) for b_ in range(32) for q_ in range(4)]
                                  + [("V", v_) for v_ in range(32)])
                        for q in range(4):
                            P.add("pool", DMA(KVW.a(q * 4096, [[1024, 4], [1, 1024]]),
                                              DAP(wout, q * 4 * 128 * 1024, [[1024, 128], [128 * 1024, 4], [1, 1024]])),
                                  writes=kv_all + [("WO", q)], dma=True)
                    if lim[2] < 3:
                        continue
                    for j in range(2):
                        for hq in range(2):
                            r_ = [("ND", j, b_) for b_ in range(hq * 8, hq * 8 + 8)]
                            apx = ND.a(4096 + j * 2048 + hq * 1024, [[1, 1024]])
                            P.add("act", ACT(apx, apx, AF.Ln), reads=r_, writes=r_)
                            P.add("act", ACT(apx, apx, AF.Exp, scale=-1.0), reads=r_, writes=r_)
                    for j in range(2):
                        for tb in range(4):
                            bz = cnt["z"] % 3
                            cnt["z"] += 1
                            zs = cnt["z"] % 2
                            for c in range(8):
                                P.add("pe", MM(PS[bz].a(0, [[1, 512]]), WZ.a(c * 256 + j * 128, [[1, 128]]),
                                               xnT.a(c * XTOK + 2048 + tb * 512, [[1, 512]]), c == 0, c == 7),
                                      reads=["WZ", "xnT"], writes=[("ps", bz)])
                            P.add("act", ACT(SZ[zs].a(0, [[1, 512]]), PS[bz].a(0, [[1, 512]]), AF.Silu),
                                  reads=[("ps", bz)], writes=[("SZ", zs)])
                            r_ = [("ND", j, b_) for b_ in range(tb * 4, tb * 4 + 4)]
                            nap = ND.a(j * 2048 + tb * 512, [[1, 512]])
                            dap_ = ND.a(4096 + j * 2048 + tb * 512, [[1, 512]])
                            P.add("dve", TTO(nap, nap, dap_, ALU.mult), reads=r_, writes=r_)
                            P.add("dve", TTO(NDb.a(8192 + j * 2048 + tb * 512, [[1, 512]]), nap, SZ[zs].a(0, [[1, 512]]), ALU.mult),
                                  reads=r_ + [("SZ", zs)], writes=[("YGS", j, tb)] + [("ND", 0, b_) for b_ in range(16)])
                    P.add("sp", DMA(DAP(ygd, 2 * bt * 128 * YTOK, [[YTOK, 128], [128 * YTOK, 2], [1, 2048]]),
                                    NDb.a(8192, [[2048, 2], [1, 2048]])),
                          reads=[("YGS", j, tb) for j in range(2) for tb in range(4)] + [("ND", 0, b_) for b_ in range(16)],
                          dma=True)
                    P.add("dve", RECIP(NDX.a(32, [[1, 32]]), NDX.a(32, [[1, 32]])), reads=[("NDX", 0), ("NDX", 1)],
                          writes=[("NDX", 0), ("NDX", 1)])
                    P.add("dve", TTO(NDX.a(0, [[1, 32]]), NDX.a(0, [[1, 32]]), NDX.a(32, [[1, 32]]), ALU.mult),
                          reads=[("NDX", 0), ("NDX", 1)], writes=[("NDX", 0), ("NDX", 1)])
                    for j in range(2):
                        bz = cnt["z"] % 3
                        cnt["z"] += 1
                        for c in range(8):
                            P.add("pe", MM(PS[bz].a(0, [[1, 16]]), WZ.a(c * 256 + j * 128, [[1, 128]]),
                                           xnT.a(c * XTOK + 4096, [[1, 16]]), c == 0, c == 7),
                                  reads=["WZ", "xnT"], writes=[("ps", bz)])
                        P.add("act", ACT(SZX.a(0, [[1, 16]]), PS[bz].a(0, [[1, 16]]), AF.Silu),
                              reads=[("ps", bz)], writes=["SZX"])
                        P.add("dve", TTO(YGX.a(j * 16, [[1, 16]]), NDX.a(j * 16, [[1, 16]]), SZX.a(0, [[1, 16]]), ALU.mult),
                              reads=[("NDX", j), "SZX"], writes=["YGX"])
                    P.add("sp", DMA(DAP(ygd, 2 * bt * 128 * YTOK + 2048, [[YTOK, 128], [128 * YTOK, 2], [1, 16]]),
                                    YGX.a(0, [[16, 2], [1, 16]])),
                          reads=["YGX"], dma=True)
            P.barrier()

        with contextlib.ExitStack() as sC:
            PS = [cx.ps(sC, "PSc%d" % i, 512, F32) for i in range(8)]
            YG = cx.sb(sC, "YG", 16 * YTOK, BF16)
            WO = KVW
            gpb = cx.sb(sC, "gpb", 1024, F32)
            P.add("sp", DMA(gpb.a(0, [[1, 1024]]), DAP(gpost, 0, [[0, 128], [1, 1024]])), writes=["gpb"], dma=True)
            for r in range(5):
                n_ = 512 if r < 4 else 16
                P.add("sp", DMA(YG.a(512 * r, [[YTOK, 16], [1, n_]]),
                                DAP(ygd, 512 * r, [[YTOK, 128], [128 * YTOK, 16], [1, n_]])),
                      writes=[("YG", "r", r)], dma=True)
            P.add("dve", MSET(YG.a(2064, [[YTOK, 16], [1, YTOK - 2064]]), 0.0), writes=[("YG", "r", 4)])
            emit_out_phase(cx, sC, PS, YG, YTOK, WO, xin, 2048, gpb, h1, ntiles=17,
                           yg_res=lambda ec, tt: [("YG", "r", min(tt // 4, 4))])
        P.barrier()
        emit_l1(cx, s0, h1, gpre1, gpost1, win1, wgrp, wout1, cst1, outd, identb, kvw=KVW)
        P.run(nc, s0)
    return nc


POOL_W = (2, 4, 8, 16)
NT1 = 2176


def emit_l1(cx, s0, hin, gpre, gpost, win, wgrp, wout, cst, out, identb, kvw=None):
    P = cx.P
    CW1 = 128 + 96
    if True:
        cf = cx.sb(s0, "cf1", 144, F32)
        YG = cx.sb(s0, "YG1", 16 * NT1, BF16)
        P.add("sp", DMA(cf.a(0, [[1, 96]]), DAP(cst, 128, [[CW1, 128], [1, 96]])), writes=["cf"], dma=True)
        for g_ in range(4):
            P.add("dve", TS(cf.a(96 + 4 * g_, [[1, 4]]), cf.a(16 + 4 * g_, [[1, 4]]), 1.0 / POOL_W[g_], ALU.mult),
                  reads=["cf"], writes=["cf"])
        P.add("dve", TS(cf.a(112, [[1, 16]]), cf.a(16, [[1, 16]]), -1.0, ALU.mult), reads=["cf"], writes=["cf"])
        P.add("dve", TTO(cf.a(128, [[1, 16]]), cf.a(0, [[1, 16]]), cf.a(16, [[1, 16]]), ALU.mult), reads=["cf"], writes=["cf"])
        with contextlib.ExitStack() as s1:
            xnT = cx.sb(s1, "xnT1", 8 * NT1, BF16)
            with contextlib.ExitStack() as sA:
                gb = cx.sb(sA, "gb1", 1024, F32)
                P.add("sp", DMA(gb.a(0, [[1, 1024]]), DAP(gpre, 0, [[0, 128], [1, 1024]])), writes=["gb"], dma=True)
                emit_norm_transpose(cx, sA, hin, 17, xnT, NT1, gb, identb)
            P.barrier()
            with contextlib.ExitStack() as sB:
                PS = [cx.ps(sB, "PS%d" % i, 512, F32) for i in range(8)]
                WG = [cx.sb(sB, "WG%d" % i, 4 * 512, BF16) for i in range(2)]
                if kvw is not None:
                    WU = [TT(kvw.t, 16 * 1024, 13088), TT(kvw.t, 16 * 1024, 14112)]
                    WZ = [TT(kvw.t, 16 * 1024, 15136), cx.sb(sB, "WZ1", 8 * 128, BF16)]
                else:
                    WU = [cx.sb(sB, "WU%d" % i, 8 * 128, BF16) for i in range(2)]
                    WZ = [cx.sb(sB, "WZ%d" % i, 8 * 128, BF16) for i in range(2)]
                UB = TT(kvw.t, 16 * 1024, 0) if kvw is not None else cx.sb(sB, "UB", 4 * NT1, BF16)
                UW = 16 + NT1
                VV = [cx.sb(sB, "VV%d" % i, UW, F32) for i in range(2)]
                SA = TT(kvw.t.bitcast(F32), 8192, 4352) if kvw is not None else cx.sb(sB, "SA", UW, F32)
                SBf = cx.sb(sB, "SB", UW, F32)
                SZ = [cx.sb(sB, "SZ%d" % i, NT1, F32) for i in range(2)]
                for i in range(2):
                    P.add("dve", MSET(VV[i].a(0, [[1, 16]]), 0.0), writes=[("VV", i, 0)])
                A1 = [cx.sb(sB, "A1%d" % i, UW, F32) for i in range(2)]
                A0 = cx.sb(sB, "A0", 32, F32)
                HALF = (("dve", 0, UW),)
                blocks = [(tb * 512, 512) for tb in range(4)] + [(2048, 128)]
                cnt = dict(a=0, z=0, h=0)

                def both(rd):
                    return [(rd[0], rd[1], 0)]

                for g in range(4):
                    w = POOL_W[g]
                    nsteps = g + 1
                    gp = g % 2
                    P.add("pool", DMA(WG[gp].a(0, [[512, 4], [1, 512]]),
                                      DAP(wgrp, g * 512 * 512, [[512, 128], [128 * 512, 4], [1, 512]])),
                          writes=[("WG", gp)], dma=True)
                    for ic in range(4):
                        uc = g * 4 + ic
                        ws = uc % 2
                        P.add("pool", DMA(WU[ws].a(0, [[128, 8], [1, 128]]),
                                          DAP(win, uc * 128, [[4096, 128], [128 * 4096, 8], [1, 128]])),
                              writes=[("WU", ws)], dma=True)
                        for (t0, n) in blocks:
                            ba = cnt["a"] % 3
                            cnt["a"] += 1
                            for c in range(8):
                                P.add("pe", MM(PS[ba].a(0, [[1, n]]), WU[ws].a(c * 128, [[1, 128]]),
                                               xnT.a(c * NT1 + t0, [[1, n]]), c == 0, c == 7),
                                      reads=[("WU", ws), "xnT"], writes=[("ps", ba)])
                            P.add("act", ACP(UB.a(ic * NT1 + t0, [[1, n]]), PS[ba].a(0, [[1, n]])),
                                  reads=[("ps", ba)], writes=[("UB", ic, t0)])
                    for oc in range(4):
                        e_ = g * 4 + oc
                        zs = e_ % 2
                        v = e_ % 2
                        P.add("pool", DMA(WZ[zs].a(0, [[128, 8], [1, 128]]),
                                          DAP(win, 2048 + e_ * 128, [[4096, 128], [128 * 4096, 8], [1, 128]])),
                              writes=[("WZ", zs)], dma=True)
                        for (t0, n) in blocks:
                            bh = 3 + cnt["h"] % 2
                            cnt["h"] += 1
                            for ic in range(4):
                                P.add("pe", MM(PS[bh].a(0, [[1, n]]), WG[gp].a(ic * 512 + oc * 128, [[1, 128]]),
                                               UB.a(ic * NT1 + t0, [[1, n]]), ic == 0, ic == 3),
                                      reads=[("WG", gp), ("UB", ic, t0)], writes=[("ps", bh)])
                            hv = 0
                            wres = [("VV", v, 0), ("VV", v, 1)] if hv == 2 else [("VV", v, hv)]
                            P.add("act", ACP(VV[v].a(16 + t0, [[1, n]]), PS[bh].a(0, [[1, n]])),
                                  reads=[("ps", bh)], writes=wres)
                            P.add("act", ACT(A1[v].a(16 + t0, [[1, n]]), PS[bh].a(0, [[1, n]]), AF.Identity,
                                             scale=cf.a(112 + e_, [[1, 1]]), bias=cf.a(128 + e_, [[1, 1]])),
                                  reads=[("ps", bh), "cf"], writes=[("A1", v)])
                        for (t0, n) in blocks:
                            bz = 5 + cnt["z"] % 3
                            cnt["z"] += 1
                            for c in range(8):
                                P.add("pe", MM(PS[bz].a(0, [[1, n]]), WZ[zs].a(c * 128, [[1, 128]]),
                                               xnT.a(c * NT1 + t0, [[1, n]]), c == 0, c == 7),
                                      reads=[("WZ", zs), "xnT"], writes=[("ps", bz)])
                            hv = 0
                            wres = [("SZ", v, 0), ("SZ", v, 1)] if hv == 2 else [("SZ", v, hv)]
                            P.add("act", ACT(SZ[v].a(t0, [[1, n]]), PS[bz].a(0, [[1, n]]), AF.Silu),
                                  reads=[("ps", bz)], writes=wres)
                        P.add("dve", CP(A0.a(16, [[1, 16]]), A1[v].a(16, [[1, 16]])), reads=[("A1", v)], writes=["A0"])
                        src, srcn = VV[v], ("VV", v)
                        m = 1
                        for st in range(nsteps):
                            dst, dstn = (SA, ("SA", 0)) if st % 2 == 0 else (SBf, ("SB", 0))
                            lo = 2 * m - 1
                            for hi_, (eng, c0, c1) in enumerate(HALF):
                                c0_ = max(c0, lo)
                                P.add(eng, TTO(dst.a(c0_, [[1, c1 - c0_]]), src.a(c0_, [[1, c1 - c0_]]),
                                               src.a(c0_ - m, [[1, c1 - c0_]]), ALU.add),
                                      reads=both(srcn) if hi_ == 1 else [(srcn[0], srcn[1], 0)],
                                      writes=[(dstn[0], dstn[1], hi_)])
                            src, srcn = dst, dstn
                            m *= 2
                        for hi_, (eng, c0, c1) in enumerate(HALF):
                            c0_ = max(c0, 16)
                            rs = [(srcn[0], srcn[1], hi_)]
                            a1s = [("A1", v)]
                            if eng == "dve":
                                P.add(eng, STT(A1[v].a(c0_, [[1, c1 - c0_]]), src.a(c0_, [[1, c1 - c0_]]), cf.a(96 + e_, [[1, 1]]),
                                               A1[v].a(c0_, [[1, c1 - c0_]]), ALU.mult, ALU.add),
                                      reads=rs + a1s + ["cf"], writes=a1s)
                                P.add(eng, TTO(src.a(16, [[1, 16]]), src.a(16, [[1, 16]]), cf.a(32 + g * 16, [[1, 16]]), ALU.mult),
                                      reads=rs + ["cf"], writes=rs)
                                P.add(eng, STT(A1[v].a(16, [[1, 16]]), src.a(16, [[1, 16]]), cf.a(16 + e_, [[1, 1]]),
                                               A0.a(16, [[1, 16]]), ALU.mult, ALU.add),
                                      reads=rs + ["A0", "cf"], writes=a1s)
                            else:
                                P.add(eng, TS(src.a(c0_, [[1, c1 - c0_]]), src.a(c0_, [[1, c1 - c0_]]), cf.a(96 + e_, [[1, 1]]), ALU.mult),
                                      reads=rs + ["cf"], writes=rs)
                                P.add(eng, TTO(A1[v].a(c0_, [[1, c1 - c0_]]), A1[v].a(c0_, [[1, c1 - c0_]]), src.a(c0_, [[1, c1 - c0_]]), ALU.add),
                                      reads=rs + a1s, writes=a1s)
                            P.add(eng, TTO(YG.a(e_ * NT1 + c0_ - 16, [[1, c1 - c0_]]), A1[v].a(c0_, [[1, c1 - c0_]]),
                                           SZ[v].a(c0_ - 16, [[1, c1 - c0_]]), ALU.mult),
                                  reads=a1s + [("SZ", v, hi_)], writes=[("YG", e_)])
            P.barrier()
        with contextlib.ExitStack() as sC:
            PS = [cx.ps(sC, "PSc%d" % i, 512, F32) for i in range(8)]
            WO = kvw if kvw is not None else cx.sb(sC, "WO", 16 * 1024, BF16)
            gpb = cx.sb(sC, "gpb", 1024, F32)
            P.add("sp", DMA(gpb.a(0, [[1, 1024]]), DAP(gpost, 0, [[0, 128], [1, 1024]])), writes=["gpb"], dma=True)
            for q in range(4):
                P.add("pool", DMA(WO.a(q * 4096, [[1024, 4], [1, 1024]]),
                                  DAP(wout, q * 4 * 128 * 1024, [[1024, 128], [128 * 1024, 4], [1, 1024]])),
                      writes=[("WO", q)], dma=True)
            emit_out_phase(cx, sC, PS, YG, NT1, WO, hin, 0, gpb, out, ntiles=17, sfx="1")


def _consts_l0(first_chunk):
    c = np.zeros((128, CW), np.float32)
    c[:, 0:128] = np.eye(128, dtype=np.float32)
    for m in range(16):
        c[m + 16, 128 + m] = 1.0
        c[m, 128 + 16 + m] = 1.0
    k = np.arange(128)[:, None]
    q = np.arange(128)[None, :]
    c[:, 256:384] = (k >= q).astype(np.float32)
    c[:, 384:512] = (k <= q).astype(np.float32)
    c[:, 512:640] = 0.0 if first_chunk else 1.0
    invf = (np.float32(500000.0) ** (-np.arange(0, 32, 2, dtype=np.float32) / np.float32(32.0))).astype(np.float32)
    p = np.arange(128)
    c[:, 640] = invf[p % 16]
    c[:, 641] = np.where((p % 32) < 16, -1.0, 1.0)
    off = 644
    q16 = np.arange(16)[None, :]
    for pt in PARTS:
        d = pt["d"]
        for r in pt["R"]:
            c[:, off:off + 16] = ((q16 % d == r) & (k >= q16 // d)).astype(np.float32)
            off += 16
    assert off == 644 + 336
    k16 = np.arange(128)[:, None]
    for pt in PARTS:
        d = pt["d"]
        inR = np.isin(q16 % d, np.array(pt["R"]))
        c[:, off:off + 16] = ((k16 <= q16) & ((q16 - k16) % d == 0) & inR & (k16 < 16)).astype(np.float32)
        off += 16
    return c


def _consts_l1(b_grp, scale, chunk):
    c = np.zeros((128, 128 + 96), np.float32)
    c[:, 0:128] = np.eye(128, dtype=np.float32)
    c[:, 128:144] = b_grp.reshape(16, 128).T
    c[:, 144:160] = scale.reshape(16, 128).T
    t = np.arange(16) + chunk * 2048
    for g, w in enumerate(POOL_W):
        c[:, 160 + g * 16:160 + (g + 1) * 16] = (1.0 / np.minimum(t + 1, w)).astype(np.float32)[None, :]
    return c


_NC_CACHE = {}


def make_in_maps(x, positions, norm_pre, norm_post, attn_w_in, attn_w_out,
                 pool_w_in, pool_w_grp, pool_b_grp, pool_scale, pool_w_out, ncores=8):
    B, S, _ = x.shape
    cpb = ncores // B
    f = lambda a: np.ascontiguousarray(np.asarray(a, dtype=np.float32))
    shared = dict(
        gpre=f(norm_pre[0:1]), gpost=f(norm_post[0:1]), win=f(attn_w_in[0]), wout=f(attn_w_out[0]),
        gpre1=f(norm_pre[1:2]), gpost1=f(norm_post[1:2]), win1=f(pool_w_in[0]),
        wgrp=f(np.asarray(pool_w_grp[0]).reshape(2048, 512)), wout1=f(pool_w_out[0]))
    in_maps = []
    for core in range(ncores):
        b, ch = core // cpb, core % cpb
        s0 = ch * 2048
        xin = np.zeros((XTOK, D), np.float32)
        pos = np.zeros((1, XTOK), np.int32)
        xin[2048:4096] = x[b, s0:s0 + 2048]
        pos[0, 2048:4096] = positions[b, s0:s0 + 2048]
        if ch > 0:
            xin[:2048] = x[b, s0 - 2048:s0]
            pos[0, :2048] = positions[b, s0 - 2048:s0]
        if s0 + 2048 < S:
            xin[4096:4112] = x[b, s0 + 2048:s0 + 2064]
            pos[0, 4096:4112] = positions[b, s0 + 2048:s0 + 2064]
        m = dict(shared)
        m.update(xin=xin, pos=pos, cst=_consts_l0(ch == 0),
                 cst1=_consts_l1(np.asarray(pool_b_grp[0], np.float32), np.asarray(pool_scale[0], np.float32), ch))
        in_maps.append(m)
    return in_maps


def kernel(x, positions, norm_pre, norm_post, attn_w_in, attn_w_out,
           pool_w_in, pool_w_grp, pool_b_grp, pool_scale, pool_w_out):
    x = np.ascontiguousarray(np.asarray(x, dtype=np.float32))
    positions = np.asarray(positions).astype(np.int32)
    B, S, _ = x.shape
    ncores = 8
    cpb = ncores // B
    in_maps = make_in_maps(x, positions, norm_pre, norm_post, attn_w_in, attn_w_out,
                           pool_w_in, pool_w_grp, pool_b_grp, pool_scale, pool_w_out, ncores)
    if "f" not in _NC_CACHE:
        _NC_CACHE["f"] = build_fused()
    res = run_bass_kernel_spmd(_NC_CACHE["f"], in_maps, core_ids=list(range(ncores)))
    out = np.zeros((B, S, D), np.float32)
    for core in range(ncores):
        b, ch = core // cpb, core % cpb
        s0 = ch * 2048
        o = res.results[core]["out"]
        lo = 0 if ch == 0 else 16
        hi = min(2064, S - s0)
        out[b, s0 + lo:s0 + hi] = o[lo:hi]
    return out
```

```python
import contextlib
import os
import math
import numpy as np
import concourse.bass as bass
import concourse.mybir as mybir
from concourse.bass_utils import run_bass_kernel_spmd

F32 = mybir.dt.float32
BF16 = mybir.dt.bfloat16
I32 = mybir.dt.int32
AF = mybir.ActivationFunctionType
ALU = mybir.AluOpType
AX = mybir.AxisListType

ENGS = ("pe", "act", "dve", "pool", "sp")
DMAQ = ("act", "pool", "sp")
NDS = 8

D = 1024
E = 2048
T_OWN = 2048
EPS = 1e-6
TWO_PI = 2.0 * math.pi
INV2PI = 1.0 / TWO_PI
C1 = 6.28125
C2 = TWO_PI - C1
MAGIC = 12582912.0
PI_LO = 3.1415925
CW = 1044
XTOK = 4224
YTOK = 2176


class Op:
    __slots__ = ("eng", "fn", "deps", "dma", "signal", "sem", "val", "prev")

    def __init__(self, eng, fn, dma):
        self.eng = eng
        self.fn = fn
        self.dma = dma
        self.deps = []
        self.signal = False
        self.sem = None
        self.val = 0
        self.prev = None


class Prog:
    def __init__(self):
        self.ops = {e: [] for e in ENGS}
        self.lastw = {}
        self.readers = {}
        self.bar_idx = {e: 0 for e in ENGS}

    def add(self, eng, fn, reads=(), writes=(), dma=False):
        op = Op(eng, fn, dma)
        deps = {}
        psr = [r for r in reads if isinstance(r, tuple) and r[0] == "ps"]
        if psr:
            reads = [r for r in reads if not (isinstance(r, tuple) and r[0] == "ps")]
            writes = list(writes) + psr

        def need(d, raw):
            if d is None:
                return
            if (not dma) and (not d.dma) and d.eng == eng:
                if eng == "pe":
                    return
            deps[id(d)] = d

        for r in reads:
            need(self.lastw.get(r), True)
        for r in writes:
            need(self.lastw.get(r), False)
            rd = self.readers.get(r)
            if rd:
                for k, v in rd.items():
                    if k == "dma":
                        for d in v:
                            need(d, False)
                    else:
                        need(v, False)
        for d in deps.values():
            d.signal = True
        op.deps = list(deps.values())
        for r in reads:
            rd = self.readers.setdefault(r, {})
            if dma:
                rd.setdefault("dma", []).append(op)
            else:
                rd[eng] = op
        for r in writes:
            self.lastw[r] = op
            self.readers[r] = {}
        self.ops[eng].append(op)
        return op

    def barrier(self):
        lasts = []
        for e in ENGS:
            for op in reversed(self.ops[e]):
                if (not op.dma) and op.fn is not None:
                    lasts.append(op)
                    break
        dmas = [op for e in ENGS for op in self.ops[e][self.bar_idx[e]:] if op.dma]
        for e in ENGS:
            w = Op(e, None, False)
            w.deps = [d for d in lasts if d.eng != e] + dmas
            for d in w.deps:
                d.signal = True
            self.ops[e].append(w)
        self.bar_idx = {e: len(self.ops[e]) for e in ENGS}
        self.lastw.clear()
        self.readers.clear()

    def finalize(self, nc, stack):
        self.psem = {e: stack.enter_context(nc.semaphore("p_" + e)) for e in ENGS}
        self.dsem = {e: [stack.enter_context(nc.semaphore("d_%s%d" % (e, i))) for i in range(NDS)]
                     for e in DMAQ}
        self.semobj = {}
        for e in ENGS:
            self.semobj[("p", e)] = self.psem[e]
        for e in DMAQ:
            for i in range(NDS):
                self.semobj[("d", e, i)] = self.dsem[e][i]
        for e in ENGS:
            cnt = 0
            di = 0
            for op in self.ops[e]:
                if op.fn is None:
                    continue
                if op.dma:
                    op.sem = ("d", e, di % NDS)
                    op.val = 16 * (di // NDS + 1)
                    op.prev = (op.sem, op.val - 16) if di >= NDS else None
                    di += 1
                elif op.signal:
                    cnt += 1
                    op.sem = ("p", e)
                    op.val = cnt

    def emit(self, eng, e):
        waited = {}
        for op in self.ops[eng]:
            w = {}
            for d in op.deps:
                if w.get(d.sem, 0) < d.val:
                    w[d.sem] = d.val
            if op.dma and op.prev is not None:
                if w.get(op.prev[0], 0) < op.prev[1]:
                    w[op.prev[0]] = op.prev[1]
            for sem, val in w.items():
                if waited.get(sem, 0) < val:
                    e.wait_ge(self.semobj[sem], val)
                    waited[sem] = val
            if op.fn is not None:
                ins = op.fn(e)
                if op.dma:
                    ins.then_inc(self.semobj[op.sem], 16)
                elif op.signal:
                    ins.then_inc(self.semobj[op.sem], 1)

    def run(self, nc, stack):
        self.barrier()
        self.finalize(nc, stack)
        block = stack.enter_context(nc.Block())

        @block.tensor
        def _(e):
            self.emit("pe", e)

        @block.scalar
        def _(e):
            self.emit("act", e)

        @block.vector
        def _(e):
            self.emit("dve", e)

        @block.gpsimd
        def _(e):
            self.emit("pool", e)

        @block.sync
        def _(e):
            self.emit("sp", e)


class TT:
    def __init__(self, t, pitch, base=0):
        self.t = t
        self.pitch = pitch
        self.base = base

    def a(self, off, dims, p0=0, np_=128):
        return bass.AP(self.t, p0 * self.pitch + self.base + off, [[self.pitch, np_]] + [list(d) for d in dims])


def DAP(t, off, dims):
    return bass.AP(t, off, [list(d) for d in dims])


def MM(out, lhsT, rhs, start, stop):
    return lambda e: e.matmul(out, lhsT=lhsT, rhs=rhs, start=start, stop=stop)


def TR(out, in_, ident):
    return lambda e: e.transpose(out, in_, ident)


def ACT(out, in_, func, scale=None, bias=None):
    kw = {}
    if scale is not None:
        kw["scale"] = scale
    if bias is not None:
        kw["bias"] = bias
    return lambda e: e.activation(out=out, in_=in_, func=func, **kw)


def TTO(out, in0, in1, op):
    return lambda e: e.tensor_tensor(out=out, in0=in0, in1=in1, op=op)


def TS(out, in0, s1, op0, s2=None, op1=None):
    if op1 is None:
        return lambda e: e.tensor_scalar(out=out, in0=in0, scalar1=s1, scalar2=None, op0=op0)
    return lambda e: e.tensor_scalar(out=out, in0=in0, scalar1=s1, scalar2=s2, op0=op0, op1=op1)


def STT(out, in0, scalar, in1, op0, op1):
    return lambda e: e.scalar_tensor_tensor(out=out, in0=in0, scalar=scalar, in1=in1, op0=op0, op1=op1)


def CP(out, in_):
    return lambda e: e.tensor_copy(out=out, in_=in_)


def ACP(out, in_):
    return lambda e: e.activation(out=out, in_=in_, func=AF.Copy)


def RECIP(out, in_):
    return lambda e: e.reciprocal(out=out, in_=in_)


def RECIPF(out, in_):
    return lambda e: e.reciprocal_approx_fast(out=out, in_=in_)


def RSUM(out, in_):
    return lambda e: e.reduce_sum(out=out, in_=in_, axis=AX.X)


def MSET(ap, c):
    return lambda e: e.memset(ap, c)


def DMA(out, in_):
    return lambda e: e.dma_start(out=out, in_=in_)


class Ctx:
    def __init__(self, nc):
        self.nc = nc
        self.P = Prog()

    def sb(self, stack, name, free, dt):
        self.n = getattr(self, "n", 0) + 1
        t = stack.enter_context(self.nc.sbuf_tensor("%s_%d" % (name, self.n), [128, free], dt))
        return TT(t, free)

    def ps(self, stack, name, free, dt):
        self.n = getattr(self, "n", 0) + 1
        t = stack.enter_context(self.nc.psum_tensor("%s_%d" % (name, self.n), [128, free], dt))
        return TT(t, free)


def emit_norm_transpose(cx, stack, xsrc, ntiles, xnT, xnT_tok, gb, identb, row0=0, keep=None, hook=None):
    P = cx.P
    NS = 4
    XT = [cx.sb(stack, "XT%d" % i, 1024, F32) for i in range(NS)]
    SQ = [cx.sb(stack, "SQ%d" % i, 1024, F32) for i in range(2)]
    XNB = [cx.sb(stack, "XNB%d" % i, 1024, BF16) for i in range(NS)]
    ST = cx.sb(stack, "STn", 3 * 64, F32)
    TP = [cx.ps(stack, "TP%d" % i, 1024, BF16) for i in range(NS)]

    def stage_a(tt):
        s, s2 = tt % NS, tt % 2
        P.add("sp", DMA(XT[s].a(0, [[1, 1024]]), DAP(xsrc, (row0 + tt * 128) * 1024, [[1024, 128], [1, 1024]])),
              writes=[("XT", s)], dma=True)
        P.add("act", ACT(SQ[s2].a(0, [[1, 1024]]), XT[s].a(0, [[1, 1024]]), AF.Square),
              reads=[("XT", s)], writes=[("SQ", s2)])
        P.add("dve", RSUM(ST.a(tt, [[1, 1]]), SQ[s2].a(0, [[1, 1024]])), reads=[("SQ", s2)], writes=[("st0", tt)])

    def stage_b(tt):
        s = tt % NS
        P.add("act", ACT(ST.a(64 + tt, [[1, 1]]), ST.a(tt, [[1, 1]]), AF.Sqrt, scale=1.0 / D, bias=EPS),
              reads=[("st0", tt)], writes=[("st1", tt)])
        P.add("dve", RECIP(ST.a(128 + tt, [[1, 1]]), ST.a(64 + tt, [[1, 1]])), reads=[("st1", tt)], writes=[("st2", tt)])
        P.add("dve", STT(XNB[s].a(0, [[1, 1024]]), XT[s].a(0, [[1, 1024]]), ST.a(128 + tt, [[1, 1]]),
                         gb.a(0, [[1, 1024]]), ALU.mult, ALU.mult),
              reads=[("XT", s), ("st2", tt), "gb"], writes=[("XNB", s)])

    def stage_c(tt):
        s = tt % NS
        for c in range(8):
            P.add("pe", TR(TP[s].a(c * 128, [[1, 128]]), XNB[s].a(c * 128, [[1, 128]]), identb.a(0, [[1, 128]])),
                  reads=[("XNB", s), "identb"], writes=[("TP", s)])
        P.add("act", ACP(xnT.a(tt * 128, [[xnT_tok, 8], [1, 128]]), TP[s].a(0, [[128, 8], [1, 128]])),
              reads=[("TP", s)], writes=["xnT"])

    for it in range(ntiles + 2):
        if it < ntiles:
            stage_a(it)
        if 0 <= it - 1 < ntiles:
            stage_b(it - 1)
        if 0 <= it - 2 < ntiles:
            stage_c(it - 2)
        if hook is not None:
            hook(it)


def emit_out_phase(cx, stack, PS, YG, yg_tok, WO, xres, xres_row0, gpb, outd, ntiles=16, sfx="", yg_res=None):
    P = cx.P
    NX = 4
    XR = [cx.sb(stack, "XR%d" % i, 1024, F32) for i in range(NX)]
    SQ = [cx.sb(stack, "SQo%d" % i, 1024, F32) for i in range(2)]
    TO = [cx.sb(stack, "TO%d" % i, 1024, F32) for i in range(2)]
    ST = cx.sb(stack, "STo", 3 * 32, F32)
    for tt in range(ntiles):
        s = tt % 2
        sx = tt % NX
        P.add("sp", DMA(XR[sx].a(0, [[1, 1024]]), DAP(xres, (xres_row0 + tt * 128) * 1024, [[1024, 128], [1, 1024]])),
              writes=[("XR", sx)], dma=True)
        banks = [(2 * tt) % 8, (2 * tt + 1) % 8]
        for hh in range(2):
            b = banks[hh]
            for ec in range(16):
                P.add("pe", MM(PS[b].a(0, [[1, 512]]), YG.a(ec * yg_tok + tt * 128, [[1, 128]]),
                               WO.a(ec * 1024 + hh * 512, [[1, 512]]), ec == 0, ec == 15),
                      reads=(yg_res(ec, tt) if yg_res else [("YG", ec)]) + [("WO", ec // 4)], writes=[("ps", b)])
            P.add("act", ACT(SQ[s].a(hh * 512, [[1, 512]]), PS[b].a(0, [[1, 512]]), AF.Square),
                  reads=[("ps", b)], writes=[("SQo", s, hh)])
        P.add("dve", RSUM(ST.a(tt, [[1, 1]]), SQ[s].a(0, [[1, 1024]])),
              reads=[("SQo", s, 0), ("SQo", s, 1)], writes=[("so0", tt)])
        P.add("act", ACT(ST.a(32 + tt, [[1, 1]]), ST.a(tt, [[1, 1]]), AF.Sqrt, scale=1.0 / D, bias=EPS),
              reads=[("so0", tt)], writes=[("so1", tt)])
        P.add("dve", RECIP(ST.a(64 + tt, [[1, 1]]), ST.a(32 + tt, [[1, 1]])), reads=[("so1", tt)], writes=[("so2", tt)])
        for hh in range(2):
            b = banks[hh]
            P.add("dve", STT(TO[s].a(hh * 512, [[1, 512]]), PS[b].a(0, [[1, 512]]), ST.a(64 + tt, [[1, 1]]),
                             gpb.a(hh * 512, [[1, 512]]), ALU.mult, ALU.mult),
                  reads=[("ps", b), ("so2", tt), "gpb"], writes=[("TO", s, hh)])
        P.add("pool", TTO(XR[sx].a(0, [[1, 1024]]), XR[sx].a(0, [[1, 1024]]), TO[s].a(0, [[1, 1024]]), ALU.add),
              reads=[("XR", sx), ("TO", s, 0), ("TO", s, 1)], writes=[("XR", sx)])
        P.add("sp", DMA(DAP(outd, tt * 128 * 1024, [[1024, 128], [1, 1024]]), XR[sx].a(0, [[1, 1024]])),
              reads=[("XR", sx)], dma=True)


PARTS = [
    dict(g=0, d=1, R=[0], nkb=17),
    dict(g=1, d=4, R=[0, 1, 2, 3], nkb=5),
    dict(g=2, d=16, R=list(range(0, 16)), nkb=2),
]


def part_blocks(pt):
    d, nkb = pt["d"], pt["nkb"]
    nqb = nkb - 1
    kb0 = 16 // d - 1
    ktok0 = 2048 - 128 * d
    nk = 4096 - ktok0
    kblocks = [(c0, min(512, nk - c0), ktok0 + c0) for c0 in range(0, nk, 512)]
    qblocks = [(nb, 512, 2048 + 512 * nb) for nb in range(4)]
    return kblocks, qblocks, nqb, kb0


def cdims(dims):
    if len(dims) == 1:
        return [[1, dims[0][1]]]
    n1, n2 = dims[0][1], dims[1][1]
    return [[n2, n1], [1, n2]]


def build_fused():
    nc = bass.Bass("TRN2", target_bir_lowering=False)
    stop = None
    lim = [8, 4, 3]
    xin = nc.dram_tensor("xin", [XTOK, 1024], F32, kind="ExternalInput")
    pos = nc.dram_tensor("pos", [1, XTOK], I32, kind="ExternalInput")
    gpre = nc.dram_tensor("gpre", [1, 1024], F32, kind="ExternalInput")
    gpost = nc.dram_tensor("gpost", [1, 1024], F32, kind="ExternalInput")
    win = nc.dram_tensor("win", [1024, 20480], F32, kind="ExternalInput")
    wout = nc.dram_tensor("wout", [2048, 1024], F32, kind="ExternalInput")
    cst = nc.dram_tensor("cst", [128, CW], F32, kind="ExternalInput")
    gpre1 = nc.dram_tensor("gpre1", [1, 1024], F32, kind="ExternalInput")
    gpost1 = nc.dram_tensor("gpost1", [1, 1024], F32, kind="ExternalInput")
    win1 = nc.dram_tensor("win1", [1024, 4096], F32, kind="ExternalInput")
    wgrp = nc.dram_tensor("wgrp", [2048, 512], F32, kind="ExternalInput")
    wout1 = nc.dram_tensor("wout1", [2048, 1024], F32, kind="ExternalInput")
    cst1 = nc.dram_tensor("cst1", [128, 128 + 96], F32, kind="ExternalInput")
    ygd = nc.dram_tensor("ygd", [16, 128, YTOK], BF16, kind="Internal")
    h1 = nc.dram_tensor("h1d", [YTOK, 1024], F32, kind="Internal")
    outd = nc.dram_tensor("out", [YTOK, 1024], F32, kind="ExternalOutput")

    cx = Ctx(nc)
    P = cx.P
    scale_qk = 1.0 / math.sqrt(128.0)

    with contextlib.ExitStack() as s0:
        identb = cx.sb(s0, "identb", 128, BF16)
        permb = cx.sb(s0, "permb", 128, BF16)
        maskb = cx.sb(s0, "maskb", 256, BF16)
        onesH = cx.sb(s0, "onesH", 128, BF16)
        onesb = cx.sb(s0, "onesb", 128, BF16)
        cf = cx.sb(s0, "cf", 4, F32)
        MX = cx.sb(s0, "MX", 400, BF16)
        KVW = cx.sb(s0, "KVW", 16 * 1024, BF16)
        P.add("pool", DMA(MX.a(0, [[1, 400]]), DAP(cst, 644, [[CW, 128], [1, 400]])), writes=["MX"], dma=True)
        P.add("pool", DMA(identb.a(0, [[1, 128]]), DAP(cst, 0, [[CW, 128], [1, 128]])), writes=["identb"], dma=True)
        P.add("pool", DMA(permb.a(0, [[1, 128]]), DAP(cst, 128, [[CW, 128], [1, 128]])), writes=["permb"], dma=True)
        P.add("pool", DMA(maskb.a(0, [[1, 256]]), DAP(cst, 256, [[CW, 128], [1, 256]])), writes=["maskb"], dma=True)
        P.add("pool", DMA(onesH.a(0, [[1, 128]]), DAP(cst, 512, [[CW, 128], [1, 128]])), writes=["onesH"], dma=True)
        P.add("sp", DMA(cf.a(0, [[1, 4]]), DAP(cst, 640, [[CW, 128], [1, 4]])), writes=["cf"], dma=True)
        P.add("dve", MSET(onesb.a(0, [[1, 128]]), 1.0), writes=["onesb"])

        with contextlib.ExitStack() as s1:
            xnT = cx.sb(s1, "xnT", 8 * XTOK, BF16)
            cosb = cx.sb(s1, "cosb", XTOK, BF16)
            sinb = cx.sb(s1, "sinb", XTOK, BF16)

            with contextlib.ExitStack() as sA:
                gb = cx.sb(sA, "gb", 1024, F32)
                P.add("sp", DMA(gb.a(0, [[1, 1024]]), DAP(gpre, 0, [[0, 128], [1, 1024]])), writes=["gb"], dma=True)
                posi = cx.sb(sA, "posi", 1024, I32)
                tA = [cx.sb(sA, "tA%d" % i, 1024, F32) for i in range(5)]
                R32 = dict(p0=0, np_=32)
                def cs_chunk(q):
                    c0 = q * 1024
                    full = [[1, 1024 if q < 4 else 128]]
                    P.add("sp", DMA(posi.a(0, full, **R32), DAP(pos, c0, [[0, 32], full[0]])), writes=["posi"], dma=True)
                    ang, a1, a2, a3 = tA[0], tA[1], tA[2], tA[3]
                    P.add("dve", CP(a1.a(0, full, **R32), posi.a(0, full, **R32)), reads=["posi"], writes=["a1"])
                    P.add("dve", TS(ang.a(0, full, **R32), a1.a(0, full, **R32), cf.a(0, [[1, 1]], **R32), ALU.mult),
                          reads=["a1", "cf"], writes=["ang"])
                    P.add("dve", TS(a1.a(0, full, **R32), ang.a(0, full, **R32), INV2PI, ALU.mult), reads=["ang"], writes=["a1"])
                    P.add("dve", TS(a2.a(0, full, **R32), a1.a(0, full, **R32), MAGIC, ALU.add), reads=["a1"], writes=["a2"])
                    P.add("dve", TS(a1.a(0, full, **R32), a2.a(0, full, **R32), MAGIC, ALU.subtract), reads=["a2"], writes=["a1"])
                    P.add("dve", STT(a2.a(0, full, **R32), a1.a(0, full, **R32), -C1, ang.a(0, full, **R32), ALU.mult, ALU.add),
                          reads=["a1", "ang"], writes=["a2"])
                    P.add("dve", STT(a3.a(0, full, **R32), a1.a(0, full, **R32), -C2, a2.a(0, full, **R32), ALU.mult, ALU.add),
                          reads=["a1", "a2"], writes=["a3"])
                    P.add("dve", TS(a2.a(0, full, **R32), a3.a(0, full, **R32), -PI_LO, ALU.max, PI_LO, ALU.min),
                          reads=["a3"], writes=["a2"])
                    P.add("act", ACT(sinb.a(c0, full, **R32), a2.a(0, full, **R32), AF.Sin, scale=cf.a(1, [[1, 1]], **R32)),
                          reads=["a2", "cf"], writes=["sinb"])
                    P.add("dve", TS(a1.a(0, full, **R32), ang.a(0, full, **R32), INV2PI, ALU.mult, 0.25, ALU.add),
                          reads=["ang"], writes=["a1"])
                    P.add("dve", TS(a3.a(0, full, **R32), a1.a(0, full, **R32), MAGIC, ALU.add), reads=["a1"], writes=["a3"])
                    P.add("dve", TS(a1.a(0, full, **R32), a3.a(0, full, **R32), MAGIC, ALU.subtract), reads=["a3"], writes=["a1"])
                    a4 = tA[4]
                    P.add("dve", STT(a3.a(0, full, **R32), a1.a(0, full, **R32), -C1, ang.a(0, full, **R32), ALU.mult, ALU.add),
                          reads=["a1", "ang"], writes=["a3"])
                    P.add("dve", STT(a4.a(0, full, **R32), a1.a(0, full, **R32), -C2, a3.a(0, full, **R32), ALU.mult, ALU.add),
                          reads=["a1", "a3"], writes=["a4"])
                    P.add("dve", TS(a3.a(0, full, **R32), a4.a(0, full, **R32), 0.5 * math.pi, ALU.add), reads=["a4"], writes=["a3"])
                    P.add("dve", TS(a4.a(0, full, **R32), a3.a(0, full, **R32), -PI_LO, ALU.max, PI_LO, ALU.min),
                          reads=["a3"], writes=["a4"])
                    P.add("act", ACT(cosb.a(c0, full, **R32), a4.a(0, full, **R32), AF.Sin), reads=["a4"], writes=["cosb"])
                cs_at = {2: 0, 8: 1, 14: 2, 20: 3, 26: 4}
                emit_norm_transpose(cx, sA, xin, 33, xnT, XTOK, gb, identb,
                                    hook=lambda it: cs_chunk(cs_at[it]) if it in cs_at else None)
            P.barrier()

            with contextlib.ExitStack() as sB:
                PS = [cx.ps(sB, "PS%d" % i, 512, F32) for i in range(8)]
                QT = cx.sb(sB, "QT", 2 * 2048, BF16)
                KT = TT(KVW.t, 16 * 1024, 0)
                V = TT(KVW.t, 16 * 1024, 8192)
                ND = cx.sb(sB, "ND", 4 * 2048, F32)
                NDb = TT(ND.t.bitcast(BF16), 16384)
                WQ = [cx.sb(sB, "WQ%d" % i, 2048, BF16) for i in range(2)]
                WK = [cx.sb(sB, "WK%d" % i, 2048, BF16) for i in range(2)]
                WV = [cx.sb(sB, "WV%d" % i, 2048, BF16) for i in range(2)]
                WZ = cx.sb(sB, "WZ", 2048, BF16)
                T1 = [cx.sb(sB, "T1%d" % i, 512, F32) for i in range(3)]
                T2 = [cx.sb(sB, "T2%d" % i, 512, F32) for i in range(2)]
                PT = [cx.sb(sB, "PT%d" % i, 512, BF16) for i in range(4)]
                mask4 = cx.sb(sB, "mask4", 512, BF16)
                for q_ in range(4):
                    P.add("pool", TS(mask4.a(q_ * 128, [[1, 128]]), maskb.a((q_ // 2) * 128, [[1, 128]]), -1.0, ALU.add,
                                     30000.0, ALU.mult),
                          reads=["maskb"], writes=["mask4"])
                SZ = [cx.sb(sB, "SZ%d" % i, 512, F32) for i in range(2)]
                QX = cx.sb(sB, "QX", 32, BF16)
                KX = cx.sb(sB, "KX", 256, BF16)
                VX = cx.sb(sB, "VX", 256, BF16)
                NDX = cx.sb(sB, "NDX", 64, F32)
                PTX = cx.sb(sB, "PTX", 256, BF16)
                PTC = cx.sb(sB, "PTC", 16, BF16)
                YGX = cx.sb(sB, "YGX", 32, BF16)
                SZX = cx.sb(sB, "SZX", 16, F32)
                MXP_OFF = [0, 16, 80]

                def wload(Wt, name, colbase):
                    P.add("pool", DMA(Wt.a(0, [[256, 8], [1, 256]]),
                                      DAP(win, colbase, [[20480, 128], [128 * 20480, 8], [1, 256]])),
                          writes=[name], dma=True)

                def load_group_weights(bt, g, par):
                    base = g * 6144 + bt * 256
                    wload(WK[par], ("WK", par), base + 2048)
                    wload(WV[par], ("WV", par), base + 4096)
                    wload(WQ[par], ("WQ", par), base)

                cnt = dict(a=0, b=0, v=0, t=0, t2=0, s=0, o=0, p=0, z=0)
                nphase = 0
                load_group_weights(0, 0, 0)

                class Pipe:
                    def __init__(self, depth):
                        self.q = []
                        self.depth = depth

                    def push(self, fn):
                        self.q.append(fn)
                        while len(self.q) > self.depth:
                            self.q.pop(0)()

                    def flush(self):
                        while self.q:
                            self.q.pop(0)()

                ppipe = Pipe(1)

                def proj_block(Wt, wname, j, dst, dres, dbase, n, tok_off, d=1, dpitch=0):
                    ba = cnt["a"] % 4
                    cnt["a"] += 1
                    t1s = cnt["t"] % 3
                    cnt["t"] += 1
                    if d == 1:
                        nat = [[1, n]]
                        ri_ = [[1, n]]
                        con = [[1, n]]
                        dap = [[1, n]]
                    else:
                        nat = [[1, n]]
                        ri_ = [[1, d], [d, n // d]]
                        con = [[n // d, d], [1, n // d]]
                        dap = [[dpitch, d], [1, n // d]]
                    for c in range(8):
                        P.add("pe", MM(PS[ba].a(0, nat), Wt.a(c * 256 + j * 128, [[1, 128]]),
                                       xnT.a(c * XTOK + tok_off, nat), c == 0, c == 7),
                              reads=[wname, "xnT"], writes=[("ps", ba)])
                    P.add("act", ACP(dst.a(dbase, dap), PS[ba].a(0, ri_)), reads=[("ps", ba)], writes=dres)
                    ppipe.flush()
                    P.add("dve", TTO(T1[t1s].a(0, con, p0=0, np_=32), PS[ba].a(0, ri_, p0=0, np_=32),
                                     cosb.a(tok_off, ri_, p0=0, np_=32), ALU.mult),
                          reads=[("ps", ba), "cosb"], writes=[("T1", t1s)])

                    def stage2():
                        bb = 4 + cnt["b"] % 2
                        cnt["b"] += 1
                        t2s = cnt["t2"] % 2
                        cnt["t2"] += 1
                        P.add("pe", MM(PS[bb].a(0, con), permb.a(0, [[1, 128]]), dst.a(dbase, dap), True, True),
                              reads=dres + ["permb"], writes=[("ps", bb)])
                        P.add("dve", TTO(T2[t2s].a(0, con, p0=0, np_=32), PS[bb].a(0, con, p0=0, np_=32),
                                         sinb.a(tok_off, ri_, p0=0, np_=32), ALU.mult),
                              reads=[("ps", bb), "sinb"], writes=[("T2", t2s)])
                        P.add("pool", TTO(dst.a(dbase, dap, p0=0, np_=32), T1[t1s].a(0, con, p0=0, np_=32),
                                          T2[t2s].a(0, con, p0=0, np_=32), ALU.add),
                              reads=[("T1", t1s), ("T2", t2s)], writes=dres)

                    ppipe.push(stage2)

                for bt in range(lim[0]):
                    for pi, pt in enumerate(PARTS[:lim[1]]):
                        g, d, R, nkb = pt["g"], pt["d"], pt["R"], pt["nkb"]
                        kblocks, qblocks, nqb, kb0 = part_blocks(pt)
                        par = nphase % 2
                        nphase += 1
                        if pi < 2:
                            load_group_weights(bt, g + 1, nphase % 2)
                        elif bt < 7:
                            load_group_weights(bt + 1, 0, nphase % 2)
                        if pi == 0:
                            wload(WZ, "WZ", 18432 + bt * 256)
                        for j in range(2):
                            for (nb, n, tok_off) in qblocks:
                                if d == 1:
                                    qres_w = [("QT", j, b_, q_) for b_ in range(4 * nb, 4 * nb + 4) for q_ in range(4)]
                                elif d == 4:
                                    qres_w = [("QT", j, r_ * 4 + nb, q_) for r_ in range(4) for q_ in range(4)]
                                else:
                                    qres_w = [("QT", j, r_, nb) for r_ in range(16)]
                                proj_block(WQ[par], ("WQ", par), j, QT, qres_w,
                                           j * 2048 + (512 * nb) // d, n, tok_off, d=d, dpitch=2048 // d)
                        for j in range(2):
                            for (c0, n, tok_off) in kblocks:
                                blk = c0 // 512
                                if d == 1:
                                    kres_w = [("KT", j, c0 // 128 + b_, q_) for b_ in range(n // 128) for q_ in range(4)]
                                elif d == 4:
                                    kres_w = [("KT", j, r_ * nkb + blk, q_) for r_ in range(4) for q_ in range(4)]
                                else:
                                    kres_w = [("KT", j, r_ * nkb + blk // 4, blk % 4) for r_ in range(16)]
                                proj_block(WK[par], ("WK", par), j, KT, kres_w,
                                           j * 4096 + c0 // d, n, tok_off, d=d, dpitch=128 * nkb)
                        nvt = len(R) * nkb
                        for vt in range(nvt):
                            ri, kb = vt // nkb, vt % nkb
                            tok0 = R[ri] + d * 128 * (kb0 + kb)
                            if vt % 2 == 0:
                                bv = 6 + cnt["v"] % 2
                                cnt["v"] += 1
                            for c in range(8):
                                P.add("pe", MM(PS[bv].a((vt % 2) * 256, [[1, 256]]), xnT.a(c * XTOK + tok0, [[d, 128]]),
                                               WV[par].a(c * 256, [[1, 256]]), c == 0, c == 7),
                                      reads=["xnT", ("WV", par)], writes=[("ps", bv)])
                            if vt % 2 == 1 or vt == nvt - 1:
                                v0 = vt - (vt % 2)
                                nn = vt - v0 + 1
                                P.add("act", ACP(V.a(v0 * 256, [[1, nn * 256]]), PS[bv].a(0, [[1, nn * 256]])),
                                      reads=[("ps", bv)], writes=[("V", v_) for v_ in range(v0, v0 + nn)])
                        ppipe.flush()
                        apipe = Pipe(2)
                        for ri in range(len(R) if lim[2] >= 2 else 0):
                            for qb in range(nqb):
                                qs = (ri * nqb + qb) * 128
                                kprev = ri * nkb + qb
                                kcur = kprev + 1
                                bs = cnt["s"] % 4
                                cnt["s"] += 1
                                pslot = cnt["p"] % 4
                                cnt["p"] += 1
                                P.add("pe", MM(PS[bs].a(0, [[1, 512]]), identb.a(0, [[1, 128]]), mask4.a(0, [[1, 512]]), True, False),
                                      reads=["identb", "mask4"], writes=[("ps", bs)])
                                for hh, kq in enumerate((qb, qb + 1)):
                                    for j in range(2):
                                        qres = [("QT", j, qs // 128, q_) for q_ in range(4)]
                                        kres = [("KT", j, ri * nkb + kq, q_) for q_ in range(4)]
                                        P.add("pe", MM(PS[bs].a(hh * 256 + j * 128, [[1, 128]]),
                                                       KT.a(j * 4096 + (ri * nkb + kq) * 128, [[1, 128]]),
                                                       QT.a(j * 2048 + qs, [[1, 128]]), False, hh == 1 and j == 1),
                                              reads=qres + kres, writes=[("ps", bs)])
                                P.add("act", ACT(PT[pslot].a(0, [[1, 512]]), PS[bs].a(0, [[1, 512]]), AF.Exp, scale=scale_qk),
                                      reads=[("ps", bs)], writes=[("PT", pslot)])

                                def stage2(ri=ri, qb=qb, kprev=kprev, kcur=kcur, pslot=pslot):
                                    bo = 4 + cnt["o"] % 4
                                    cnt["o"] += 1
                                    for j in range(2):
                                        for hh, kbk in enumerate((kprev, kcur)):
                                            P.add("pe", MM(PS[bo].a(j * 128, [[1, 128]]), V.a(kbk * 256 + j * 128, [[1, 128]]),
                                                           PT[pslot].a(hh * 256 + j * 128, [[1, 128]]), hh == 0, hh == 1),
                                                  reads=[("V", kbk), ("PT", pslot)], writes=[("ps", bo)])
                                    for hh in range(2):
                                        ones_t, ones_n = (onesH, "onesH") if (hh == 0 and qb == 0) else (onesb, "onesb")
                                        P.add("pe", MM(PS[bo].a(256, [[1, 256]]), ones_t.a(0, [[1, 128]]),
                                                       PT[pslot].a(hh * 256, [[1, 256]]), hh == 0, hh == 1),
                                              reads=[ones_n, ("PT", pslot)], writes=[("ps", bo)])
                                    tq0 = R[ri] + d * 128 * (16 // d + qb) - 2048
                                    ndres = [("ND", j, b_) for j in range(2) for b_ in range(tq0 // 128, tq0 // 128 + d)]
                                    ndap = ND.a(tq0, [[4096, 2], [2048, 2], [d, 128]])
                                    psap = PS[bo].a(0, [[256, 2], [128, 2], [1, 128]])
                                    if pi == 0:
                                        P.add("dve", CP(ndap, psap), reads=[("ps", bo)], writes=ndres)
                                    else:
                                        P.add("dve", TTO(ndap, psap, ndap, ALU.add),
                                              reads=[("ps", bo)] + ndres, writes=ndres)

                                apipe.push(stage2)
                        apipe.flush()
                        if True:
                            for j in range(2):
                                proj_block(WQ[par], ("WQ", par), j, QX, [("QX", j, 0)], j * 16, 16, 4096)
                                proj_block(WK[par], ("WK", par), j, KX, [("KX", j, 0)], j * 128, 128, 4096)
                            bv = 6 + cnt["v"] % 2
                            cnt["v"] += 1
                            for c in range(8):
                                P.add("pe", MM(PS[bv].a(0, [[1, 256]]), xnT.a(c * XTOK + 4096, [[1, 128]]),
                                               WV[par].a(c * 256, [[1, 256]]), c == 0, c == 7),
                                      reads=["xnT", ("WV", par)], writes=[("ps", bv)])
                            P.add("act", ACP(VX.a(0, [[1, 256]]), PS[bv].a(0, [[1, 256]])),
                                  reads=[("ps", bv)], writes=["VX"])
                            ppipe.flush()
                        nR = len(R)
                        for j in range(2):
                            bs = cnt["s"] % 4
                            cnt["s"] += 1
                            bo = 4 + cnt["o"] % 4
                            cnt["o"] += 1
                            for ri in range(nR):
                                kq = nkb - 1
                                kres = [("KT", j, ri * nkb + kq, q_) for q_ in range(4)]
                                P.add("pe", MM(PS[bs].a(ri * 16, [[1, 16]]), KT.a(j * 4096 + (ri * nkb + kq) * 128, [[1, 128]]),
                                               QX.a(j * 16, [[1, 16]]), True, True),
                                      reads=[("QX", j, 0)] + kres, writes=[("ps", bs)])
                            P.add("pe", MM(PS[bs].a(256, [[1, 16]]), KX.a(j * 128, [[1, 128]]),
                                           QX.a(j * 16, [[1, 16]]), True, True),
                                  reads=[("QX", j, 0), ("KX", j, 0)], writes=[("ps", bs)])
                            P.add("act", ACT(PTX.a(0, [[1, nR * 16]]), PS[bs].a(0, [[1, nR * 16]]), AF.Exp, scale=scale_qk),
                                  reads=[("ps", bs)], writes=["PTX"])
                            P.add("act", ACT(PTC.a(0, [[1, 16]]), PS[bs].a(256, [[1, 16]]),
                                             AF.Exp, scale=scale_qk),
                                  reads=[("ps", bs)], writes=["PTC"])
                            P.add("pool", TTO(PTX.a(0, [[1, nR * 16]]), PTX.a(0, [[1, nR * 16]]),
                                              MX.a(MXP_OFF[pi], [[1, nR * 16]]), ALU.mult),
                                  reads=["PTX", "MX"], writes=["PTX"])
                            P.add("pool", TTO(PTC.a(0, [[1, 16]]), PTC.a(0, [[1, 16]]),
                                              MX.a(336 + pi * 16, [[1, 16]]), ALU.mult),
                                  reads=["PTC", "MX"], writes=["PTC"])
                            for which in range(2):
                                for ri in range(nR):
                                    kbl = ri * nkb + nkb - 1
                                    lhs = V.a(kbl * 256 + j * 128, [[1, 128]]) if which == 0 else onesb.a(0, [[1, 128]])
                                    P.add("pe", MM(PS[bo].a(which * 16, [[1, 16]]), lhs, PTX.a(ri * 16, [[1, 16]]),
                                                   ri == 0, False),
                                          reads=[("V", kbl), "PTX", "onesb"], writes=[("ps", bo)])
                                lhs = VX.a(j * 128, [[1, 128]]) if which == 0 else onesb.a(0, [[1, 128]])
                                P.add("pe", MM(PS[bo].a(which * 16, [[1, 16]]), lhs, PTC.a(0, [[1, 16]]),
                                               False, True),
                                      reads=["VX", "PTC", "onesb"], writes=[("ps", bo)])
                            ndx = NDX.a(j * 16, [[32, 2], [1, 16]])
                            if pi == 0:
                                P.add("dve", CP(ndx, PS[bo].a(0, [[16, 2], [1, 16]])), reads=[("ps", bo)], writes=[("NDX", j)])
                            else:
                                P.add("dve", TTO(ndx, PS[bo].a(0, [[16, 2], [1, 16]]), ndx, ALU.add),
                                      reads=[("ps", bo), ("NDX", j)], writes=[("NDX", j)])
                    if bt == lim[0] - 1:
                        kv_all = ([("KT", j_, b_, q_) for j_ in range(2) for b_ in range(32) for q_ in range(4)]
                                  + [("V", v_) for v_ in range(32)])
                        for q in range(4):
                            P.add("pool", DMA(KVW.a(q * 4096, [[1024, 4], [1, 1024]]),
                                              DAP(wout, q * 4 * 128 * 1024, [[1024, 128], [128 * 1024, 4], [1, 1024]])),
                                  writes=kv_all + [("WO", q)], dma=True)
                    if lim[2] < 3:
                        continue
                    for j in range(2):
                        for hq in range(2):
                            r_ = [("ND", j, b_) for b_ in range(hq * 8, hq * 8 + 8)]
                            apx = ND.a(4096 + j * 2048 + hq * 1024, [[1, 1024]])
                            P.add("act", ACT(apx, apx, AF.Ln), reads=r_, writes=r_)
                            P.add("act", ACT(apx, apx, AF.Exp, scale=-1.0), reads=r_, writes=r_)
                    for j in range(2):
                        for tb in range(4):
                            bz = cnt["z"] % 3
                            cnt["z"] += 1
                            zs = cnt["z"] % 2
                            for c in range(8):
                                P.add("pe", MM(PS[bz].a(0, [[1, 512]]), WZ.a(c * 256 + j * 128, [[1, 128]]),
                                               xnT.a(c * XTOK + 2048 + tb * 512, [[1, 512]]), c == 0, c == 7),
                                      reads=["WZ", "xnT"], writes=[("ps", bz)])
                            P.add("act", ACT(SZ[zs].a(0, [[1, 512]]), PS[bz].a(0, [[1, 512]]), AF.Silu),
                                  reads=[("ps", bz)], writes=[("SZ", zs)])
                            r_ = [("ND", j, b_) for b_ in range(tb * 4, tb * 4 + 4)]
                            nap = ND.a(j * 2048 + tb * 512, [[1, 512]])
                            dap_ = ND.a(4096 + j * 2048 + tb * 512, [[1, 512]])
                            P.add("dve", TTO(nap, nap, dap_, ALU.mult), reads=r_, writes=r_)
                            P.add("dve", TTO(NDb.a(8192 + j * 2048 + tb * 512, [[1, 512]]), nap, SZ[zs].a(0, [[1, 512]]), ALU.mult),
                                  reads=r_ + [("SZ", zs)], writes=[("YGS", j, tb)] + [("ND", 0, b_) for b_ in range(16)])
                    P.add("sp", DMA(DAP(ygd, 2 * bt * 128 * YTOK, [[YTOK, 128], [128 * YTOK, 2], [1, 2048]]),
                                    NDb.a(8192, [[2048, 2], [1, 2048]])),
                          reads=[("YGS", j, tb) for j in range(2) for tb in range(4)] + [("ND", 0, b_) for b_ in range(16)],
                          dma=True)
                    P.add("dve", RECIP(NDX.a(32, [[1, 32]]), NDX.a(32, [[1, 32]])), reads=[("NDX", 0), ("NDX", 1)],
                          writes=[("NDX", 0), ("NDX", 1)])
                    P.add("dve", TTO(NDX.a(0, [[1, 32]]), NDX.a(0, [[1, 32]]), NDX.a(32, [[1, 32]]), ALU.mult),
                          reads=[("NDX", 0), ("NDX", 1)], writes=[("NDX", 0), ("NDX", 1)])
                    for j in range(2):
                        bz = cnt["z"] % 3
                        cnt["z"] += 1
                        for c in range(8):
                            P.add("pe", MM(PS[bz].a(0, [[1, 16]]), WZ.a(c * 256 + j * 128, [[1, 128]]),
                                           xnT.a(c * XTOK + 4096, [[1, 16]]), c == 0, c == 7),
                                  reads=["WZ", "xnT"], writes=[("ps", bz)])
                        P.add("act", ACT(SZX.a(0, [[1, 16]]), PS[bz].a(0, [[1, 16]]), AF.Silu),
                              reads=[("ps", bz)], writes=["SZX"])
                        P.add("dve", TTO(YGX.a(j * 16, [[1, 16]]), NDX.a(j * 16, [[1, 16]]), SZX.a(0, [[1, 16]]), ALU.mult),
                              reads=[("NDX", j), "SZX"], writes=["YGX"])
                    P.add("sp", DMA(DAP(ygd, 2 * bt * 128 * YTOK + 2048, [[YTOK, 128], [128 * YTOK, 2], [1, 16]]),
                                    YGX.a(0, [[16, 2], [1, 16]])),
                          reads=["YGX"], dma=True)
            P.barrier()

        with contextlib.ExitStack() as sC:
            PS = [cx.ps(sC, "PSc%d" % i, 512, F32) for i in range(8)]
            YG = cx.sb(sC, "YG", 16 * YTOK, BF16)
            WO = KVW
            gpb = cx.sb(sC, "gpb", 1024, F32)
            P.add("sp", DMA(gpb.a(0, [[1, 1024]]), DAP(gpost, 0, [[0, 128], [1, 1024]])), writes=["gpb"], dma=True)
            for r in range(5):
                n_ = 512 if r < 4 else 16
                P.add("sp", DMA(YG.a(512 * r, [[YTOK, 16], [1, n_]]),
                                DAP(ygd, 512 * r, [[YTOK, 128], [128 * YTOK, 16], [1, n_]])),
                      writes=[("YG", "r", r)], dma=True)
            P.add("dve", MSET(YG.a(2064, [[YTOK, 16], [1, YTOK - 2064]]), 0.0), writes=[("YG", "r", 4)])
            emit_out_phase(cx, sC, PS, YG, YTOK, WO, xin, 2048, gpb, h1, ntiles=17,
                           yg_res=lambda ec, tt: [("YG", "r", min(tt // 4, 4))])
        P.barrier()
        emit_l1(cx, s0, h1, gpre1, gpost1, win1, wgrp, wout1, cst1, outd, identb, kvw=KVW)
        P.run(nc, s0)
    return nc


POOL_W = (2, 4, 8, 16)
NT1 = 2176


def emit_l1(cx, s0, hin, gpre, gpost, win, wgrp, wout, cst, out, identb, kvw=None):
    P = cx.P
    CW1 = 128 + 96
    if True:
        cf = cx.sb(s0, "cf1", 144, F32)
        YG = cx.sb(s0, "YG1", 16 * NT1, BF16)
        P.add("sp", DMA(cf.a(0, [[1, 96]]), DAP(cst, 128, [[CW1, 128], [1, 96]])), writes=["cf"], dma=True)
        for g_ in range(4):
            P.add("dve", TS(cf.a(96 + 4 * g_, [[1, 4]]), cf.a(16 + 4 * g_, [[1, 4]]), 1.0 / POOL_W[g_], ALU.mult),
                  reads=["cf"], writes=["cf"])
        P.add("dve", TS(cf.a(112, [[1, 16]]), cf.a(16, [[1, 16]]), -1.0, ALU.mult), reads=["cf"], writes=["cf"])
        P.add("dve", TTO(cf.a(128, [[1, 16]]), cf.a(0, [[1, 16]]), cf.a(16, [[1, 16]]), ALU.mult), reads=["cf"], writes=["cf"])
        with contextlib.ExitStack() as s1:
            xnT = cx.sb(s1, "xnT1", 8 * NT1, BF16)
            with contextlib.ExitStack() as sA:
                gb = cx.sb(sA, "gb1", 1024, F32)
                P.add("sp", DMA(gb.a(0, [[1, 1024]]), DAP(gpre, 0, [[0, 128], [1, 1024]])), writes=["gb"], dma=True)
                emit_norm_transpose(cx, sA, hin, 17, xnT, NT1, gb, identb)
            P.barrier()
            with contextlib.ExitStack() as sB:
                PS = [cx.ps(sB, "PS%d" % i, 512, F32) for i in range(8)]
                WG = [cx.sb(sB, "WG%d" % i, 4 * 512, BF16) for i in range(2)]
                if kvw is not None:
                    WU = [TT(kvw.t, 16 * 1024, 13088), TT(kvw.t, 16 * 1024, 14112)]
                    WZ = [TT(kvw.t, 16 * 1024, 15136), cx.sb(sB, "WZ1", 8 * 128, BF16)]
                else:
                    WU = [cx.sb(sB, "WU%d" % i, 8 * 128, BF16) for i in range(2)]
                    WZ = [cx.sb(sB, "WZ%d" % i, 8 * 128, BF16) for i in range(2)]
                UB = TT(kvw.t, 16 * 1024, 0) if kvw is not None else cx.sb(sB, "UB", 4 * NT1, BF16)
                UW = 16 + NT1
                VV = [cx.sb(sB, "VV%d" % i, UW, F32) for i in range(2)]
                SA = TT(kvw.t.bitcast(F32), 8192, 4352) if kvw is not None else cx.sb(sB, "SA", UW, F32)
                SBf = cx.sb(sB, "SB", UW, F32)
                SZ = [cx.sb(sB, "SZ%d" % i, NT1, F32) for i in range(2)]
                for i in range(2):
                    P.add("dve", MSET(VV[i].a(0, [[1, 16]]), 0.0), writes=[("VV", i, 0)])
                A1 = [cx.sb(sB, "A1%d" % i, UW, F32) for i in range(2)]
                A0 = cx.sb(sB, "A0", 32, F32)
                HALF = (("dve", 0, UW),)
                blocks = [(tb * 512, 512) for tb in range(4)] + [(2048, 128)]
                cnt = dict(a=0, z=0, h=0)

                def both(rd):
                    return [(rd[0], rd[1], 0)]

                for g in range(4):
                    w = POOL_W[g]
                    nsteps = g + 1
                    gp = g % 2
                    P.add("pool", DMA(WG[gp].a(0, [[512, 4], [1, 512]]),
                                      DAP(wgrp, g * 512 * 512, [[512, 128], [128 * 512, 4], [1, 512]])),
                          writes=[("WG", gp)], dma=True)
                    for ic in range(4):
                        uc = g * 4 + ic
                        ws = uc % 2
                        P.add("pool", DMA(WU[ws].a(0, [[128, 8], [1, 128]]),
                                          DAP(win, uc * 128, [[4096, 128], [128 * 4096, 8], [1, 128]])),
                              writes=[("WU", ws)], dma=True)
                        for (t0, n) in blocks:
                            ba = cnt["a"] % 3
                            cnt["a"] += 1
                            for c in range(8):
                                P.add("pe", MM(PS[ba].a(0, [[1, n]]), WU[ws].a(c * 128, [[1, 128]]),
                                               xnT.a(c * NT1 + t0, [[1, n]]), c == 0, c == 7),
                                      reads=[("WU", ws), "xnT"], writes=[("ps", ba)])
                            P.add("act", ACP(UB.a(ic * NT1 + t0, [[1, n]]), PS[ba].a(0, [[1, n]])),
                                  reads=[("ps", ba)], writes=[("UB", ic, t0)])
                    for oc in range(4):
                        e_ = g * 4 + oc
                        zs = e_ % 2
                        v = e_ % 2
                        P.add("pool", DMA(WZ[zs].a(0, [[128, 8], [1, 128]]),
                                          DAP(win, 2048 + e_ * 128, [[4096, 128], [128 * 4096, 8], [1, 128]])),
                              writes=[("WZ", zs)], dma=True)
                        for (t0, n) in blocks:
                            bh = 3 + cnt["h"] % 2
                            cnt["h"] += 1
                            for ic in range(4):
                                P.add("pe", MM(PS[bh].a(0, [[1, n]]), WG[gp].a(ic * 512 + oc * 128, [[1, 128]]),
                                               UB.a(ic * NT1 + t0, [[1, n]]), ic == 0, ic == 3),
                                      reads=[("WG", gp), ("UB", ic, t0)], writes=[("ps", bh)])
                            hv = 0
                            wres = [("VV", v, 0), ("VV", v, 1)] if hv == 2 else [("VV", v, hv)]
                            P.add("act", ACP(VV[v].a(16 + t0, [[1, n]]), PS[bh].a(0, [[1, n]])),
                                  reads=[("ps", bh)], writes=wres)
                            P.add("act", ACT(A1[v].a(16 + t0, [[1, n]]), PS[bh].a(0, [[1, n]]), AF.Identity,
                                             scale=cf.a(112 + e_, [[1, 1]]), bias=cf.a(128 + e_, [[1, 1]])),
                                  reads=[("ps", bh), "cf"], writes=[("A1", v)])
                        for (t0, n) in blocks:
                            bz = 5 + cnt["z"] % 3
                            cnt["z"] += 1
                            for c in range(8):
                                P.add("pe", MM(PS[bz].a(0, [[1, n]]), WZ[zs].a(c * 128, [[1, 128]]),
                                               xnT.a(c * NT1 + t0, [[1, n]]), c == 0, c == 7),
                                      reads=[("WZ", zs), "xnT"], writes=[("ps", bz)])
                            hv = 0
                            wres = [("SZ", v, 0), ("SZ", v, 1)] if hv == 2 else [("SZ", v, hv)]
                            P.add("act", ACT(SZ[v].a(t0, [[1, n]]), PS[bz].a(0, [[1, n]]), AF.Silu),
                                  reads=[("ps", bz)], writes=wres)
                        P.add("dve", CP(A0.a(16, [[1, 16]]), A1[v].a(16, [[1, 16]])), reads=[("A1", v)], writes=["A0"])
                        src, srcn = VV[v], ("VV", v)
                        m = 1
                        for st in range(nsteps):
                            dst, dstn = (SA, ("SA", 0)) if st % 2 == 0 else (SBf, ("SB", 0))
                            lo = 2 * m - 1
                            for hi_, (eng, c0, c1) in enumerate(HALF):
                                c0_ = max(c0, lo)
                                P.add(eng, TTO(dst.a(c0_, [[1, c1 - c0_]]), src.a(c0_, [[1, c1 - c0_]]),
                                               src.a(c0_ - m, [[1, c1 - c0_]]), ALU.add),
                                      reads=both(srcn) if hi_ == 1 else [(srcn[0], srcn[1], 0)],
                                      writes=[(dstn[0], dstn[1], hi_)])
                            src, srcn = dst, dstn
                            m *= 2
                        for hi_, (eng, c0, c1) in enumerate(HALF):
                            c0_ = max(c0, 16)
                            rs = [(srcn[0], srcn[1], hi_)]
                            a1s = [("A1", v)]
                            if eng == "dve":
                                P.add(eng, STT(A1[v].a(c0_, [[1, c1 - c0_]]), src.a(c0_, [[1, c1 - c0_]]), cf.a(96 + e_, [[1, 1]]),
                                               A1[v].a(c0_, [[1, c1 - c0_]]), ALU.mult, ALU.add),
                                      reads=rs + a1s + ["cf"], writes=a1s)
                                P.add(eng, TTO(src.a(16, [[1, 16]]), src.a(16, [[1, 16]]), cf.a(32 + g * 16, [[1, 16]]), ALU.mult),
                                      reads=rs + ["cf"], writes=rs)
                                P.add(eng, STT(A1[v].a(16, [[1, 16]]), src.a(16, [[1, 16]]), cf.a(16 + e_, [[1, 1]]),
                                               A0.a(16, [[1, 16]]), ALU.mult, ALU.add),
                                      reads=rs + ["A0", "cf"], writes=a1s)
                            else:
                                P.add(eng, TS(src.a(c0_, [[1, c1 - c0_]]), src.a(c0_, [[1, c1 - c0_]]), cf.a(96 + e_, [[1, 1]]), ALU.mult),
                                      reads=rs + ["cf"], writes=rs)
                                P.add(eng, TTO(A1[v].a(c0_, [[1, c1 - c0_]]), A1[v].a(c0_, [[1, c1 - c0_]]), src.a(c0_, [[1, c1 - c0_]]), ALU.add),
                                      reads=rs + a1s, writes=a1s)
                            P.add(eng, TTO(YG.a(e_ * NT1 + c0_ - 16, [[1, c1 - c0_]]), A1[v].a(c0_, [[1, c1 - c0_]]),
                                           SZ[v].a(c0_ - 16, [[1, c1 - c0_]]), ALU.mult),
                                  reads=a1s + [("SZ", v, hi_)], writes=[("YG", e_)])
            P.barrier()
        with contextlib.ExitStack() as sC:
            PS = [cx.ps(sC, "PSc%d" % i, 512, F32) for i in range(8)]
            WO = kvw if kvw is not None else cx.sb(sC, "WO", 16 * 1024, BF16)
            gpb = cx.sb(sC, "gpb", 1024, F32)
            P.add("sp", DMA(gpb.a(0, [[1, 1024]]), DAP(gpost, 0, [[0, 128], [1, 1024]])), writes=["gpb"], dma=True)
            for q in range(4):
                P.add("pool", DMA(WO.a(q * 4096, [[1024, 4], [1, 1024]]),
                                  DAP(wout, q * 4 * 128 * 1024, [[1024, 128], [128 * 1024, 4], [1, 1024]])),
                      writes=[("WO", q)], dma=True)
            emit_out_phase(cx, sC, PS, YG, NT1, WO, hin, 0, gpb, out, ntiles=17, sfx="1")


def _consts_l0(first_chunk):
    c = np.zeros((128, CW), np.float32)
    c[:, 0:128] = np.eye(128, dtype=np.float32)
    for m in range(16):
        c[m + 16, 128 + m] = 1.0
        c[m, 128 + 16 + m] = 1.0
    k = np.arange(128)[:, None]
    q = np.arange(128)[None, :]
    c[:, 256:384] = (k >= q).astype(np.float32)
    c[:, 384:512] = (k <= q).astype(np.float32)
    c[:, 512:640] = 0.0 if first_chunk else 1.0
    invf = (np.float32(500000.0) ** (-np.arange(0, 32, 2, dtype=np.float32) / np.float32(32.0))).astype(np.float32)
    p = np.arange(128)
    c[:, 640] = invf[p % 16]
    c[:, 641] = np.where((p % 32) < 16, -1.0, 1.0)
    off = 644
    q16 = np.arange(16)[None, :]
    for pt in PARTS:
        d = pt["d"]
        for r in pt["R"]:
            c[:, off:off + 16] = ((q16 % d == r) & (k >= q16 // d)).astype(np.float32)
            off += 16
    assert off == 644 + 336
    k16 = np.arange(128)[:, None]
    for pt in PARTS:
        d = pt["d"]
        inR = np.isin(q16 % d, np.array(pt["R"]))
        c[:, off:off + 16] = ((k16 <= q16) & ((q16 - k16) % d == 0) & inR & (k16 < 16)).astype(np.float32)
        off += 16
    return c


def _consts_l1(b_grp, scale, chunk):
    c = np.zeros((128, 128 + 96), np.float32)
    c[:, 0:128] = np.eye(128, dtype=np.float32)
    c[:, 128:144] = b_grp.reshape(16, 128).T
    c[:, 144:160] = scale.reshape(16, 128).T
    t = np.arange(16) + chunk * 2048
    for g, w in enumerate(POOL_W):
        c[:, 160 + g * 16:160 + (g + 1) * 16] = (1.0 / np.minimum(t + 1, w)).astype(np.float32)[None, :]
    return c


_NC_CACHE = {}


def make_in_maps(x, positions, norm_pre, norm_post, attn_w_in, attn_w_out,
                 pool_w_in, pool_w_grp, pool_b_grp, pool_scale, pool_w_out, ncores=8):
    B, S, _ = x.shape
    cpb = ncores // B
    f = lambda a: np.ascontiguousarray(np.asarray(a, dtype=np.float32))
    shared = dict(
        gpre=f(norm_pre[0:1]), gpost=f(norm_post[0:1]), win=f(attn_w_in[0]), wout=f(attn_w_out[0]),
        gpre1=f(norm_pre[1:2]), gpost1=f(norm_post[1:2]), win1=f(pool_w_in[0]),
        wgrp=f(np.asarray(pool_w_grp[0]).reshape(2048, 512)), wout1=f(pool_w_out[0]))
    in_maps = []
    for core in range(ncores):
        b, ch = core // cpb, core % cpb
        s0 = ch * 2048
        xin = np.zeros((XTOK, D), np.float32)
        pos = np.zeros((1, XTOK), np.int32)
        xin[2048:4096] = x[b, s0:s0 + 2048]
        pos[0, 2048:4096] = positions[b, s0:s0 + 2048]
        if ch > 0:
            xin[:2048] = x[b, s0 - 2048:s0]
            pos[0, :2048] = positions[b, s0 - 2048:s0]
        if s0 + 2048 < S:
            xin[4096:4112] = x[b, s0 + 2048:s0 + 2064]
            pos[0, 4096:4112] = positions[b, s0 + 2048:s0 + 2064]
        m = dict(shared)
        m.update(xin=xin, pos=pos, cst=_consts_l0(ch == 0),
                 cst1=_consts_l1(np.asarray(pool_b_grp[0], np.float32), np.asarray(pool_scale[0], np.float32), ch))
        in_maps.append(m)
    return in_maps


def kernel(x, positions, norm_pre, norm_post, attn_w_in, attn_w_out,
           pool_w_in, pool_w_grp, pool_b_grp, pool_scale, pool_w_out):
    x = np.ascontiguousarray(np.asarray(x, dtype=np.float32))
    positions = np.asarray(positions).astype(np.int32)
    B, S, _ = x.shape
    ncores = 8
    cpb = ncores // B
    in_maps = make_in_maps(x, positions, norm_pre, norm_post, attn_w_in, attn_w_out,
                           pool_w_in, pool_w_grp, pool_b_grp, pool_scale, pool_w_out, ncores)
    if "f" not in _NC_CACHE:
        _NC_CACHE["f"] = build_fused()
    res = run_bass_kernel_spmd(_NC_CACHE["f"], in_maps, core_ids=list(range(ncores)))
    out = np.zeros((B, S, D), np.float32)
    for core in range(ncores):
        b, ch = core // cpb, core % cpb
        s0 = ch * 2048
        o = res.results[core]["out"]
        lo = 0 if ch == 0 else 16
        hi = min(2064, S - s0)
        out[b, s0 + lo:s0 + hi] = o[lo:hi]
    return out
```
